# Optimizing a Trainium2 kernel written in Bass

```python
import math
import jax, jax.numpy as jnp
from jax import lax
import numpy as np

D_MODEL = 1024
BATCH = 8
SEQ = 4096
DEPTH = 4

GRID_W = 64
HEAD_DIM = 64
EPS = 1e-6
PLE_DIM = 256

NA_HEADS = 16
NA_WIDTH = NA_HEADS * HEAD_DIM
NA_WIN_ROWS = 8
NA_WIN_COLS = 16
NA_QCOLS = 16
NA_KCOLS = 32

DIL_PAIRS = ((128, 1), (512, 4), (2048, 16))
DIL_HEADS_PER_GROUP = 8
DIL_HEADS = DIL_HEADS_PER_GROUP * len(DIL_PAIRS)
DIL_WIDTH = DIL_HEADS * HEAD_DIM
DIL_OUT_WIDTH = DIL_HEADS_PER_GROUP * HEAD_DIM
ROPE_THETA = 500000.0
ROPE_DIM = HEAD_DIM // 4

SSM_INNER = 1536
SSM_HEAD_DIM = 64
SSM_HEADS = SSM_INNER // SSM_HEAD_DIM
SSM_GROUPS = 4
SSM_STATE = 128
SSM_CONV = 5
SSM_CHUNK = 128
SSM_CONV_DIM = SSM_INNER + 2 * SSM_GROUPS * SSM_STATE

IN_SPLITS = (NA_WIDTH, NA_WIDTH, NA_WIDTH, NA_WIDTH,
             DIL_WIDTH, DIL_WIDTH, DIL_WIDTH, DIL_OUT_WIDTH,
             SSM_CONV_DIM, SSM_INNER, 2 * SSM_HEADS,
             D_MODEL, D_MODEL, D_MODEL)
IN_WIDTH = sum(IN_SPLITS)

kernel_name = 'hybrid_natten_dilated_ssd_encoder'


def rms_norm(x, g):
    xf = x.astype(jnp.float32)
    y = xf * lax.rsqrt(jnp.mean(xf * xf, axis=-1, keepdims=True) + EPS)
    return (y * g.astype(jnp.float32)).astype(x.dtype)


def split_heads(t, n):
    b, s, _ = t.shape
    return t.reshape(b, s, n, HEAD_DIM)


def rotary_tables(pos):
    inv = ROPE_THETA ** (-jnp.arange(0, ROPE_DIM, 2, dtype=jnp.float32) / ROPE_DIM)
    ang = pos.astype(jnp.float32)[:, None] * inv[None, :]
    return jnp.cos(ang), jnp.sin(ang)


def apply_partial_rotary(t, cos, sin):
    half = ROPE_DIM // 2
    t1 = t[..., :half].astype(jnp.float32)
    t2 = t[..., half:ROPE_DIM].astype(jnp.float32)
    c = cos[None, :, None, :]
    s_ = sin[None, :, None, :]
    rot = jnp.concatenate([t1 * c - t2 * s_, t2 * c + t1 * s_], axis=-1).astype(t.dtype)
    return jnp.concatenate([rot, t[..., ROPE_DIM:]], axis=-1)


def neighbourhood_attention(q, k, v, rpb):
    b, s, h, dh = q.shape
    rows = s // GRID_W
    kh = min(NA_WIN_ROWS, rows)
    ncb = GRID_W // NA_QCOLS
    qcol = np.arange(GRID_W).reshape(ncb, NA_QCOLS)
    win_start = np.clip(qcol - NA_WIN_COLS // 2, 0, GRID_W - NA_WIN_COLS)
    band_start = np.clip(np.arange(ncb) * NA_QCOLS - NA_WIN_COLS // 2, 0, GRID_W - NA_KCOLS)
    kcol = band_start[:, None] + np.arange(NA_KCOLS)[None, :]
    in_win = (kcol[:, None, :] >= win_start[:, :, None]) & (kcol[:, None, :] < win_start[:, :, None] + NA_WIN_COLS)
    dcol = np.clip(kcol[:, None, :] - qcol[:, :, None] + NA_WIN_COLS - 1, 0, 2 * NA_WIN_COLS - 2)
    col_bias = jnp.where(in_win, rpb.astype(jnp.float32)[:, :, dcol], -jnp.inf)
    qg = jnp.moveaxis(q.reshape(b, rows, ncb, NA_QCOLS, h, dh), 1, 0)
    kg = k.reshape(b, rows, GRID_W, h, dh)
    vg = v.reshape(b, rows, GRID_W, h, dh)
    scale = dh ** -0.5

    def one_row(args):
        r, qr = args
        r0 = jnp.clip(r - kh // 2, 0, rows - kh)
        kb = lax.dynamic_slice_in_dim(kg, r0, kh, axis=1)[:, :, kcol]
        vb = lax.dynamic_slice_in_dim(vg, r0, kh, axis=1)[:, :, kcol]
        bias = jnp.take(col_bias, r0 + jnp.arange(kh) - r + NA_WIN_ROWS - 1, axis=1)
        sc = jnp.einsum('bjqhd,bkjchd->bhjqkc', qr, kb, preferred_element_type=jnp.float32) * scale
        sc = sc + jnp.transpose(bias, (0, 2, 3, 1, 4))
        pr = jax.nn.softmax(sc.reshape(b, h, ncb, NA_QCOLS, kh * NA_KCOLS), axis=-1).reshape(sc.shape)
        return jnp.einsum('bhjqkc,bkjchd->bjqhd', pr.astype(v.dtype), vb)

    out = lax.map(one_row, (jnp.arange(rows), qg))
    return jnp.moveaxis(out, 0, 1).reshape(b, s, h, dh)


def dilated_attention(q, k, v, window, dilation):
    b, s, h, dh = q.shape
    blk = window // (2 * dilation)
    L = s // dilation
    nb = -(-L // blk)
    lp = nb * blk

    def to_residue(t):
        return t.reshape(b, L, dilation, h, dh).transpose(0, 2, 1, 3, 4)

    qs = jnp.pad(to_residue(q), ((0, 0), (0, 0), (0, lp - L), (0, 0), (0, 0))).reshape(b, dilation, nb, blk, h, dh)
    kpad = ((0, 0), (0, 0), (blk, lp - L + blk), (0, 0), (0, 0))

    def windows(t):
        tb = jnp.pad(to_residue(t), kpad).reshape(b, dilation, nb + 2, blk, h, dh)
        return jnp.concatenate([tb[:, :, :-2], tb[:, :, 1:-1], tb[:, :, 2:]], axis=3)

    kw, vw = windows(k), windows(v)
    mpos = np.arange(lp).reshape(nb, blk)
    kpos = np.arange(nb)[:, None] * blk - blk + np.arange(3 * blk)[None, :]
    valid = ((kpos[:, None, :] >= 0) & (kpos[:, None, :] < L)
             & (np.abs(kpos[:, None, :] - mpos[:, :, None]) <= blk))
    sc = jnp.einsum('bdnqhe,bdnkhe->bdhnqk', qs, kw, preferred_element_type=jnp.float32) * (dh ** -0.5)
    sc = jnp.where(valid, sc, -jnp.inf)
    m = jnp.max(sc, axis=-1, keepdims=True)
    e = jnp.exp(sc - m)
    den = jnp.sum(e, axis=-1, keepdims=True)
    o = jnp.einsum('bdhnqk,bdnkhe->bdnqhe', (e / den).astype(v.dtype), vw)
    lse = (m + jnp.log(den))[..., 0]
    o = o.reshape(b, dilation, lp, h, dh)[:, :, :L].transpose(0, 2, 1, 3, 4).reshape(b, s, h, dh)
    lse = lse.transpose(0, 1, 3, 4, 2).reshape(b, dilation, lp, h)[:, :, :L].transpose(0, 2, 1, 3).reshape(b, s, h)
    return o, lse


def dilated_mixture(q, k, v):
    outs, lses = [], []
    for g, (window, dilation) in enumerate(DIL_PAIRS):
        sl = slice(g * DIL_HEADS_PER_GROUP, (g + 1) * DIL_HEADS_PER_GROUP)
        o, lse = dilated_attention(q[:, :, sl], k[:, :, sl], v[:, :, sl], window, dilation)
        outs.append(o)
        lses.append(lse)
    wts = jax.nn.softmax(jnp.stack(lses, axis=0), axis=0)
    out = jnp.sum(wts[..., None] * jnp.stack(outs, axis=0).astype(jnp.float32), axis=0)
    return out.astype(q.dtype)


def centred_depthwise_conv(t, w, bias):
    c = t.shape[-1]
    pad = SSM_CONV // 2
    y = lax.conv_general_dilated(t, w.reshape(SSM_CONV, 1, c).astype(t.dtype), (1,), [(pad, pad)],
                                 dimension_numbers=('NWC', 'WIO', 'NWC'), feature_group_count=c)
    return y + bias.astype(t.dtype)


def ssd_chunked(xs, dt, A, bm, cm):
    b, s, nh, hp = xs.shape
    g, n = bm.shape[-2:]
    hg = nh // g
    l = SSM_CHUNK
    c = s // l
    f32 = jnp.float32
    xdt = (xs.astype(f32) * dt[..., None]).reshape(b, c, l, g, hg, hp)
    bc = bm.astype(f32).reshape(b, c, l, g, n)
    cc = cm.astype(f32).reshape(b, c, l, g, n)
    a = (dt * A).reshape(b, c, l, g, hg).transpose(0, 3, 4, 1, 2)
    a_cum = jnp.cumsum(a, axis=-1)
    causal = np.tril(np.ones((l, l), dtype=bool))
    seg = jnp.exp(jnp.where(causal, a_cum[..., :, None] - a_cum[..., None, :], -jnp.inf))
    cb = jnp.einsum('bclgn,bcsgn->bgcls', cc, bc)
    y_diag = jnp.einsum('bgcls,bghcls,bcsghp->bclghp', cb, seg, xdt)
    decay_to_end = jnp.exp(a_cum[..., -1:] - a_cum)
    chunk_states = jnp.einsum('bclgn,bghcl,bclghp->cbghpn', bc, decay_to_end, xdt)
    chunk_decay = jnp.exp(a_cum[..., -1]).transpose(3, 0, 1, 2)

    def carry_state(state, inp):
        st, dec = inp
        return state * dec[..., None, None] + st, state

    init = jnp.zeros((b, g, hg, hp, n), f32)
    _, entering = lax.scan(carry_state, init, (chunk_states, chunk_decay))
    y_off = jnp.einsum('bclgn,cbghpn,bghcl->bclghp', cc, entering, jnp.exp(a_cum))
    return (y_diag + y_off).reshape(b, s, nh, hp)


def mamba2_bidirectional(xbc, z, dt_raw, conv_w, conv_b, a_log, dt_bias, d_skip, norm_w):
    b, s, _ = xbc.shape
    xbc = jax.nn.silu(centred_depthwise_conv(xbc, conv_w, conv_b))
    xs, bm, cm = jnp.split(xbc, [SSM_INNER, SSM_INNER + SSM_GROUPS * SSM_STATE], axis=-1)
    xs = xs.reshape(b, s, SSM_HEADS, SSM_HEAD_DIM)
    bm = bm.reshape(b, s, SSM_GROUPS, SSM_STATE)
    cm = cm.reshape(b, s, SSM_GROUPS, SSM_STATE)
    dt = jax.nn.softplus(dt_raw.astype(jnp.float32).reshape(b, s, 2, SSM_HEADS) + dt_bias.astype(jnp.float32))
    A = -jnp.exp(a_log.astype(jnp.float32))
    y_f = ssd_chunked(xs, dt[:, :, 0], A[0], bm, cm)
    y_b = jnp.flip(ssd_chunked(jnp.flip(xs, 1), jnp.flip(dt[:, :, 1], 1), A[1],
                               jnp.flip(bm, 1), jnp.flip(cm, 1)), 1)
    y = y_f + y_b + d_skip.astype(jnp.float32)[:, None] * xs.astype(jnp.float32)
    y = y.reshape(b, s, SSM_INNER) * jax.nn.silu(z.astype(jnp.float32))
    return rms_norm(y, norm_w).astype(z.dtype)


def setup_inputs(seed: int = 0) -> dict:
    key = jax.random.key(seed)
    ks = jax.random.split(key, 20)
    f32 = jnp.float32

    def nrm(k, shape, scale):
        return jax.random.normal(k, shape, f32) * scale

    x = jax.random.normal(ks[0], (BATCH, SEQ, D_MODEL), f32)
    p = jax.random.normal(ks[1], (DEPTH, BATCH, SEQ, PLE_DIM), f32)
    norm_w = 1.0 + nrm(ks[2], (DEPTH, D_MODEL), 0.01)
    w_in = nrm(ks[3], (DEPTH, D_MODEL, IN_WIDTH), D_MODEL ** -0.5)
    na_rpb = nrm(ks[4], (DEPTH, NA_HEADS, 2 * NA_WIN_ROWS - 1, 2 * NA_WIN_COLS - 1), 0.02)
    conv_w = nrm(ks[5], (DEPTH, SSM_CONV, SSM_CONV_DIM), SSM_CONV ** -0.5)
    conv_b = nrm(ks[6], (DEPTH, SSM_CONV_DIM), 0.01)
    a_log = jnp.log(jax.random.uniform(ks[7], (DEPTH, 2, SSM_HEADS), f32, 1.0, 16.0))
    dt0 = jnp.exp(jax.random.uniform(ks[8], (DEPTH, 2, SSM_HEADS), f32, math.log(1e-3), math.log(1e-1)))
    dt_bias = dt0 + jnp.log(-jnp.expm1(-dt0))
    d_skip = 1.0 + nrm(ks[9], (DEPTH, SSM_HEADS), 0.01)
    ssm_norm_w = 1.0 + nrm(ks[10], (DEPTH, SSM_INNER), 0.01)
    w_oa = nrm(ks[11], (DEPTH, NA_WIDTH, D_MODEL), NA_WIDTH ** -0.5)
    w_ob = nrm(ks[12], (DEPTH, DIL_OUT_WIDTH, D_MODEL), DIL_OUT_WIDTH ** -0.5)
    w_oc = nrm(ks[13], (DEPTH, SSM_INNER, D_MODEL), SSM_INNER ** -0.5)
    w_out = nrm(ks[14], (DEPTH, D_MODEL, D_MODEL), D_MODEL ** -0.5)
    ple_norm_w = 1.0 + nrm(ks[15], (DEPTH, D_MODEL), 0.01)
    w_ple = nrm(ks[16], (DEPTH, PLE_DIM, D_MODEL), PLE_DIM ** -0.5)
    w_ple_gate = nrm(ks[17], (DEPTH, D_MODEL, D_MODEL), D_MODEL ** -0.5)
    final_norm_w = 1.0 + nrm(ks[18], (D_MODEL,), 0.01)
    return {'x': x, 'p': p, 'norm_w': norm_w, 'w_in': w_in, 'na_rpb': na_rpb,
            'conv_w': conv_w, 'conv_b': conv_b, 'a_log': a_log, 'dt_bias': dt_bias,
            'd_skip': d_skip, 'ssm_norm_w': ssm_norm_w, 'w_oa': w_oa, 'w_ob': w_ob,
            'w_oc': w_oc, 'w_out': w_out, 'ple_norm_w': ple_norm_w, 'w_ple': w_ple,
            'w_ple_gate': w_ple_gate, 'final_norm_w': final_norm_w}


def reference(x, p, norm_w, w_in, na_rpb, conv_w, conv_b, a_log, dt_bias, d_skip, ssm_norm_w,
              w_oa, w_ob, w_oc, w_out, ple_norm_w, w_ple, w_ple_gate, final_norm_w):
    b, s, _ = x.shape
    cos, sin = rotary_tables(jnp.arange(s))
    split_idx = [int(v) for v in np.cumsum(IN_SPLITS)[:-1]]
    for i in range(DEPTH):
        h = rms_norm(x, norm_w[i])
        (qa, ka, va, ga, qb, kb, vb, gb, xbc, z, dt_raw, ua, ub, uc) = [
            h @ w for w in jnp.split(w_in[i], split_idx, axis=1)]
        ya = neighbourhood_attention(split_heads(qa, NA_HEADS), split_heads(ka, NA_HEADS),
                                     split_heads(va, NA_HEADS), na_rpb[i]).reshape(b, s, NA_WIDTH)
        ya = (ya * jax.nn.silu(ga)) @ w_oa[i]
        qbh = apply_partial_rotary(split_heads(qb, DIL_HEADS), cos, sin)
        kbh = apply_partial_rotary(split_heads(kb, DIL_HEADS), cos, sin)
        yb = dilated_mixture(qbh, kbh, split_heads(vb, DIL_HEADS)).reshape(b, s, DIL_OUT_WIDTH)
        yb = (yb * jax.nn.silu(gb)) @ w_ob[i]
        yc = mamba2_bidirectional(xbc, z, dt_raw, conv_w[i], conv_b[i], a_log[i], dt_bias[i],
                                  d_skip[i], ssm_norm_w[i]) @ w_oc[i]
        merged = jax.nn.sigmoid(ua) * ya + jax.nn.sigmoid(ub) * yb + jax.nn.sigmoid(uc) * yc
        x = x + merged @ w_out[i]
        gate = jax.nn.sigmoid(rms_norm(x, ple_norm_w[i]) @ w_ple_gate[i])
        x = x + (p[i] @ w_ple[i]) * gate
    return rms_norm(x, final_norm_w)
```

```python
import numpy as np
import concourse.bass as bass
import concourse.mybir as mybir
from concourse.bass_utils import run_bass_kernel_spmd
from concourse.alu_op_type import AluOpType as ALU
from contextlib import ExitStack

F32 = mybir.dt.float32
BF16 = mybir.dt.bfloat16
AF = mybir.ActivationFunctionType

T = 4096
NT = 32
D = 1024
KC = 8
DEPTH = 4
IN_W = 16432
EPS = 1e-6
NEG = -30000.0
PIPE_B = True
PIPE_A = True
PIPE_C = True
DEPTH_B = 0
LIM_SP = 4
LIM_G = 3

OFF_QA, OFF_KA, OFF_VA, OFF_GA = 0, 1024, 2048, 3072
OFF_QB, OFF_KB, OFF_VB, OFF_GB = 4096, 5632, 7168, 8704
OFF_XBC, OFF_Z, OFF_DT = 9216, 11776, 13312
OFF_UA, OFF_UB, OFF_UC = 13360, 14384, 15408


class Res:
    __slots__ = ("w", "r", "war")

    def __init__(self):
        self.w = {}
        self.r = {}
        self.war = {}


def _merge(dst, src):
    for k, sv in src.items():
        o = dst.get(k)
        if o is None or o[1] < sv[1]:
            dst[k] = sv


class Buf:
    def __init__(self, t, psum=False):
        self.t = t
        self.r = Res()
        self.psum = psum


class Sched:
    ND = 12

    def __init__(self, nc, st):
        self.nc = nc
        self.st = st
        self.engs = {"pe": nc.tensor, "act": nc.scalar, "dve": nc.vector, "pool": nc.gpsimd, "sp": nc.sync}
        self.csem = {e: st.enter_context(nc.semaphore("c_" + e)) for e in self.engs}
        self.ccnt = {e: 0 for e in self.engs}
        self.seen = {e: {} for e in self.engs}
        self.dsems = {q: [st.enter_context(nc.semaphore("d_%s_%d" % (q, i))) for i in range(self.ND)] for q in ("sp", "pool")}
        self.dcnt = {q: [0] * self.ND for q in ("sp", "pool")}
        self.dnext = {q: 0 for q in ("sp", "pool")}
        self.nbuf = 0

    def sb(self, shape, dt, name=None):
        self.nbuf += 1
        return Buf(self.st.enter_context(self.nc.sbuf_tensor("%s_%d" % (name or "b", self.nbuf), list(shape), dt)))

    def ps(self, shape, dt, name=None):
        self.nbuf += 1
        return Buf(self.st.enter_context(self.nc.psum_tensor("%s_%d" % (name or "p", self.nbuf), list(shape), dt)), psum=True)

    def _deps(self, reads, writes, pwrites):
        deps = {}
        for r in reads:
            _merge(deps, r.w)
        for w in writes:
            _merge(deps, w.w)
            _merge(deps, w.r)
            _merge(deps, w.war)
        for w in pwrites:
            if w.r:
                nw = {}
                _merge(nw, w.r)
                _merge(nw, w.w)
                w.war = nw
                w.w = {}
                w.r = {}
            _merge(deps, w.war)
        return deps

    def _wait(self, e, deps):
        seen = self.seen[e]
        eng = self.engs[e]
        for k, (sem, v) in deps.items():
            if seen.get(k, 0) < v:
                eng.wait_ge(sem, v)
                seen[k] = v

    def _commit(self, tok, reads, writes, pwrites):
        for r in reads:
            _merge(r.r, tok)
        for w in writes:
            w.w = dict(tok)
            w.r = {}
            w.war = {}
        for w in pwrites:
            _merge(w.w, tok)

    def op(self, e, fn, reads=(), writes=(), pwrites=()):
        writes = list(writes) + [b for b in reads if isinstance(b, Buf) and b.psum]
        reads = [b.r if isinstance(b, Buf) else b for b in reads if not (isinstance(b, Buf) and b.psum)]
        writes = [b.r if isinstance(b, Buf) else b for b in writes]
        pwrites = [b.r if isinstance(b, Buf) else b for b in pwrites]
        deps = self._deps(reads, writes, pwrites)
        own = self.csem[e]
        if e == "pe":
            deps.pop(own.num, None)
        self._wait(e, deps)
        ins = fn(self.engs[e])
        self.ccnt[e] += 1
        ins.then_inc(own, 1)
        tok = {own.num: (own, self.ccnt[e])}
        self._commit(tok, reads, writes, pwrites)

    def dma(self, q, out, in_, reads=(), writes=(), pwrites=()):
        reads = [b.r if isinstance(b, Buf) else b for b in reads]
        writes = [b.r if isinstance(b, Buf) else b for b in writes]
        pwrites = [b.r if isinstance(b, Buf) else b for b in pwrites]
        deps = self._deps(reads, writes, pwrites)
        i = self.dnext[q]
        self.dnext[q] = (i + 1) % self.ND
        sem = self.dsems[q][i]
        if self.dcnt[q][i] > 0:
            _merge(deps, {sem.num: (sem, self.dcnt[q][i])})
        self._wait(q, deps)
        self.engs[q].dma_start(out=out, in_=in_).then_inc(sem, 16)
        self.dcnt[q][i] += 16
        tok = {sem.num: (sem, self.dcnt[q][i])}
        self._commit(tok, reads, writes, pwrites)

    def _all_tokens(self):
        deps = {}
        for e in self.engs:
            if self.ccnt[e] > 0:
                deps[self.csem[e].num] = (self.csem[e], self.ccnt[e])
        for q in self.dsems:
            for i, sem in enumerate(self.dsems[q]):
                if self.dcnt[q][i] > 0:
                    deps[sem.num] = (sem, self.dcnt[q][i])
        return deps

    def barrier(self):
        allt = self._all_tokens()
        for e in self.engs:
            deps = dict(allt)
            if e in ("pe", "sp"):
                deps.pop(self.csem[e].num, None)
            self._wait(e, deps)

    def finish(self):
        deps = {}
        for e in self.engs:
            if self.ccnt[e] > 0:
                deps[self.csem[e].num] = (self.csem[e], self.ccnt[e])
        for q in self.dsems:
            for i, sem in enumerate(self.dsems[q]):
                if self.dcnt[q][i] > 0:
                    deps[sem.num] = (sem, self.dcnt[q][i])
        deps.pop(self.csem["sp"].num, None)
        self._wait("sp", deps)


def na_cases():
    cases = []
    per_a = []
    idx = {}

    def win(r):
        r0 = min(max(r - 4, 0), 56)
        return r0

    for a in range(32):
        lst = []
        rows = (2 * a, 2 * a + 1)
        lo = min(win(r) for r in rows)
        hi = max(win(r) + 7 for r in rows)
        for kt in range(lo // 2, hi // 2 + 1):
            key = ("i", kt - a) if 2 <= a <= 29 else (a, kt - a)
            if key not in idx:
                idx[key] = len(cases)
                cases.append((a, kt))
            lst.append((kt, idx[key], kt - a))
        per_a.append(lst)
    return cases, per_a


def na_tables():
    cases, per_a = na_cases()
    nc_ = len(cases)
    mask = np.zeros((nc_, 128, 128), np.float32)
    kr = np.arange(128) // 64
    kc = np.arange(128) % 64
    qr = np.arange(128) // 64
    qc = np.arange(128) % 64
    ws = np.clip(qc - 8, 0, 48)
    colok = (kc[:, None] >= ws[None, :]) & (kc[:, None] < ws[None, :] + 16)
    for ci, (a, kt) in enumerate(cases):
        krow = 2 * kt + kr
        qrow = 2 * a + qr
        r0 = np.clip(qrow - 4, 0, 56)
        rowok = (krow[:, None] >= r0[None, :]) & (krow[:, None] < r0[None, :] + 8)
        ok = rowok & colok
        mask[ci] = np.where(ok, 1.0, 0.0)
    dr = np.zeros((7, 128, 128), np.int64)
    dc = np.zeros((7, 128, 128), np.int64)
    for di, d in enumerate(range(-3, 4)):
        dr[di] = np.clip(2 * d + kr[:, None] - qr[None, :] + 7, 0, 14)
        dc[di] = np.clip(kc[:, None] - qc[None, :] + 15, 0, 30)
    return cases, per_a, mask, dr, dc


def dil_masks():
    m = np.zeros((3, 128, 128), np.float32)
    pk = np.arange(128)[:, None]
    pq = np.arange(128)[None, :]
    for i, dl in enumerate((-1, 0, 1)):
        m[i] = np.where(np.abs(128 * dl + pk - pq) <= 64, 1.0, 0.0)
    return m


def rope_tables():
    inv = 500000.0 ** (-np.arange(0, 16, 2, dtype=np.float32) / 16.0)
    ang = np.arange(T, dtype=np.float32)[:, None] * inv[None, :]
    cos = np.cos(ang).astype(np.float32).T
    sin = np.sin(ang).astype(np.float32).T
    C = np.ones((128, T), np.float32)
    S = np.zeros((128, T), np.float32)
    for hh in range(2):
        b = hh * 64
        C[b:b + 8] = cos
        C[b + 8:b + 16] = cos
        S[b:b + 8] = -sin
        S[b + 8:b + 16] = sin
    return C, S


class Kern:
    pass


def res_cols(ap2, d, i0, n):
    if d == 1:
        return ap2[:, i0:i0 + n]
    L = T // d
    v = ap2.rearrange("p (m r) -> p r m", r=d)
    r0, m0 = i0 // L, i0 % L
    if n <= L - m0:
        return v[:, r0, m0:m0 + n]
    assert m0 == 0 and n % L == 0
    return v[:, r0:r0 + n // L, :]


def build(nlayers=DEPTH, stop=None, dbg=False):
    nc = bass.Bass("TRN2", target_bir_lowering=False)

    def din(name, shape, dt=F32):
        return nc.dram_tensor(name, list(shape), dt, kind="ExternalInput").ap()

    def dscr(name, shape, dt):
        return nc.dram_tensor(name, list(shape), dt, kind="Internal").ap()

    cases, per_a, _, _, _ = na_tables()
    NCASE = len(cases)
    EB_GROUPS = []
    seen_c = set()
    for a in range(32):
        lst = per_a[a]
        if lst[0][1] in seen_c:
            continue
        seen_c.add(lst[0][1])
        assert [c for (_, c, _) in lst] == list(range(lst[0][1], lst[0][1] + len(lst)))
        assert [d for (_, _, d) in lst] == list(range(lst[0][2], lst[0][2] + len(lst)))
        EB_GROUPS.append((lst[0][1], len(lst), lst[0][2]))

    x_in = din("x", [T, D])
    p_in = din("p", [DEPTH * T, 256])
    norm_w = din("norm_w", [DEPTH, D])
    ple_norm_w = din("ple_norm_w", [DEPTH, D])
    final_norm_w = din("final_norm_w", [1, D])
    w_in = din("w_in", [DEPTH * D, IN_W])
    w_oa = din("w_oa", [DEPTH * 1024, D])
    w_ob = din("w_ob", [DEPTH * 512, D])
    w_oc = din("w_oc", [DEPTH * 1536, D])
    w_out = din("w_out", [DEPTH * D, D])
    w_ple = din("w_ple", [DEPTH * 256, D])
    w_pg = din("w_ple_gate", [DEPTH * D, D])
    rpbg = din("rpbg", [DEPTH * 16 * 128, 7 * 128])
    na_mask = din("na_mask", [128, NCASE * 128])
    dil_mask = din("dil_mask", [128, 3 * 128])
    rope_c = din("rope_c", [128, T])
    rope_s = din("rope_s", [128, T])
    rope_pm = din("rope_pm", [128, 128])
    convw = din("convw", [DEPTH * 128, 20 * 5])
    convb = din("convb", [DEPTH * 128, 20])
    a_log = din("a_log", [DEPTH, 48])
    dt_bias = din("dt_bias", [DEPTH, 48])
    d_skip = din("d_skip", [DEPTH, 24])
    ssm_nw = din("ssm_norm_w", [DEPTH, 1536])
    y_out = nc.dram_tensor("y", [T, D], F32, kind="ExternalOutput").ap()

    xs = dscr("xs", [T, D], F32)
    yagT = dscr("yagT", [1024, T], BF16)
    ybgT = dscr("ybgT", [512, T], BF16)
    ycT = dscr("ycT", [1536, T], BF16)
    mrgT = dscr("mrgT", [1024, T], BF16)
    nd = [dscr("nd%d" % g, [T, 8 * 65], F32) for g in range(3)]
    xtok = dscr("xtok", [T, 1536], BF16)
    btok = dscr("btok", [T, 512], BF16)
    bcT = dscr("bcT", [1024, T], BF16)
    szd = dscr("szd", [T, 1536], BF16)
    yfd = dscr("yfd", [T, 1536], F32)
    dbg_out = None
    if dbg:
        dbg_out = nc.dram_tensor("dbg", [1536, T], F32, kind="ExternalOutput").ap()

    with ExitStack() as st:
        K = Sched(nc, st)
        V = Kern()
        identf = K.sb([128, 128], F32, "identf")
        identb = K.sb([128, 128], BF16, "identb")
        epsb = K.sb([128, 1], F32, "epsb")
        oneb = K.sb([128, 1], F32, "oneb")
        onesf = K.sb([128, 128], F32, "onesf")
        triF = K.sb([128, 128], F32, "triF")
        triB = K.sb([128, 128], F32, "triB")
        ntriF = K.sb([128, 128], F32, "ntriF")
        ntriB = K.sb([128, 128], F32, "ntriB")
        mFB = K.sb([128, 2, 128], BF16, "mFB")
        mtmp = K.sb([128, 128], F32, "mtmp")
        K.op("pool", lambda e: e.memset(identf.t[:], 0.0), writes=[identf])
        K.op("pool", lambda e: e.affine_select(out=identf.t[:], in_=identf.t[:], pattern=[[-1, 128]], compare_op=ALU.not_equal,
                                               fill=1.0, base=0, channel_multiplier=1), writes=[identf])
        K.op("pool", lambda e: e.tensor_copy(out=identb.t[:], in_=identf.t[:]), reads=[identf], writes=[identb])
        K.op("pool", lambda e: e.memset(epsb.t[:], EPS), writes=[epsb])
        K.op("pool", lambda e: e.memset(oneb.t[:], 1.0), writes=[oneb])
        K.op("pool", lambda e: e.memset(onesf.t[:], 1.0), writes=[onesf])
        for (tb_, sgn) in ((triF, 1), (triB, -1)):
            K.op("pool", lambda e, tb_=tb_: e.memset(tb_.t[:], 1.0), writes=[tb_])
            K.op("pool", lambda e, tb_=tb_, sgn=sgn: e.affine_select(out=tb_.t[:], in_=tb_.t[:], pattern=[[sgn, 128]], compare_op=ALU.is_ge,
                                                                     fill=0.0, base=0, channel_multiplier=-sgn), writes=[tb_])
        K.op("pool", lambda e: e.tensor_scalar(out=ntriF.t[:], in0=triF.t[:], scalar1=-1.0, scalar2=None, op0=ALU.mult), reads=[triF], writes=[ntriF])
        K.op("pool", lambda e: e.tensor_scalar(out=ntriB.t[:], in0=triB.t[:], scalar1=-1.0, scalar2=None, op0=ALU.mult), reads=[triB], writes=[ntriB])
        for i_, tb_ in enumerate((triF, triB)):
            K.op("pool", lambda e, tb_=tb_: e.tensor_scalar(out=mtmp.t[:], in0=tb_.t[:], scalar1=-1.0, scalar2=-NEG, op0=ALU.add, op1=ALU.mult),
                 reads=[tb_], writes=[mtmp])
            K.op("pool", lambda e, i_=i_: e.tensor_copy(out=mFB.t[:, i_, :], in_=mtmp.t[:]), reads=[mtmp], writes=[mFB])
        namask = K.sb([128, NCASE, 128], BF16, "namask")
        dmask = K.sb([128, 3, 128], BF16, "dmask")
        K.dma("pool", namask.t[:], na_mask.rearrange("p (c n) -> p c n", c=NCASE), writes=[namask])
        K.dma("pool", dmask.t[:], dil_mask.rearrange("p (c n) -> p c n", c=3), writes=[dmask])

        hT = K.sb([128, KC, T], BF16, "hT")
        gB = K.sb([128, D], F32, "gB")
        psb = [K.ps([128, 512], F32, "psb%d" % i) for i in range(7)]
        pst = K.ps([128, 1024], BF16, "pst")
        xt = [K.sb([128, D], F32, "xt%d" % i) for i in range(2)]
        hf = [K.sb([128, D], F32, "hf%d" % i) for i in range(2)]
        sml = [K.sb([128, 4], F32, "sml%d" % i) for i in range(2)]
        wsl = [K.sb([128, KC, 128], BF16, "wsl%d" % i) for i in range(5)]
        wctr = [0]
        nd_res = [Res() for _ in range(3)]
        yagT_res, ybgT_res, ycT_res, mrgT_res, xs_res = Res(), Res(), Res(), Res(), Res()
        xtok_res, btok_res, bcT_res, szd_res, yfd_res = Res(), Res(), Res(), Res(), Res()

        def rms_scale(src_ap, s, src_reads, n=D, junk_ap=None, junk_res=None):
            K.op("act", lambda e: e.activation(out=junk_ap, in_=src_ap, func=AF.Square, accum_out=s.t[:, 0:1]),
                 reads=src_reads, writes=[junk_res, s])
            K.op("act", lambda e: e.activation(out=s.t[:, 1:2], in_=s.t[:, 0:1], func=AF.Sqrt, bias=epsb.t[:, 0:1], scale=1.0 / n),
                 reads=[s, epsb], writes=[s])
            K.op("dve", lambda e: e.reciprocal(out=s.t[:, 2:3], in_=s.t[:, 1:2]), reads=[s], writes=[s])

        def rmsnorm_tile(src, i, gBuf):
            s = sml[i]
            rms_scale(src.t[:], s, [src], D, hf[i].t[:], hf[i])
            K.op("dve", lambda e: e.scalar_tensor_tensor(out=hf[i].t[:], in0=src.t[:], scalar=s.t[:, 2:3], in1=gBuf.t[:],
                                                         op0=ALU.mult, op1=ALU.mult),
                 reads=[src, s, gBuf], writes=[hf[i]])

        def to_T(i, dst_fn, dres, pa=0):
            for half in range(2):
                pb = psb[pa + half]
                for k4 in range(4):
                    kc = half * 4 + k4
                    K.op("pe", lambda e, kc=kc, k4=k4, pb=pb: e.transpose(out=pb.t[:, k4 * 128:(k4 + 1) * 128],
                                                                          in_=hf[i].t[:, kc * 128:(kc + 1) * 128], identity=identf.t[:]),
                         reads=[hf[i], identf], writes=[pb])
                src = pb.t[:, :].rearrange("p (k n) -> p k n", k=4)
                dst = dst_fn(half)
                if half == 0:
                    K.op("act", lambda e, src=src, dst=dst: e.activation(out=dst, in_=src, func=AF.Copy), reads=[pb], pwrites=[dres])
                else:
                    K.op("dve", lambda e, src=src, dst=dst: e.tensor_copy(out=dst, in_=src), reads=[pb], pwrites=[dres])

        def phase_h(x_src, l):
            K.dma("sp", gB.t[:], norm_w[l:l + 1, :].broadcast_to([128, D]), writes=[gB])
            for t in range(NT):
                i = t % 2
                K.dma("sp", xt[i].t[:], x_src[t * 128:(t + 1) * 128, :], reads=[xs_res], writes=[xt[i]])
                rmsnorm_tile(xt[i], i, gB)
                to_T(i, lambda half, t=t: hT.t[:, half * 4:(half + 1) * 4, t * 128:(t + 1) * 128], hT)

        def load_w(src_rows_ap, ncols=128, nk=KC):
            b = wsl[wctr[0] % len(wsl)]
            wctr[0] += 1
            K.dma("pool", b.t[:, 0:nk, 0:ncols], src_rows_ap.rearrange("(k p) n -> p k n", p=128), writes=[b])
            return b

        def win_cols(l, c0, n=128):
            return w_in[l * D:(l + 1) * D, c0:c0 + n]

        def proj_F(wb, evac_fn, nblk=8):
            for blk in range(nblk):
                pb = psb[2 + (blk % 2)]
                for kc in range(KC):
                    rhs = hT.t[:, kc, blk * 512:(blk + 1) * 512]
                    K.op("pe", lambda e, kc=kc, rhs=rhs, pb=pb: e.matmul(pb.t[:, :], lhsT=wb.t[:, kc, :], rhs=rhs,
                                                                         start=(kc == 0), stop=(kc == KC - 1)),
                         reads=[wb, hT], writes=[pb])
                evac_fn(blk, pb)

        def proj_T(wb, evac_fn, tok_fn=None, ncols=128):
            for t4 in range(NT // 4):
                pb = psb[2 + (t4 % 2)]
                for j in range(4):
                    t = t4 * 4 + j
                    for kc in range(KC):
                        lhsT = tok_fn(hT.t[:, kc, :], t) if tok_fn else hT.t[:, kc, t * 128:(t + 1) * 128]
                        K.op("pe", lambda e, kc=kc, lhsT=lhsT, pb=pb, j=j: e.matmul(pb.t[:, j * 128:j * 128 + ncols], lhsT=lhsT,
                                                                                    rhs=wb.t[:, kc, 0:ncols],
                                                                                    start=(kc == 0), stop=(kc == KC - 1)),
                             reads=[wb, hT], writes=[pb])
                evac_fn(t4, pb)

        sset = [(psb[4], psb[5]), (psb[0], psb[1])]
        oslot = [(psb[6].t[:, 0:65], psb[6]), (psb[2].t[:, 0:65], psb[2])]
        sset1 = [psb[4], psb[5], psb[0], psb[1]]

        def attn_scores(u):
            ktiles = u["kt"]
            n = len(ktiles)
            if u.get("single"):
                k_ = V.ptc % 4
                pa, pb5 = sset1[k_], None
                u["os"] = (pa.t[:, 384:449], pa)
            else:
                k_ = V.ptc % 2
                pa, pb5 = sset[k_]
                u["os"] = oslot[k_]
            V.ptc += 1
            pt = V.PT[k_]
            u["pt"] = pt
            for j, kt in enumerate(ktiles):
                dres = pa if j < 4 else pb5
                dst = pa.t[:, j * 128:(j + 1) * 128] if j < 4 else pb5.t[:, 0:128]
                nadd = len(kt["add"])
                K.op("pe", lambda e, dst=dst, kt=kt, nadd=nadd: e.matmul(dst, lhsT=kt["k"], rhs=u["q"], start=True, stop=(nadd == 0)),
                     reads=[V.qT, V.kT], writes=[dres])
                for ai, (aap, ares) in enumerate(kt["add"]):
                    last = ai == nadd - 1
                    K.op("pe", lambda e, dst=dst, aap=aap, last=last: e.matmul(dst, lhsT=identb.t[:], rhs=aap, start=False, stop=last),
                         reads=[identb, ares], writes=[dres])
            n4 = min(n, 4)
            mul = u.get("mul")
            ex = (V.PTf[k_] if u.get("single") else V.PTf[k_ % 2]) if mul else pt
            K.op("act", lambda e: e.activation(out=ex.t[:, 0:n4 * 128], in_=pa.t[:, 0:n4 * 128], func=AF.Exp, scale=0.125),
                 reads=[pa], writes=[ex])
            if n > 4:
                K.op("act", lambda e: e.activation(out=ex.t[:, 512:640], in_=pb5.t[:, 0:128], func=AF.Exp, scale=0.125),
                     reads=[pb5], pwrites=[ex])
            if mul:
                K.op("dve", lambda e: e.tensor_tensor(out=pt.t[:, 0:n * 128], in0=ex.t[:, 0:n * 128], in1=mul[0], op=ALU.mult),
                     reads=[ex, mul[1]], writes=[pt])

        def attn_pv(u):
            ktiles = u["kt"]
            n = len(ktiles)
            pt = u["pt"]
            oap, ores = u["os"]
            for j, kt in enumerate(ktiles):
                K.op("pe", lambda e, j=j, kt=kt: e.matmul(oap, lhsT=pt.t[:, j * 128:(j + 1) * 128], rhs=kt["v"],
                                                          start=(j == 0), stop=(j == n - 1)),
                     reads=[pt, V.vaug], writes=[ores])
            u["post"](oap, ores)

        def run_units(units, pipelined=True, depth=0):
            if depth:
                V.ptc = 0
                for idx, u in enumerate(units):
                    u["single"] = True
                    attn_scores(u)
                    if idx >= depth:
                        attn_pv(units[idx - depth])
                for u in units[max(0, len(units) - depth):]:
                    attn_pv(u)
                V.ptc = 0
                return
            if not pipelined:
                for u in units:
                    attn_scores(u)
                    attn_pv(u)
                return
            prev = None
            for u in units:
                attn_scores(u)
                if prev is not None:
                    attn_pv(prev)
                prev = u
            if prev is not None:
                attn_pv(prev)

        def transpose_out(src, dstT, dram_dst, dres):
            for t4 in range(NT // 4):
                for j in range(4):
                    t = t4 * 4 + j
                    K.op("pe", lambda e, t=t, j=j: e.transpose(out=pst.t[:, j * 128:(j + 1) * 128], in_=src.t[:, t, :], identity=identb.t[:]),
                         reads=[src, identb], writes=[pst])
                if t4 % 2 == 0:
                    K.op("act", lambda e, t4=t4: e.activation(out=dstT.t[:, t4 * 512:(t4 + 1) * 512], in_=pst.t[:, 0:512], func=AF.Copy),
                         reads=[pst], pwrites=[dstT])
                else:
                    K.op("dve", lambda e, t4=t4: e.tensor_copy(out=dstT.t[:, t4 * 512:(t4 + 1) * 512], in_=pst.t[:, 0:512]),
                         reads=[pst], pwrites=[dstT])
            K.dma("sp", dram_dst, dstT.t[:], reads=[dstT], pwrites=[dres])

        def alloc_AB():
            V.qT = K.sb([128, T], BF16, "qT")
            V.kT = K.sb([128, T], BF16, "kT")
            V.vaug = K.sb([128, NT, 2, 65], BF16, "vaug")
            V.sg = K.sb([128, NT, 128], BF16, "sg")
            V.yg = K.sb([128, NT, 128], BF16, "yg")
            V.ygT = K.sb([128, T], BF16, "ygT")
            V.g8 = K.sb([128, 2, 7, 128], BF16, "g8")
            V.wvg = K.sb([128, KC, 256], BF16, "wvg")
            V.EB = K.sb([128, 2, NCASE, 128], BF16, "EB")
            V.PTf = [K.sb([128, 640], F32, "PTf%d" % i) for i in range(2)]
            V.PT = [K.sb([128, 640 if i < 2 else 384], BF16, "PT%d" % i) for i in range(4)]
            V.ptc = 0
            V.rd = [K.sb([128, 2], F32, "rd%d" % i) for i in range(2)]
            K.op("pool", lambda e: e.memset(V.vaug.t[:], 1.0), writes=[V.vaug])
            V.ropeC = K.sb([128, T], BF16, "ropeC")
            V.ropeS = K.sb([128, T], BF16, "ropeS")
            for c8 in range(8):
                cs = slice(c8 * 512, (c8 + 1) * 512)
                K.dma("pool", V.ropeC.t[:, cs], rope_c[:, cs], pwrites=[V.ropeC])
                K.dma("pool", V.ropeS.t[:, cs], rope_s[:, cs], pwrites=[V.ropeS])
            V.pm = K.sb([128, 128], BF16, "pm")
            K.dma("pool", V.pm.t[:], rope_pm, writes=[V.pm])
            V.qraw = [K.sb([128, 512], BF16, "qraw%d" % i) for i in range(2)]
            V.rt1 = K.sb([128, 512], F32, "rt1")
            V.rt2 = K.sb([128, 512], F32, "rt2")
            V.ndst = [K.sb([128, 2, 65], F32, "ndst%d" % i) for i in range(2)]
            V.nda = [K.sb([128, 3, 130], F32, "nda%d" % i) for i in range(2)]
            V.nds = [K.sb([128, 2, 65], F32, "nds%d" % i) for i in range(2)]

        def phase_A(l):
            qT, kT, vaug, sg, yg, g8 = V.qT, V.kT, V.vaug, V.sg, V.yg, V.g8
            for hp in range(8):
                wq = load_w(win_cols(l, OFF_QA + hp * 128))
                wk = load_w(win_cols(l, OFF_KA + hp * 128))
                K.dma("pool", V.wvg.t[:, :, 0:128], win_cols(l, OFF_VA + hp * 128).rearrange("(k p) n -> p k n", p=128), pwrites=[V.wvg])
                K.dma("pool", V.wvg.t[:, :, 128:256], win_cols(l, OFF_GA + hp * 128).rearrange("(k p) n -> p k n", p=128), pwrites=[V.wvg])
                for hl in range(2):
                    h = hp * 2 + hl
                    r0 = (l * 16 + h) * 128
                    K.dma("pool", g8.t[:, hl, :, :].rearrange("p b c -> p (b c)"), rpbg[r0:r0 + 128, :], pwrites=[g8])
                g8v = g8.t[:].rearrange("p a b c -> p (a b c)")
                K.op("act", lambda e: e.activation(out=g8v, in_=g8v, func=AF.Exp), reads=[g8], writes=[g8])
                for hl in range(2):
                    for (c0_, n_, d0_) in EB_GROUPS:
                        K.op("pool", lambda e, hl=hl, c0_=c0_, n_=n_, d0_=d0_: e.tensor_tensor(
                            out=V.EB.t[:, hl, c0_:c0_ + n_, :], in0=g8.t[:, hl, d0_ + 3:d0_ + 3 + n_, :], in1=namask.t[:, c0_:c0_ + n_, :], op=ALU.mult),
                             reads=[g8, namask], pwrites=[V.EB])
                proj_F(wq, lambda blk, pb: K.op("act", lambda e: e.activation(out=qT.t[:, blk * 512:(blk + 1) * 512], in_=pb.t[:, :], func=AF.Copy),
                                                reads=[pb], pwrites=[qT]))
                proj_F(wk, lambda blk, pb: K.op("dve", lambda e: e.tensor_copy(out=kT.t[:, blk * 512:(blk + 1) * 512], in_=pb.t[:, :]),
                                                reads=[pb], pwrites=[kT]))
                for t2 in range(NT // 2):
                    pb = psb[2 + (t2 % 2)]
                    for j in range(2):
                        t = t2 * 2 + j
                        for kc in range(KC):
                            K.op("pe", lambda e, kc=kc, t=t, j=j, pb=pb: e.matmul(pb.t[:, j * 256:(j + 1) * 256], lhsT=hT.t[:, kc, t * 128:(t + 1) * 128],
                                                                                 rhs=V.wvg.t[:, kc, :], start=(kc == 0), stop=(kc == KC - 1)),
                                 reads=[V.wvg, hT], writes=[pb])
                    pv_ = pb.t[:, :].rearrange("p (t x c) -> p t x c", t=2, x=2)
                    K.op("dve", lambda e, t2=t2, pv_=pv_: e.tensor_copy(out=vaug.t[:, t2 * 2:(t2 + 1) * 2, :, 0:64],
                                                                        in_=pv_[:, :, 0, :].rearrange("p t (h c) -> p t h c", h=2)),
                         reads=[pb], pwrites=[vaug])
                    K.op("act", lambda e, t2=t2, pv_=pv_: e.activation(out=sg.t[:, t2 * 2:(t2 + 1) * 2, :], in_=pv_[:, :, 1, :], func=AF.Silu),
                         reads=[pb], pwrites=[sg])
                units = []
                cnt = 0
                for hl in range(2):
                    prt = slice(hl * 64, hl * 64 + 64)
                    for a in range(NT):
                        cnt += 1
                        kts = []
                        for (kt, ci, d) in per_a[a]:
                            kts.append(dict(k=kT.t[prt, kt * 128:(kt + 1) * 128], add=[], v=vaug.t[:, kt, hl, :]))
                        ci0 = per_a[a][0][1]
                        nci = len(per_a[a])

                        def post(oap, ores, r=V.rd[cnt % 2], a=a, hl=hl):
                            K.op("dve", lambda e: e.reciprocal(out=r.t[:, 0:1], in_=oap[:, 64:65]), reads=[ores], writes=[r])
                            K.op("dve", lambda e: e.scalar_tensor_tensor(out=yg.t[:, a, hl * 64:(hl + 1) * 64], in0=oap[:, 0:64],
                                                                          scalar=r.t[:, 0:1], in1=sg.t[:, a, hl * 64:(hl + 1) * 64],
                                                                          op0=ALU.mult, op1=ALU.mult),
                                 reads=[ores, r, sg], pwrites=[yg])
                        units.append(dict(q=qT.t[prt, a * 128:(a + 1) * 128], kt=kts, post=post,
                                          mul=(V.EB.t[:, hl, ci0:ci0 + nci, :].rearrange("p a b -> p (a b)"), V.EB)))
                run_units(units, pipelined=PIPE_A)
                transpose_out(yg, V.ygT, yagT[hp * 128:(hp + 1) * 128, :], yagT_res)

        def proj_rope(wb, dst, d):
            rt1, rt2 = V.rt1, V.rt2
            for blk in range(8):
                pa, pb2 = psb[2 + blk % 2], psb[4 + blk % 2]
                qr = V.qraw[blk % 2]
                for kc in range(KC):
                    rhs = res_cols(hT.t[:, kc, :], d, blk * 512, 512)
                    K.op("pe", lambda e, kc=kc, rhs=rhs, pa=pa: e.matmul(pa.t[:, :], lhsT=wb.t[:, kc, :], rhs=rhs, start=(kc == 0), stop=(kc == KC - 1)),
                         reads=[wb, hT], writes=[pa])
                K.op("dve", lambda e, pa=pa, qr=qr: e.tensor_copy(out=qr.t[:, :], in_=pa.t[:, :]), reads=[pa], writes=[qr])
                K.op("pe", lambda e, pb2=pb2, qr=qr: e.matmul(pb2.t[:, :], lhsT=V.pm.t[:], rhs=qr.t[:, :], start=True, stop=True),
                     reads=[V.pm, qr], writes=[pb2])
                cc = res_cols(V.ropeC.t[:, :], d, blk * 512, 512)
                sc = res_cols(V.ropeS.t[:, :], d, blk * 512, 512)
                shp = cc.shape[1] if len(cc.shape) == 3 else None

                def v(ap, shp=shp):
                    return ap.rearrange("p (a b) -> p a b", a=shp) if shp else ap
                K.op("dve", lambda e, cc=cc, v=v, pa=pa: e.tensor_tensor(out=v(rt1.t[:, :]), in0=v(pa.t[:, :]), in1=cc, op=ALU.mult),
                     reads=[pa, V.ropeC], writes=[rt1])
                K.op("dve", lambda e, sc=sc, v=v, pb2=pb2: e.tensor_tensor(out=v(rt2.t[:, :]), in0=v(pb2.t[:, :]), in1=sc, op=ALU.mult),
                     reads=[pb2, V.ropeS], writes=[rt2])
                K.op("pool", lambda e, blk=blk: e.tensor_tensor(out=dst.t[:, blk * 512:(blk + 1) * 512], in0=rt1.t[:, :], in1=rt2.t[:, :], op=ALU.add),
                     reads=[rt1, rt2], pwrites=[dst])

        def phase_B(l):
            qT, kT, vaug, sg, yg = V.qT, V.kT, V.vaug, V.sg, V.yg
            DIL = (1, 4, 16)
            for sp_ in range(LIM_SP):
                for g, d in list(enumerate(DIL))[:LIM_G]:
                    L = T // d
                    tpr = L // 128
                    hc = (g * 8 + 2 * sp_) * 64
                    wq = load_w(win_cols(l, OFF_QB + hc))
                    wk = load_w(win_cols(l, OFF_KB + hc))
                    wv = load_w(win_cols(l, OFF_VB + hc))
                    proj_rope(wq, qT, d)
                    proj_rope(wk, kT, d)
                    proj_T(wv, lambda t4, pb: K.op("dve", lambda e: e.tensor_copy(out=vaug.t[:, t4 * 4:(t4 + 1) * 4, :, 0:64],
                                                                                   in_=pb.t[:, :].rearrange("p (t h c) -> p t h c", t=4, h=2)),
                                                   reads=[pb], pwrites=[vaug]),
                           tok_fn=lambda ap2, t, d=d: res_cols(ap2, d, t * 128, 128))
                    ndv = nd[g].rearrange("(m r) c -> r m c", r=d)
                    units = []
                    for Tq in range(NT):
                        tq = Tq % tpr
                        r0, m0 = (Tq * 128) // L, (Tq * 128) % L
                        stg = V.ndst[Tq % 2]
                        for hl in range(2):
                            prt = slice(hl * 64, hl * 64 + 64)
                            kts = []
                            dls = [dl for dl in (-1, 0, 1) if 0 <= tq + dl < tpr]
                            for dl in dls:
                                kt = Tq + dl
                                kts.append(dict(k=kT.t[prt, kt * 128:(kt + 1) * 128], add=[], v=vaug.t[:, kt, hl, :]))
                            mulB = (dmask.t[:, dls[0] + 1:dls[0] + 1 + len(dls), :].rearrange("p a b -> p (a b)"), dmask)

                            def post(oap, ores, hl=hl, stg=stg, r0=r0, m0=m0, g=g, ndv=ndv):
                                K.op("dve", lambda e: e.tensor_copy(out=stg.t[:, hl, :], in_=oap), reads=[ores], pwrites=[stg])
                                if hl == 1:
                                    K.dma("sp", ndv[r0, m0:m0 + 128, sp_ * 130:(sp_ + 1) * 130], stg.t[:, :, :].rearrange("p a b -> p (a b)"),
                                          reads=[stg], pwrites=[nd_res[g]])
                            units.append(dict(q=qT.t[prt, Tq * 128:(Tq + 1) * 128], kt=kts, post=post, mul=mulB))
                    run_units(units, pipelined=PIPE_B, depth=DEPTH_B)
                wg = load_w(win_cols(l, OFF_GB + sp_ * 128))
                proj_T(wg, lambda t4, pb: K.op("act", lambda e: e.activation(out=sg.t[:, t4 * 4:(t4 + 1) * 4, :],
                                                                             in_=pb.t[:, :].rearrange("p (t c) -> p t c", t=4), func=AF.Silu),
                                               reads=[pb], pwrites=[sg]))
                for t in range(NT):
                    na_, ns_ = V.nda[t % 2], V.nds[t % 2]
                    for g in range(3):
                        K.dma("sp", na_.t[:, g, :], nd[g][t * 128:(t + 1) * 128, sp_ * 130:(sp_ + 1) * 130], reads=[nd_res[g]], pwrites=[na_])
                    nsv = ns_.t[:, :, :].rearrange("p a b -> p (a b)")
                    K.op("pool", lambda e, na_=na_, nsv=nsv: e.tensor_tensor(out=nsv, in0=na_.t[:, 0, :], in1=na_.t[:, 1, :], op=ALU.add),
                         reads=[na_], writes=[ns_])
                    K.op("pool", lambda e, na_=na_, nsv=nsv: e.tensor_tensor(out=nsv, in0=nsv, in1=na_.t[:, 2, :], op=ALU.add),
                         reads=[na_, ns_], writes=[ns_])
                    r = V.rd[t % 2]
                    K.op("dve", lambda e, r=r, ns_=ns_: e.reciprocal(out=r.t[:, 0:2], in_=ns_.t[:, :, 64]), reads=[ns_], writes=[r])
                    for hl in range(2):
                        K.op("dve", lambda e, r=r, ns_=ns_, t=t, hl=hl: e.scalar_tensor_tensor(
                            out=yg.t[:, t, hl * 64:(hl + 1) * 64], in0=ns_.t[:, hl, 0:64], scalar=r.t[:, hl:hl + 1],
                            in1=sg.t[:, t, hl * 64:(hl + 1) * 64], op0=ALU.mult, op1=ALU.mult),
                             reads=[ns_, r, sg], pwrites=[yg])
                transpose_out(yg, V.ygT, ybgT[sp_ * 128:(sp_ + 1) * 128, :], ybgT_res)

        def tr_to_tok(srcT, dst, dram_view, dres):
            for t4 in range(NT // 4):
                for j in range(4):
                    t = t4 * 4 + j
                    K.op("pe", lambda e, t=t, j=j: e.transpose(out=pst.t[:, j * 128:(j + 1) * 128], in_=srcT.t[:, t * 128:(t + 1) * 128], identity=identb.t[:]),
                         reads=[srcT, identb], writes=[pst])
                src = pst.t[:, 0:512].rearrange("p (t c) -> p t c", t=4)
                if t4 % 2 == 0:
                    K.op("act", lambda e, t4=t4, src=src: e.activation(out=dst.t[:, t4 * 4:(t4 + 1) * 4, :], in_=src, func=AF.Copy),
                         reads=[pst], pwrites=[dst])
                else:
                    K.op("dve", lambda e, t4=t4, src=src: e.tensor_copy(out=dst.t[:, t4 * 4:(t4 + 1) * 4, :], in_=src),
                         reads=[pst], pwrites=[dst])
            for q4 in range(4):
                K.dma("sp", dram_view[:, q4 * 8:(q4 + 1) * 8, :], dst.t[:, q4 * 8:(q4 + 1) * 8, :], reads=[dst], pwrites=[dres])

        def phase_C(l):
            with ExitStack() as subC:
                K.st = subC
                dt_all = K.sb([128, NT, 48], F32, "dt_all")
                a_all = K.sb([128, NT, 48], F32, "a_all")
                dec_all = K.sb([128, NT, 48], F32, "dec_all")
                with ExitStack() as sub1:
                    K.st = sub1
                    cw = K.sb([128, 20, 5], F32, "cw")
                    cb_ = K.sb([128, 20], F32, "cb_")
                    K.dma("sp", cw.t[:], convw[l * 128:(l + 1) * 128, :].rearrange("p (c k) -> p c k", c=20), writes=[cw])
                    K.dma("sp", cb_.t[:], convb[l * 128:(l + 1) * 128, :], writes=[cb_])
                    xpad = K.sb([128, T + 4], BF16, "xpad")
                    K.op("pool", lambda e: e.memset(xpad.t[:, 0:2], 0.0), pwrites=[xpad])
                    K.op("pool", lambda e: e.memset(xpad.t[:, T + 2:T + 4], 0.0), pwrites=[xpad])
                    dg = [K.sb([128, 5, 128], BF16, "dg%d" % i) for i in range(2)]
                    xcT = K.sb([128, T], BF16, "xcT")
                    xtk = K.sb([128, NT, 128], BF16, "xtk")
                    for cbk in range(20):
                        w = load_w(win_cols(l, OFF_XBC + cbk * 128))
                        dgc = dg[cbk % 2]
                        for k in range(5):
                            K.op("pool", lambda e, k=k, dgc=dgc: e.tensor_scalar(out=dgc.t[:, k, :], in0=identf.t[:], scalar1=cw.t[:, cbk, k:k + 1], scalar2=None, op0=ALU.mult),
                                 reads=[identf, cw], pwrites=[dgc])
                        proj_F(w, lambda blk, pb: K.op("act", lambda e: e.activation(out=xpad.t[:, 2 + blk * 512:2 + (blk + 1) * 512], in_=pb.t[:, :], func=AF.Copy),
                                                       reads=[pb], pwrites=[xpad]))
                        for blk in range(8):
                            pc = psb[4 + blk % 2]
                            for k in range(5):
                                K.op("pe", lambda e, k=k, blk=blk, pc=pc: e.matmul(pc.t[:, :], lhsT=dgc.t[:, k, :], rhs=xpad.t[:, blk * 512 + k:blk * 512 + k + 512],
                                                                                  start=(k == 0), stop=(k == 4)),
                                     reads=[dgc, xpad], writes=[pc])
                            K.op("act", lambda e, blk=blk, pc=pc: e.activation(out=xcT.t[:, blk * 512:(blk + 1) * 512], in_=pc.t[:, :], func=AF.Silu,
                                                                               bias=cb_.t[:, cbk:cbk + 1], scale=1.0),
                                 reads=[pc, cb_], pwrites=[xcT])
                        if cbk < 12:
                            tr_to_tok(xcT, xtk, xtok[:, cbk * 128:(cbk + 1) * 128].rearrange("(t p) c -> p t c", p=128), xtok_res)
                        else:
                            K.dma("sp", bcT[(cbk - 12) * 128:(cbk - 11) * 128, :], xcT.t[:], reads=[xcT], pwrites=[bcT_res])
                            if cbk < 16:
                                tr_to_tok(xcT, xtk, btok[:, (cbk - 12) * 128:(cbk - 11) * 128].rearrange("(t p) c -> p t c", p=128), btok_res)
                    alb = K.sb([128, 48], F32, "alb")
                    dtb = K.sb([128, 4, 48], F32, "dtb")
                    A4 = K.sb([128, 4, 48], F32, "A4")
                    dtt = K.sb([128, 4, 48], F32, "dtt")
                    K.dma("sp", alb.t[:], a_log[l:l + 1, :].broadcast_to([128, 48]), writes=[alb])
                    K.op("act", lambda e: e.activation(out=alb.t[:], in_=alb.t[:], func=AF.Exp), reads=[alb], writes=[alb])
                    for j in range(4):
                        K.op("pool", lambda e, j=j: e.tensor_scalar(out=A4.t[:, j, :], in0=alb.t[:], scalar1=-1.0, scalar2=None, op0=ALU.mult),
                             reads=[alb], pwrites=[A4])
                        K.dma("sp", dtb.t[:, j, :], dt_bias[l:l + 1, :].broadcast_to([128, 48]), pwrites=[dtb])
                    wdt = load_w(win_cols(l, OFF_DT, 48), ncols=48)

                    def dt_evac(t4, pb):
                        src = pb.t[:, :].rearrange("p (j c) -> p j c", j=4)[:, :, 0:48]
                        K.op("dve", lambda e: e.tensor_tensor(out=dtt.t[:], in0=src, in1=dtb.t[:], op=ALU.add), reads=[pb, dtb], writes=[dtt])
                        K.op("act", lambda e: e.activation(out=dtt.t[:], in_=dtt.t[:], func=AF.Exp), reads=[dtt], writes=[dtt])
                        K.op("act", lambda e: e.activation(out=dt_all.t[:, t4 * 4:(t4 + 1) * 4, :], in_=dtt.t[:], func=AF.Ln, bias=oneb.t[:, 0:1], scale=1.0),
                             reads=[dtt, oneb], pwrites=[dt_all])
                        K.op("pool", lambda e: e.tensor_tensor(out=a_all.t[:, t4 * 4:(t4 + 1) * 4, :], in0=dt_all.t[:, t4 * 4:(t4 + 1) * 4, :], in1=A4.t[:], op=ALU.mult),
                             reads=[dt_all, A4], pwrites=[a_all])
                    proj_T(wdt, dt_evac, ncols=48)
                    av = a_all.t[:].rearrange("p c h -> p (c h)")
                    dv = dec_all.t[:].rearrange("p c h -> p (c h)")
                    for j in range(3):
                        pb = psb[2 + (j % 2)]
                        K.op("pe", lambda e, j=j, pb=pb: e.matmul(pb.t[:, :], lhsT=onesf.t[:], rhs=av[:, j * 512:(j + 1) * 512], start=True, stop=True),
                             reads=[onesf, a_all], writes=[pb])
                        K.op("act", lambda e, j=j, pb=pb: e.activation(out=dv[:, j * 512:(j + 1) * 512], in_=pb.t[:, :], func=AF.Exp),
                             reads=[pb], pwrites=[dec_all])
                    wz = [K.sb([128, KC, 512], BF16, "wz%d" % i) for i in range(2)]
                    zst = [K.sb([128, 512], BF16, "zst%d" % i) for i in range(2)]
                    for zb in range(3):
                        wzb = wz[zb % 2]
                        K.dma("pool", wzb.t[:], win_cols(l, OFF_Z + zb * 512, 512).rearrange("(k p) n -> p k n", p=128), writes=[wzb])
                        for t in range(NT):
                            pb = psb[2 + (t % 2)]
                            zs = zst[t % 2]
                            for kc in range(KC):
                                K.op("pe", lambda e, kc=kc, t=t, pb=pb: e.matmul(pb.t[:, :], lhsT=hT.t[:, kc, t * 128:(t + 1) * 128], rhs=wzb.t[:, kc, :],
                                                                                start=(kc == 0), stop=(kc == KC - 1)),
                                     reads=[wzb, hT], writes=[pb])
                            K.op("act", lambda e, pb=pb, zs=zs: e.activation(out=zs.t[:, :], in_=pb.t[:, :], func=AF.Silu), reads=[pb], writes=[zs])
                            K.dma("sp", szd[t * 128:(t + 1) * 128, zb * 512:(zb + 1) * 512], zs.t[:, :], reads=[zs], pwrites=[szd_res])
                    K.barrier()
                with ExitStack() as sub2:
                    K.st = sub2
                    xtk2 = [K.sb([128, 1536], BF16, "xtk2_%d" % i) for i in range(2)]
                    btk2 = [K.sb([128, 512], BF16, "btk2_%d" % i) for i in range(2)]
                    bct = [K.sb([128, 8, 128], BF16, "bct%d" % i) for i in range(2)]
                    xdt = K.sb([128, 24, 64], BF16, "xdt")
                    Est = K.sb([128, 24, 64], F32, "Est")
                    Ebf = K.sb([128, 24, 64], BF16, "Ebf")
                    gts = K.sb([128, 128], F32, "gts")
                    E3 = [K.sb([128, 3, 128], F32, "E3_%d" % i) for i in range(2)]
                    MT = [K.sb([128, 3, 128], BF16, "MT%d" % i) for i in range(2)]
                    xw = [K.sb([128, 3, 64], BF16, "xw%d" % i) for i in range(2)]
                    ecum = K.sb([128, 24], F32, "ecum")
                    tmpo = K.sb([128, 6, 64], F32, "tmpo")
                    yacc = [K.sb([128, 1536], F32, "yacc%d" % i) for i in range(2)]
                    yft = K.sb([128, 1536], F32, "yft")
                    tmpx = K.sb([128, 1536], F32, "tmpx")
                    szt2 = K.sb([128, 1536], BF16, "szt2")
                    ycb = K.sb([128, 1536], BF16, "ycb")
                    ycTt = K.sb([128, 12, 128], BF16, "ycTt")
                    dskb = K.sb([128, 24], F32, "dskb")
                    snw = K.sb([128, 1536], F32, "snw")
                    K.dma("sp", dskb.t[:], d_skip[l:l + 1, :].broadcast_to([128, 24]), writes=[dskb])
                    K.dma("sp", snw.t[:], ssm_nw[l:l + 1, :].broadcast_to([128, 1536]), writes=[snw])
                    xdt2 = [xdt, K.sb([128, 24, 64], BF16, "xdtb")]
                    ecum2 = [ecum, K.sb([128, 24], F32, "ecumb")]
                    gts2 = [gts, K.sb([128, 128], F32, "gtsb")]
                    ncum2 = [K.sb([128, 24], F32, "ncum%d" % i) for i in range(2)]
                    pending = []
                    for dirn in (0, 1):
                        K.op("pool", lambda e: e.memset(Est.t[:], 0.0), writes=[Est])
                        K.op("pool", lambda e: e.memset(Ebf.t[:], 0.0), writes=[Ebf])
                        chunks = list(range(NT)) if dirn == 0 else list(range(NT - 1, -1, -1))
                        tri, ntri = (triF, ntriF) if dirn == 0 else (triB, ntriB)
                        mk = mFB.t[:, dirn, :]
                        wc = 127 if dirn == 0 else 0

                        def stage1(u, dirn=dirn, tri=tri, ntri=ntri, mk=mk, wc=wc):
                            c, g, half = u["c"], u["g"], u["half"]
                            i = c % 2
                            xd, ec = xdt2[i], ecum2[i]
                            if g == 0 and half == 0:
                                K.dma("sp", xtk2[i].t[:], xtok[c * 128:(c + 1) * 128, :], reads=[xtok_res], writes=[xtk2[i]])
                                K.dma("sp", btk2[i].t[:], btok[c * 128:(c + 1) * 128, :], reads=[btok_res], writes=[btk2[i]])
                                K.dma("sp", bct[i].t[:], bcT[:, c * 128:(c + 1) * 128].rearrange("(k p) n -> p k n", p=128), reads=[bcT_res], writes=[bct[i]])
                                xv = xtk2[i].t[:].rearrange("p (h c) -> p h c", h=24)
                                dtv = dt_all.t[:, c, dirn * 24:(dirn + 1) * 24].unsqueeze(2).broadcast_to([128, 24, 64])
                                K.op("dve", lambda e: e.tensor_tensor(out=xd.t[:], in0=xv, in1=dtv, op=ALU.mult),
                                     reads=[xtk2[i], dt_all], writes=[xd])
                                K.op("pe", lambda e: e.matmul(psb[0].t[:, 0:24], lhsT=tri.t[:], rhs=a_all.t[:, c, dirn * 24:(dirn + 1) * 24], start=True, stop=True),
                                     reads=[tri, a_all], writes=[psb[0]])
                                K.op("act", lambda e: e.activation(out=ec.t[:], in_=psb[0].t[:, 0:24], func=AF.Exp), reads=[psb[0]], writes=[ec])
                            gt = gts2[g % 2]
                            ncm = ncum2[i]
                            if half == 0:
                                K.op("pe", lambda e: e.matmul(psb[1].t[:, 0:128], lhsT=bct[i].t[:, g, :], rhs=bct[i].t[:, 4 + g, :], start=True, stop=True),
                                     reads=[bct[i]], writes=[psb[1]])
                                K.op("act", lambda e: e.activation(out=gt.t[:], in_=psb[1].t[:, 0:128], func=AF.Copy), reads=[psb[1]], writes=[gt])
                            h0 = g * 6 + half * 3
                            pr = psb[2 + half]
                            for j in range(3):
                                col = dirn * 24 + h0 + j
                                abc = a_all.t[:, c, col:col + 1].broadcast_to([128, 128])
                                dst = pr.t[:, j * 128:(j + 1) * 128]
                                K.op("pe", lambda e, abc=abc, dst=dst: e.matmul(dst, lhsT=abc, rhs=tri.t[:], start=True, stop=False),
                                     reads=[a_all, tri], writes=[pr])
                                K.op("pe", lambda e, abc=abc, dst=dst: e.matmul(dst, lhsT=ntri.t[:], rhs=abc, start=False, stop=False),
                                     reads=[a_all, ntri], writes=[pr])
                                K.op("pe", lambda e, dst=dst: e.matmul(dst, lhsT=identb.t[:], rhs=mk, start=False, stop=True),
                                     reads=[identb, mFB], writes=[pr])
                            e3 = E3[half]
                            K.op("act", lambda e: e.activation(out=e3.t[:].rearrange("p a b -> p (a b)"), in_=pr.t[:, 0:384], func=AF.Exp),
                                 reads=[pr], writes=[e3])
                            mt = MT[half]
                            K.op("dve", lambda e: e.tensor_tensor(out=mt.t[:], in0=e3.t[:], in1=gt.t[:].unsqueeze(1).broadcast_to([128, 3, 128]), op=ALU.mult),
                                 reads=[e3, gt], writes=[mt])
                            xw_ = xw[half]
                            K.op("dve", lambda e: e.tensor_tensor(out=xw_.t[:], in0=xd.t[:, h0:h0 + 3, :],
                                                                  in1=e3.t[:, :, wc:wc + 1].broadcast_to([128, 3, 64]), op=ALU.mult),
                                 reads=[e3, xd], writes=[xw_])

                        def stage2(u, dirn=dirn):
                            c, g, half = u["c"], u["g"], u["half"]
                            i = c % 2
                            xd, ec = xdt2[i], ecum2[i]
                            ya = yacc[i]
                            h0 = g * 6 + half * 3
                            mt, xw_ = MT[half], xw[half]
                            for j in range(3):
                                h = h0 + j
                                hj = half * 3 + j
                                K.op("pe", lambda e, j=j, h=h, hj=hj: e.matmul(psb[4].t[:, hj * 64:(hj + 1) * 64], lhsT=mt.t[:, j, :], rhs=xd.t[:, h, :], start=True, stop=True),
                                     reads=[mt, xd], writes=[psb[4]])
                                K.op("pe", lambda e, h=h, hj=hj: e.matmul(psb[5].t[:, hj * 64:(hj + 1) * 64], lhsT=bct[i].t[:, 4 + g, :], rhs=Ebf.t[:, h, :], start=True, stop=True),
                                     reads=[bct[i], Ebf], writes=[psb[5]])
                                K.op("pe", lambda e, j=j, hj=hj: e.matmul(psb[6].t[:, hj * 64:(hj + 1) * 64], lhsT=btk2[i].t[:, g * 128:(g + 1) * 128], rhs=xw_.t[:, j, :], start=True, stop=True),
                                     reads=[btk2[i], xw_], writes=[psb[6]])
                            if half == 1:
                                g6 = slice(g * 6, (g + 1) * 6)
                                ecv = ec.t[:, g6].unsqueeze(2).broadcast_to([128, 6, 64])
                                K.op("dve", lambda e: e.tensor_tensor(out=tmpo.t[:], in0=psb[5].t[:, 0:384].rearrange("p (a b) -> p a b", a=6), in1=ecv, op=ALU.mult),
                                     reads=[psb[5], ec], writes=[tmpo])
                                K.op("dve", lambda e: e.tensor_tensor(out=ya.t[:, g * 384:(g + 1) * 384], in0=tmpo.t[:].rearrange("p a b -> p (a b)"), in1=psb[4].t[:, 0:384], op=ALU.add),
                                     reads=[tmpo, psb[4]], pwrites=[ya])
                                dcv = dec_all.t[:, c, dirn * 24 + g * 6:dirn * 24 + (g + 1) * 6].unsqueeze(2).broadcast_to([128, 6, 64])
                                K.op("dve", lambda e: e.tensor_tensor(out=Est.t[:, g6, :], in0=Est.t[:, g6, :], in1=dcv, op=ALU.mult),
                                     reads=[Est, dec_all], writes=[Est])
                                K.op("dve", lambda e: e.tensor_tensor(out=Est.t[:, g6, :], in0=Est.t[:, g6, :], in1=psb[6].t[:, 0:384].rearrange("p (a b) -> p a b", a=6), op=ALU.add),
                                     reads=[Est, psb[6]], writes=[Est])
                                K.op("act", lambda e: e.activation(out=Ebf.t[:, g6, :], in_=Est.t[:, g6, :], func=AF.Copy), reads=[Est], writes=[Ebf])
                            if g == 3 and half == 1:
                                if dirn == 0:
                                    K.dma("sp", yfd[c * 128:(c + 1) * 128, :], ya.t[:], reads=[ya], pwrites=[yfd_res])
                                else:
                                    xv = xtk2[i].t[:].rearrange("p (h c) -> p h c", h=24)
                                    K.dma("sp", yft.t[:], yfd[c * 128:(c + 1) * 128, :], reads=[yfd_res], writes=[yft])
                                    K.dma("sp", szt2.t[:], szd[c * 128:(c + 1) * 128, :], reads=[szd_res], writes=[szt2])
                                    K.op("pool", lambda e: e.tensor_tensor(out=ya.t[:], in0=ya.t[:], in1=yft.t[:], op=ALU.add), reads=[ya, yft], writes=[ya])
                                    K.op("dve", lambda e: e.tensor_tensor(out=tmpx.t[:].rearrange("p (h c) -> p h c", h=24), in0=xv,
                                                                          in1=dskb.t[:].unsqueeze(2).broadcast_to([128, 24, 64]), op=ALU.mult),
                                         reads=[xtk2[i], dskb], writes=[tmpx])
                                    K.op("pool", lambda e: e.tensor_tensor(out=ya.t[:], in0=ya.t[:], in1=tmpx.t[:], op=ALU.add), reads=[ya, tmpx], writes=[ya])
                                    K.op("dve", lambda e: e.tensor_tensor(out=ya.t[:], in0=ya.t[:], in1=szt2.t[:], op=ALU.mult), reads=[ya, szt2], writes=[ya])
                                    s_ = sml[i]
                                    rms_scale(ya.t[:], s_, [ya], 1536, tmpx.t[:], tmpx)
                                    K.op("dve", lambda e: e.scalar_tensor_tensor(out=ycb.t[:], in0=ya.t[:], scalar=s_.t[:, 2:3], in1=snw.t[:], op0=ALU.mult, op1=ALU.mult),
                                         reads=[ya, s_, snw], writes=[ycb])
                                    def part2(c=c):
                                        for rnd, (k0, k1) in enumerate(((0, 8), (8, 12))):
                                            for k in range(k0, k1):
                                                K.op("pe", lambda e, k=k, k0=k0: e.transpose(out=pst.t[:, (k - k0) * 128:(k - k0 + 1) * 128], in_=ycb.t[:, k * 128:(k + 1) * 128], identity=identb.t[:]),
                                                     reads=[ycb, identb], writes=[pst])
                                            n_ = k1 - k0
                                            K.op("act", lambda e, k0=k0, k1=k1, n_=n_: e.activation(out=ycTt.t[:, k0:k1, :], in_=pst.t[:, 0:n_ * 128].rearrange("p (k n) -> p k n", k=n_), func=AF.Copy),
                                                 reads=[pst], pwrites=[ycTt])
                                        K.dma("sp", ycT[:, c * 128:(c + 1) * 128].rearrange("(k p) n -> p k n", p=128), ycTt.t[:], reads=[ycTt], pwrites=[ycT_res])
                                    pending.append([part2, 0])

                        units = [dict(c=c, g=g, half=half) for c in chunks for g in range(4) for half in range(2)]
                        prev = None
                        for u in units:
                            stage1(u)
                            if PIPE_C and prev is not None:
                                stage2(prev)
                            if not PIPE_C:
                                stage2(u)
                            prev = u
                            for pd in list(pending):
                                pd[1] += 1
                                if pd[1] >= 4:
                                    pd[0]()
                                    pending.remove(pd)
                        if PIPE_C:
                            stage2(prev)
                        for pd in list(pending):
                            pd[0]()
                            pending.remove(pd)
                    K.barrier()
            K.st = st
        def phase_DE(l, last):
            x_src = x_in if l == 0 else xs
            with ExitStack() as subD:
                K.st = subD
                wD = [K.sb([128, 48, 128], BF16, "wD%d" % i) for i in range(2)]
                yblk = [K.sb([128, 24, 512], BF16, "yblk%d" % i) for i in range(2)]
                sgm = [K.sb([128, 512], F32, "sgm%d" % i) for i in range(3)]
                mm_ = [K.sb([128, 512], F32, "mm%d" % i) for i in range(3)]
                mTb = [K.sb([128, 512], BF16, "mTb%d" % i) for i in range(2)]
                cnt = 0
                for db in range(8):
                    w = wD[db % 2]
                    c0 = db * 128
                    K.dma("pool", w.t[:, 0:8, :], w_oa[l * 1024:(l + 1) * 1024, c0:c0 + 128].rearrange("(k p) n -> p k n", p=128), pwrites=[w])
                    K.dma("pool", w.t[:, 8:12, :], w_ob[l * 512:(l + 1) * 512, c0:c0 + 128].rearrange("(k p) n -> p k n", p=128), pwrites=[w])
                    K.dma("pool", w.t[:, 12:24, :], w_oc[l * 1536:(l + 1) * 1536, c0:c0 + 128].rearrange("(k p) n -> p k n", p=128), pwrites=[w])
                    for ui, off in enumerate((OFF_UA, OFF_UB, OFF_UC)):
                        K.dma("pool", w.t[:, 24 + ui * 8:32 + ui * 8, :], win_cols(l, off + c0).rearrange("(k p) n -> p k n", p=128), pwrites=[w])
                    for tb in range(8):
                        yb = yblk[cnt % 2]
                        mo = mTb[cnt % 2]
                        cnt += 1
                        tc_ = slice(tb * 512, (tb + 1) * 512)
                        K.dma("sp", yb.t[:, 0:8, :], yagT[:, tc_].rearrange("(k p) n -> p k n", p=128), reads=[yagT_res], pwrites=[yb])
                        K.dma("sp", yb.t[:, 8:12, :], ybgT[:, tc_].rearrange("(k p) n -> p k n", p=128), reads=[ybgT_res], pwrites=[yb])
                        K.dma("sp", yb.t[:, 12:24, :], ycT[:, tc_].rearrange("(k p) n -> p k n", p=128), reads=[ycT_res], pwrites=[yb])
                        for ui in range(3):
                            for kc in range(KC):
                                K.op("pe", lambda e, ui=ui, kc=kc: e.matmul(psb[ui].t[:, :], lhsT=w.t[:, 24 + ui * 8 + kc, :], rhs=hT.t[:, kc, tc_], start=(kc == 0), stop=(kc == KC - 1)),
                                     reads=[w, hT], writes=[psb[ui]])
                        for yi, (a, b) in enumerate(((0, 8), (8, 12), (12, 24))):
                            for k in range(a, b):
                                K.op("pe", lambda e, yi=yi, k=k, a=a, b=b: e.matmul(psb[3 + yi].t[:, :], lhsT=w.t[:, k, :], rhs=yb.t[:, k, :], start=(k == a), stop=(k == b - 1)),
                                     reads=[w, yb], writes=[psb[3 + yi]])
                        for ui in range(3):
                            K.op("act", lambda e, ui=ui: e.activation(out=sgm[ui].t[:, :], in_=psb[ui].t[:, :], func=AF.Sigmoid), reads=[psb[ui]], writes=[sgm[ui]])
                            K.op("dve", lambda e, ui=ui: e.tensor_tensor(out=mm_[ui].t[:, :], in0=sgm[ui].t[:, :], in1=psb[3 + ui].t[:, :], op=ALU.mult),
                                 reads=[sgm[ui], psb[3 + ui]], writes=[mm_[ui]])
                        K.op("pool", lambda e: e.tensor_tensor(out=mm_[0].t[:, :], in0=mm_[0].t[:, :], in1=mm_[1].t[:, :], op=ALU.add), reads=[mm_[0], mm_[1]], writes=[mm_[0]])
                        K.op("pool", lambda e, mo=mo: e.tensor_tensor(out=mo.t[:, :], in0=mm_[0].t[:, :], in1=mm_[2].t[:, :], op=ALU.add), reads=[mm_[0], mm_[2]], writes=[mo])
                        K.dma("pool", mrgT[c0:c0 + 128, tc_], mo.t[:, :], reads=[mo], pwrites=[mrgT_res])
                K.barrier()
            K.st = st
            if stop == "D":
                return
            with ExitStack() as subE:
                K.st = subE
                wout = K.sb([128, 8, 1024], BF16, "wout")
                wpgs = K.sb([128, 8, 1024], BF16, "wpgs")
                wpl = K.sb([128, 2, 1024], BF16, "wpl")
                gP = K.sb([128, D], F32, "gP")
                gF = K.sb([128, D], F32, "gF")
                mtl = [K.sb([128, 8, 128], BF16, "mtl%d" % i) for i in range(2)]
                ptl = [K.sb([128, 256], F32, "ptl%d" % i) for i in range(2)]
                x1b = [K.sb([128, D], F32, "x1b%d" % i) for i in range(2)]
                hx = [K.sb([128, 8, 128], BF16, "hx%d" % i) for i in range(2)]
                gate = K.sb([128, D], F32, "gate")
                pe_ = K.sb([128, D], F32, "pe_")
                pT = [K.sb([128, 2, 128], BF16, "pT%d" % i) for i in range(2)]
                for k in range(8):
                    K.dma("pool", wout.t[:, k, :], w_out[l * D + k * 128:l * D + (k + 1) * 128, :], pwrites=[wout])
                    K.dma("pool", wpgs.t[:, k, :], w_pg[l * D + k * 128:l * D + (k + 1) * 128, :], pwrites=[wpgs])
                for k in range(2):
                    K.dma("pool", wpl.t[:, k, :], w_ple[l * 256 + k * 128:l * 256 + (k + 1) * 128, :], pwrites=[wpl])
                K.dma("sp", gP.t[:], ple_norm_w[l:l + 1, :].broadcast_to([128, D]), writes=[gP])
                K.dma("sp", gF.t[:], final_norm_w[0:1, :].broadcast_to([128, D]), writes=[gF])
                def e_stage1(t):
                    i = t % 2
                    mt = mtl[i]
                    x1 = x1b[i]
                    rows = slice(t * 128, (t + 1) * 128)
                    K.dma("sp", mt.t[:], mrgT[:, rows].rearrange("(k p) n -> p k n", p=128), reads=[mrgT_res], writes=[mt])
                    K.dma("sp", xt[i].t[:], x_src[rows, :], reads=[xs_res], writes=[xt[i]])
                    K.dma("sp", ptl[i].t[:], p_in[l * T + t * 128:l * T + (t + 1) * 128, :], writes=[ptl[i]])
                    for half in range(2):
                        hs = slice(half * 512, (half + 1) * 512)
                        for kc in range(KC):
                            K.op("pe", lambda e, half=half, hs=hs, kc=kc: e.matmul(psb[half].t[:, :], lhsT=mt.t[:, kc, :], rhs=wout.t[:, kc, hs], start=(kc == 0), stop=(kc == KC - 1)),
                                 reads=[mt, wout], writes=[psb[half]])
                        K.op("dve", lambda e, half=half, hs=hs: e.tensor_tensor(out=x1.t[:, hs], in0=xt[i].t[:, hs], in1=psb[half].t[:, :], op=ALU.add),
                             reads=[xt[i], psb[half]], pwrites=[x1])
                    rmsnorm_tile(x1, i, gP)
                    to_T(i, lambda half: hx[i].t[:, half * 4:(half + 1) * 4, :], hx[i], pa=2)
                    for k in range(2):
                        K.op("pe", lambda e, k=k: e.transpose(out=psb[6].t[:, k * 128:(k + 1) * 128], in_=ptl[i].t[:, k * 128:(k + 1) * 128], identity=identf.t[:]),
                             reads=[ptl[i], identf], writes=[psb[6]])
                    K.op("act", lambda e: e.activation(out=pT[i].t[:].rearrange("p k n -> p (k n)"), in_=psb[6].t[:, 0:256], func=AF.Copy), reads=[psb[6]], writes=[pT[i]])

                def e_stage2(t):
                    i = t % 2
                    x1 = x1b[i]
                    rows = slice(t * 128, (t + 1) * 128)
                    for half in range(2):
                        hs = slice(half * 512, (half + 1) * 512)
                        for kc in range(KC):
                            K.op("pe", lambda e, half=half, hs=hs, kc=kc: e.matmul(psb[4 + half].t[:, :], lhsT=hx[i].t[:, kc, :], rhs=wpgs.t[:, kc, hs], start=(kc == 0), stop=(kc == KC - 1)),
                                 reads=[hx[i], wpgs], writes=[psb[4 + half]])
                        K.op("act", lambda e, half=half, hs=hs: e.activation(out=gate.t[:, hs], in_=psb[4 + half].t[:, :], func=AF.Sigmoid), reads=[psb[4 + half]], pwrites=[gate])
                    for half in range(2):
                        hs = slice(half * 512, (half + 1) * 512)
                        for k in range(2):
                            K.op("pe", lambda e, half=half, hs=hs, k=k: e.matmul(psb[4 + half].t[:, :], lhsT=pT[i].t[:, k, :], rhs=wpl.t[:, k, hs], start=(k == 0), stop=(k == 1)),
                                 reads=[pT[i], wpl], writes=[psb[4 + half]])
                        K.op("dve", lambda e, half=half, hs=hs: e.tensor_tensor(out=pe_.t[:, hs], in0=gate.t[:, hs], in1=psb[4 + half].t[:, :], op=ALU.mult),
                             reads=[gate, psb[4 + half]], pwrites=[pe_])
                    K.op("pool", lambda e: e.tensor_tensor(out=x1.t[:], in0=x1.t[:], in1=pe_.t[:], op=ALU.add), reads=[x1, pe_], writes=[x1])
                    if not last:
                        K.dma("pool", xs[rows, :], x1.t[:], reads=[x1], pwrites=[xs_res])
                    else:
                        fo = fout[i]
                        s_ = sml2[i]
                        rms_scale(x1.t[:], s_, [x1], D, fo.t[:], fo)
                        K.op("dve", lambda e: e.scalar_tensor_tensor(out=fo.t[:], in0=x1.t[:], scalar=s_.t[:, 2:3], in1=gF.t[:], op0=ALU.mult, op1=ALU.mult),
                             reads=[x1, s_, gF], writes=[fo])
                        K.dma("sp", y_out[rows, :], fo.t[:], reads=[fo])

                fout = [K.sb([128, D], F32, "fout%d" % i) for i in range(2)] if last else None
                sml2 = [K.sb([128, 4], F32, "sml2_%d" % i) for i in range(2)]
                prev = None
                for t in range(NT):
                    e_stage1(t)
                    if prev is not None:
                        e_stage2(prev)
                    prev = t
                e_stage2(prev)
                K.barrier()
            K.st = st

        for l in range(nlayers):
            phase_h(x_in if l == 0 else xs, l)
            if stop == "h":
                break
            if stop in (None, "A", "B", "D", "E"):
                with ExitStack() as sub:
                    K.st = sub
                    alloc_AB()
                    if stop != "B":
                        phase_A(l)
                    if stop != "A":
                        phase_B(l)
                    K.barrier()
                K.st = st
                if stop in ("A", "B"):
                    break
            if stop in (None, "C", "D", "E"):
                phase_C(l)
                if stop == "C":
                    break
            if stop in (None, "D", "E"):
                phase_DE(l, last=(l == nlayers - 1))
                if stop in ("D", "E"):
                    break

        if dbg:
            dk = K.sb([128, 1024], BF16, "dbgk")
            if stop == "h":
                for kc in range(KC):
                    for c4 in range(4):
                        K.op("act", lambda e, kc=kc, c4=c4: e.activation(out=xt[0].t[:, :], in_=hT.t[:, kc, c4 * 1024:(c4 + 1) * 1024], func=AF.Copy),
                             reads=[hT], writes=[xt[0]])
                        K.dma("sp", dbg_out[kc * 128:(kc + 1) * 128, c4 * 1024:(c4 + 1) * 1024], xt[0].t[:, :], reads=[xt[0]])
            elif stop in ("A", "B", "C", "D"):
                srcd, nk_ = {"A": (yagT, 8), "B": (ybgT, 4), "C": (ycT, 12), "D": (mrgT, 8)}[stop]
                for kc in range(nk_):
                    for c4 in range(4):
                        K.dma("sp", dk.t[:, :], srcd[kc * 128:(kc + 1) * 128, c4 * 1024:(c4 + 1) * 1024],
                              reads=[yagT_res, ybgT_res, ycT_res, mrgT_res], writes=[dk])
                        K.op("act", lambda e: e.activation(out=xt[0].t[:, :], in_=dk.t[:, :], func=AF.Copy), reads=[dk], writes=[xt[0]])
                        K.dma("sp", dbg_out[kc * 128:(kc + 1) * 128, c4 * 1024:(c4 + 1) * 1024], xt[0].t[:, :], reads=[xt[0]])
        K.finish()
    return nc


def host_inputs(inputs, b):
    cases, per_a, mask, dr, dc = na_tables()
    f = np.float32
    m = {}
    m["x"] = np.ascontiguousarray(inputs["x"][b], dtype=f)
    m["p"] = np.ascontiguousarray(inputs["p"][:, b], dtype=f).reshape(DEPTH * T, 256)
    m["norm_w"] = np.asarray(inputs["norm_w"], f)
    m["ple_norm_w"] = np.asarray(inputs["ple_norm_w"], f)
    m["final_norm_w"] = np.asarray(inputs["final_norm_w"], f).reshape(1, D)
    m["w_in"] = np.asarray(inputs["w_in"], f).reshape(DEPTH * D, IN_W)
    m["w_oa"] = np.asarray(inputs["w_oa"], f).reshape(DEPTH * 1024, D)
    m["w_ob"] = np.asarray(inputs["w_ob"], f).reshape(DEPTH * 512, D)
    m["w_oc"] = np.asarray(inputs["w_oc"], f).reshape(DEPTH * 1536, D)
    m["w_out"] = np.asarray(inputs["w_out"], f).reshape(DEPTH * D, D)
    m["w_ple"] = np.asarray(inputs["w_ple"], f).reshape(DEPTH * 256, D)
    m["w_ple_gate"] = np.asarray(inputs["w_ple_gate"], f).reshape(DEPTH * D, D)
    rpb = np.asarray(inputs["na_rpb"], f)
    g = rpb[:, :, dr, dc]
    m["rpbg"] = np.ascontiguousarray(g.transpose(0, 1, 3, 2, 4)).reshape(DEPTH * 16 * 128, 7 * 128)
    m["na_mask"] = np.ascontiguousarray(mask.transpose(1, 0, 2)).reshape(128, -1)
    m["dil_mask"] = np.ascontiguousarray(dil_masks().transpose(1, 0, 2)).reshape(128, -1)
    C, S = rope_tables()
    m["rope_c"], m["rope_s"] = C, S
    pm = np.eye(128, dtype=np.float32)
    for hh in range(2):
        b0 = hh * 64
        for i in range(8):
            pm[b0 + i, b0 + i] = 0.0
            pm[b0 + 8 + i, b0 + 8 + i] = 0.0
            pm[b0 + i, b0 + 8 + i] = 1.0
            pm[b0 + 8 + i, b0 + i] = 1.0
    m["rope_pm"] = pm
    cw = np.asarray(inputs["conv_w"], f)
    m["convw"] = np.ascontiguousarray(cw.reshape(DEPTH, 5, 20, 128).transpose(0, 3, 2, 1)).reshape(DEPTH * 128, 100)
    cb = np.asarray(inputs["conv_b"], f)
    m["convb"] = np.ascontiguousarray(cb.reshape(DEPTH, 20, 128).transpose(0, 2, 1)).reshape(DEPTH * 128, 20)
    m["a_log"] = np.asarray(inputs["a_log"], f).reshape(DEPTH, 48)
    m["dt_bias"] = np.asarray(inputs["dt_bias"], f).reshape(DEPTH, 48)
    m["d_skip"] = np.asarray(inputs["d_skip"], f).reshape(DEPTH, 24)
    m["ssm_norm_w"] = np.asarray(inputs["ssm_norm_w"], f).reshape(DEPTH, 1536)
    return m


def kernel(**inputs):
    nc = build()
    in_maps = [host_inputs(inputs, b) for b in range(8)]
    res = run_bass_kernel_spmd(nc, in_maps, core_ids=list(range(8)))
    return np.stack([r["y"] for r in res.results], axis=0).astype(np.float32)
```

```python
import numpy as np
import concourse.bass as bass
import concourse.mybir as mybir
from concourse.bass_utils import run_bass_kernel_spmd
from concourse.alu_op_type import AluOpType as ALU
from contextlib import ExitStack

F32 = mybir.dt.float32
BF16 = mybir.dt.bfloat16
AF = mybir.ActivationFunctionType

T = 4096
NT = 32
D = 1024
KC = 8
DEPTH = 4
IN_W = 16432
EPS = 1e-6
NEG = -30000.0
PIPE_B = True
PIPE_A = True
PIPE_C = True
DEPTH_B = 0
LIM_SP = 4
LIM_G = 3

OFF_QA, OFF_KA, OFF_VA, OFF_GA = 0, 1024, 2048, 3072
OFF_QB, OFF_KB, OFF_VB, OFF_GB = 4096, 5632, 7168, 8704
OFF_XBC, OFF_Z, OFF_DT = 9216, 11776, 13312
OFF_UA, OFF_UB, OFF_UC = 13360, 14384, 15408


class Res:
    __slots__ = ("w", "r", "war")

    def __init__(self):
        self.w = {}
        self.r = {}
        self.war = {}


def _merge(dst, src):
    for k, sv in src.items():
        o = dst.get(k)
        if o is None or o[1] < sv[1]:
            dst[k] = sv


class Buf:
    def __init__(self, t, psum=False):
        self.t = t
        self.r = Res()
        self.psum = psum


class Sched:
    ND = 12

    def __init__(self, nc, st):
        self.nc = nc
        self.st = st
        self.engs = {"pe": nc.tensor, "act": nc.scalar, "dve": nc.vector, "pool": nc.gpsimd, "sp": nc.sync}
        self.csem = {e: st.enter_context(nc.semaphore("c_" + e)) for e in self.engs}
        self.ccnt = {e: 0 for e in self.engs}
        self.seen = {e: {} for e in self.engs}
        self.dsems = {q: [st.enter_context(nc.semaphore("d_%s_%d" % (q, i))) for i in range(self.ND)] for q in ("sp", "pool")}
        self.dcnt = {q: [0] * self.ND for q in ("sp", "pool")}
        self.dnext = {q: 0 for q in ("sp", "pool")}
        self.nbuf = 0

    def sb(self, shape, dt, name=None):
        self.nbuf += 1
        return Buf(self.st.enter_context(self.nc.sbuf_tensor("%s_%d" % (name or "b", self.nbuf), list(shape), dt)))

    def ps(self, shape, dt, name=None):
        self.nbuf += 1
        return Buf(self.st.enter_context(self.nc.psum_tensor("%s_%d" % (name or "p", self.nbuf), list(shape), dt)), psum=True)

    def _deps(self, reads, writes, pwrites):
        deps = {}
        for r in reads:
            _merge(deps, r.w)
        for w in writes:
            _merge(deps, w.w)
            _merge(deps, w.r)
            _merge(deps, w.war)
        for w in pwrites:
            if w.r:
                nw = {}
                _merge(nw, w.r)
                _merge(nw, w.w)
                w.war = nw
                w.w = {}
                w.r = {}
            _merge(deps, w.war)
        return deps

    def _wait(self, e, deps):
        seen = self.seen[e]
        eng = self.engs[e]
        for k, (sem, v) in deps.items():
            if seen.get(k, 0) < v:
                eng.wait_ge(sem, v)
                seen[k] = v

    def _commit(self, tok, reads, writes, pwrites):
        for r in reads:
            _merge(r.r, tok)
        for w in writes:
            w.w = dict(tok)
            w.r = {}
            w.war = {}
        for w in pwrites:
            _merge(w.w, tok)

    def op(self, e, fn, reads=(), writes=(), pwrites=()):
        writes = list(writes) + [b for b in reads if isinstance(b, Buf) and b.psum]
        reads = [b.r if isinstance(b, Buf) else b for b in reads if not (isinstance(b, Buf) and b.psum)]
        writes = [b.r if isinstance(b, Buf) else b for b in writes]
        pwrites = [b.r if isinstance(b, Buf) else b for b in pwrites]
        deps = self._deps(reads, writes, pwrites)
        own = self.csem[e]
        if e == "pe":
            deps.pop(own.num, None)
        self._wait(e, deps)
        ins = fn(self.engs[e])
        self.ccnt[e] += 1
        ins.then_inc(own, 1)
        tok = {own.num: (own, self.ccnt[e])}
        self._commit(tok, reads, writes, pwrites)

    def dma(self, q, out, in_, reads=(), writes=(), pwrites=()):
        reads = [b.r if isinstance(b, Buf) else b for b in reads]
        writes = [b.r if isinstance(b, Buf) else b for b in writes]
        pwrites = [b.r if isinstance(b, Buf) else b for b in pwrites]
        deps = self._deps(reads, writes, pwrites)
        i = self.dnext[q]
        self.dnext[q] = (i + 1) % self.ND
        sem = self.dsems[q][i]
        if self.dcnt[q][i] > 0:
            _merge(deps, {sem.num: (sem, self.dcnt[q][i])})
        self._wait(q, deps)
        self.engs[q].dma_start(out=out, in_=in_).then_inc(sem, 16)
        self.dcnt[q][i] += 16
        tok = {sem.num: (sem, self.dcnt[q][i])}
        self._commit(tok, reads, writes, pwrites)

    def _all_tokens(self):
        deps = {}
        for e in self.engs:
            if self.ccnt[e] > 0:
                deps[self.csem[e].num] = (self.csem[e], self.ccnt[e])
        for q in self.dsems:
            for i, sem in enumerate(self.dsems[q]):
                if self.dcnt[q][i] > 0:
                    deps[sem.num] = (sem, self.dcnt[q][i])
        return deps

    def barrier(self):
        allt = self._all_tokens()
        for e in self.engs:
            deps = dict(allt)
            if e in ("pe", "sp"):
                deps.pop(self.csem[e].num, None)
            self._wait(e, deps)

    def finish(self):
        deps = {}
        for e in self.engs:
            if self.ccnt[e] > 0:
                deps[self.csem[e].num] = (self.csem[e], self.ccnt[e])
        for q in self.dsems:
            for i, sem in enumerate(self.dsems[q]):
                if self.dcnt[q][i] > 0:
                    deps[sem.num] = (sem, self.dcnt[q][i])
        deps.pop(self.csem["sp"].num, None)
        self._wait("sp", deps)


def na_cases():
    cases = []
    per_a = []
    idx = {}

    def win(r):
        r0 = min(max(r - 4, 0), 56)
        return r0

    for a in range(32):
        lst = []
        rows = (2 * a, 2 * a + 1)
        lo = min(win(r) for r in rows)
        hi = max(win(r) + 7 for r in rows)
        for kt in range(lo // 2, hi // 2 + 1):
            key = ("i", kt - a) if 2 <= a <= 29 else (a, kt - a)
            if key not in idx:
                idx[key] = len(cases)
                cases.append((a, kt))
            lst.append((kt, idx[key], kt - a))
        per_a.append(lst)
    return cases, per_a


def na_tables():
    cases, per_a = na_cases()
    nc_ = len(cases)
    mask = np.zeros((nc_, 128, 128), np.float32)
    kr = np.arange(128) // 64
    kc = np.arange(128) % 64
    qr = np.arange(128) // 64
    qc = np.arange(128) % 64
    ws = np.clip(qc - 8, 0, 48)
    colok = (kc[:, None] >= ws[None, :]) & (kc[:, None] < ws[None, :] + 16)
    for ci, (a, kt) in enumerate(cases):
        krow = 2 * kt + kr
        qrow = 2 * a + qr
        r0 = np.clip(qrow - 4, 0, 56)
        rowok = (krow[:, None] >= r0[None, :]) & (krow[:, None] < r0[None, :] + 8)
        ok = rowok & colok
        mask[ci] = np.where(ok, 1.0, 0.0)
    dr = np.zeros((7, 128, 128), np.int64)
    dc = np.zeros((7, 128, 128), np.int64)
    for di, d in enumerate(range(-3, 4)):
        dr[di] = np.clip(2 * d + kr[:, None] - qr[None, :] + 7, 0, 14)
        dc[di] = np.clip(kc[:, None] - qc[None, :] + 15, 0, 30)
    return cases, per_a, mask, dr, dc


def dil_masks():
    m = np.zeros((3, 128, 128), np.float32)
    pk = np.arange(128)[:, None]
    pq = np.arange(128)[None, :]
    for i, dl in enumerate((-1, 0, 1)):
        m[i] = np.where(np.abs(128 * dl + pk - pq) <= 64, 1.0, 0.0)
    return m


def rope_tables():
    inv = 500000.0 ** (-np.arange(0, 16, 2, dtype=np.float32) / 16.0)
    ang = np.arange(T, dtype=np.float32)[:, None] * inv[None, :]
    cos = np.cos(ang).astype(np.float32).T
    sin = np.sin(ang).astype(np.float32).T
    C = np.ones((128, T), np.float32)
    S = np.zeros((128, T), np.float32)
    for hh in range(2):
        b = hh * 64
        C[b:b + 8] = cos
        C[b + 8:b + 16] = cos
        S[b:b + 8] = -sin
        S[b + 8:b + 16] = sin
    return C, S


class Kern:
    pass


def res_cols(ap2, d, i0, n):
    if d == 1:
        return ap2[:, i0:i0 + n]
    L = T // d
    v = ap2.rearrange("p (m r) -> p r m", r=d)
    r0, m0 = i0 // L, i0 % L
    if n <= L - m0:
        return v[:, r0, m0:m0 + n]
    assert m0 == 0 and n % L == 0
    return v[:, r0:r0 + n // L, :]


def build(nlayers=DEPTH, stop=None, dbg=False):
    nc = bass.Bass("TRN2", target_bir_lowering=False)

    def din(name, shape, dt=F32):
        return nc.dram_tensor(name, list(shape), dt, kind="ExternalInput").ap()

    def dscr(name, shape, dt):
        return nc.dram_tensor(name, list(shape), dt, kind="Internal").ap()

    cases, per_a, _, _, _ = na_tables()
    NCASE = len(cases)
    EB_GROUPS = []
    seen_c = set()
    for a in range(32):
        lst = per_a[a]
        if lst[0][1] in seen_c:
            continue
        seen_c.add(lst[0][1])
        assert [c for (_, c, _) in lst] == list(range(lst[0][1], lst[0][1] + len(lst)))
        assert [d for (_, _, d) in lst] == list(range(lst[0][2], lst[0][2] + len(lst)))
        EB_GROUPS.append((lst[0][1], len(lst), lst[0][2]))

    x_in = din("x", [T, D])
    p_in = din("p", [DEPTH * T, 256])
    norm_w = din("norm_w", [DEPTH, D])
    ple_norm_w = din("ple_norm_w", [DEPTH, D])
    final_norm_w = din("final_norm_w", [1, D])
    w_in = din("w_in", [DEPTH * D, IN_W])
    w_oa = din("w_oa", [DEPTH * 1024, D])
    w_ob = din("w_ob", [DEPTH * 512, D])
    w_oc = din("w_oc", [DEPTH * 1536, D])
    w_out = din("w_out", [DEPTH * D, D])
    w_ple = din("w_ple", [DEPTH * 256, D])
    w_pg = din("w_ple_gate", [DEPTH * D, D])
    rpbg = din("rpbg", [DEPTH * 16 * 128, 7 * 128])
    na_mask = din("na_mask", [128, NCASE * 128])
    dil_mask = din("dil_mask", [128, 3 * 128])
    rope_c = din("rope_c", [128, T])
    rope_s = din("rope_s", [128, T])
    rope_pm = din("rope_pm", [128, 128])
    convw = din("convw", [DEPTH * 128, 20 * 5])
    convb = din("convb", [DEPTH * 128, 20])
    a_log = din("a_log", [DEPTH, 48])
    dt_bias = din("dt_bias", [DEPTH, 48])
    d_skip = din("d_skip", [DEPTH, 24])
    ssm_nw = din("ssm_norm_w", [DEPTH, 1536])
    y_out = nc.dram_tensor("y", [T, D], F32, kind="ExternalOutput").ap()

    xs = dscr("xs", [T, D], F32)
    yagT = dscr("yagT", [1024, T], BF16)
    ybgT = dscr("ybgT", [512, T], BF16)
    ycT = dscr("ycT", [1536, T], BF16)
    mrgT = dscr("mrgT", [1024, T], BF16)
    nd = [dscr("nd%d" % g, [T, 8 * 65], F32) for g in range(3)]
    xtok = dscr("xtok", [T, 1536], BF16)
    btok = dscr("btok", [T, 512], BF16)
    bcT = dscr("bcT", [1024, T], BF16)
    szd = dscr("szd", [T, 1536], BF16)
    yfd = dscr("yfd", [T, 1536], F32)
    dbg_out = None
    if dbg:
        dbg_out = nc.dram_tensor("dbg", [1536, T], F32, kind="ExternalOutput").ap()

    with ExitStack() as st:
        K = Sched(nc, st)
        V = Kern()
        identf = K.sb([128, 128], F32, "identf")
        identb = K.sb([128, 128], BF16, "identb")
        epsb = K.sb([128, 1], F32, "epsb")
        oneb = K.sb([128, 1], F32, "oneb")
        onesf = K.sb([128, 128], F32, "onesf")
        triF = K.sb([128, 128], F32, "triF")
        triB = K.sb([128, 128], F32, "triB")
        ntriF = K.sb([128, 128], F32, "ntriF")
        ntriB = K.sb([128, 128], F32, "ntriB")
        mFB = K.sb([128, 2, 128], BF16, "mFB")
        mtmp = K.sb([128, 128], F32, "mtmp")
        K.op("pool", lambda e: e.memset(identf.t[:], 0.0), writes=[identf])
        K.op("pool", lambda e: e.affine_select(out=identf.t[:], in_=identf.t[:], pattern=[[-1, 128]], compare_op=ALU.not_equal,
                                               fill=1.0, base=0, channel_multiplier=1), writes=[identf])
        K.op("pool", lambda e: e.tensor_copy(out=identb.t[:], in_=identf.t[:]), reads=[identf], writes=[identb])
        K.op("pool", lambda e: e.memset(epsb.t[:], EPS), writes=[epsb])
        K.op("pool", lambda e: e.memset(oneb.t[:], 1.0), writes=[oneb])
        K.op("pool", lambda e: e.memset(onesf.t[:], 1.0), writes=[onesf])
        for (tb_, sgn) in ((triF, 1), (triB, -1)):
            K.op("pool", lambda e, tb_=tb_: e.memset(tb_.t[:], 1.0), writes=[tb_])
            K.op("pool", lambda e, tb_=tb_, sgn=sgn: e.affine_select(out=tb_.t[:], in_=tb_.t[:], pattern=[[sgn, 128]], compare_op=ALU.is_ge,
                                                                     fill=0.0, base=0, channel_multiplier=-sgn), writes=[tb_])
        K.op("pool", lambda e: e.tensor_scalar(out=ntriF.t[:], in0=triF.t[:], scalar1=-1.0, scalar2=None, op0=ALU.mult), reads=[triF], writes=[ntriF])
        K.op("pool", lambda e: e.tensor_scalar(out=ntriB.t[:], in0=triB.t[:], scalar1=-1.0, scalar2=None, op0=ALU.mult), reads=[triB], writes=[ntriB])
        for i_, tb_ in enumerate((triF, triB)):
            K.op("pool", lambda e, tb_=tb_: e.tensor_scalar(out=mtmp.t[:], in0=tb_.t[:], scalar1=-1.0, scalar2=-NEG, op0=ALU.add, op1=ALU.mult),
                 reads=[tb_], writes=[mtmp])
            K.op("pool", lambda e, i_=i_: e.tensor_copy(out=mFB.t[:, i_, :], in_=mtmp.t[:]), reads=[mtmp], writes=[mFB])
        namask = K.sb([128, NCASE, 128], BF16, "namask")
        dmask = K.sb([128, 3, 128], BF16, "dmask")
        K.dma("pool", namask.t[:], na_mask.rearrange("p (c n) -> p c n", c=NCASE), writes=[namask])
        K.dma("pool", dmask.t[:], dil_mask.rearrange("p (c n) -> p c n", c=3), writes=[dmask])

        hT = K.sb([128, KC, T], BF16, "hT")
        gB = K.sb([128, D], F32, "gB")
        psb = [K.ps([128, 512], F32, "psb%d" % i) for i in range(7)]
        pst = K.ps([128, 1024], BF16, "pst")
        xt = [K.sb([128, D], F32, "xt%d" % i) for i in range(2)]
        hf = [K.sb([128, D], F32, "hf%d" % i) for i in range(2)]
        sml = [K.sb([128, 4], F32, "sml%d" % i) for i in range(2)]
        wsl = [K.sb([128, KC, 128], BF16, "wsl%d" % i) for i in range(5)]
        wctr = [0]
        nd_res = [Res() for _ in range(3)]
        yagT_res, ybgT_res, ycT_res, mrgT_res, xs_res = Res(), Res(), Res(), Res(), Res()
        xtok_res, btok_res, bcT_res, szd_res, yfd_res = Res(), Res(), Res(), Res(), Res()

        def rms_scale(src_ap, s, src_reads, n=D, junk_ap=None, junk_res=None):
            K.op("act", lambda e: e.activation(out=junk_ap, in_=src_ap, func=AF.Square, accum_out=s.t[:, 0:1]),
                 reads=src_reads, writes=[junk_res, s])
            K.op("act", lambda e: e.activation(out=s.t[:, 1:2], in_=s.t[:, 0:1], func=AF.Sqrt, bias=epsb.t[:, 0:1], scale=1.0 / n),
                 reads=[s, epsb], writes=[s])
            K.op("dve", lambda e: e.reciprocal(out=s.t[:, 2:3], in_=s.t[:, 1:2]), reads=[s], writes=[s])

        def rmsnorm_tile(src, i, gBuf):
            s = sml[i]
            rms_scale(src.t[:], s, [src], D, hf[i].t[:], hf[i])
            K.op("dve", lambda e: e.scalar_tensor_tensor(out=hf[i].t[:], in0=src.t[:], scalar=s.t[:, 2:3], in1=gBuf.t[:],
                                                         op0=ALU.mult, op1=ALU.mult),
                 reads=[src, s, gBuf], writes=[hf[i]])

        def to_T(i, dst_fn, dres, pa=0):
            for half in range(2):
                pb = psb[pa + half]
                for k4 in range(4):
                    kc = half * 4 + k4
                    K.op("pe", lambda e, kc=kc, k4=k4, pb=pb: e.transpose(out=pb.t[:, k4 * 128:(k4 + 1) * 128],
                                                                          in_=hf[i].t[:, kc * 128:(kc + 1) * 128], identity=identf.t[:]),
                         reads=[hf[i], identf], writes=[pb])
                src = pb.t[:, :].rearrange("p (k n) -> p k n", k=4)
                dst = dst_fn(half)
                if half == 0:
                    K.op("act", lambda e, src=src, dst=dst: e.activation(out=dst, in_=src, func=AF.Copy), reads=[pb], pwrites=[dres])
                else:
                    K.op("dve", lambda e, src=src, dst=dst: e.tensor_copy(out=dst, in_=src), reads=[pb], pwrites=[dres])

        def phase_h(x_src, l):
            K.dma("sp", gB.t[:], norm_w[l:l + 1, :].broadcast_to([128, D]), writes=[gB])
            for t in range(NT):
                i = t % 2
                K.dma("sp", xt[i].t[:], x_src[t * 128:(t + 1) * 128, :], reads=[xs_res], writes=[xt[i]])
                rmsnorm_tile(xt[i], i, gB)
                to_T(i, lambda half, t=t: hT.t[:, half * 4:(half + 1) * 4, t * 128:(t + 1) * 128], hT)

        def load_w(src_rows_ap, ncols=128, nk=KC):
            b = wsl[wctr[0] % len(wsl)]
            wctr[0] += 1
            K.dma("pool", b.t[:, 0:nk, 0:ncols], src_rows_ap.rearrange("(k p) n -> p k n", p=128), writes=[b])
            return b

        def win_cols(l, c0, n=128):
            return w_in[l * D:(l + 1) * D, c0:c0 + n]

        def proj_F(wb, evac_fn, nblk=8):
            for blk in range(nblk):
                pb = psb[2 + (blk % 2)]
                for kc in range(KC):
                    rhs = hT.t[:, kc, blk * 512:(blk + 1) * 512]
                    K.op("pe", lambda e, kc=kc, rhs=rhs, pb=pb: e.matmul(pb.t[:, :], lhsT=wb.t[:, kc, :], rhs=rhs,
                                                                         start=(kc == 0), stop=(kc == KC - 1)),
                         reads=[wb, hT], writes=[pb])
                evac_fn(blk, pb)

        def proj_T(wb, evac_fn, tok_fn=None, ncols=128):
            for t4 in range(NT // 4):
                pb = psb[2 + (t4 % 2)]
                for j in range(4):
                    t = t4 * 4 + j
                    for kc in range(KC):
                        lhsT = tok_fn(hT.t[:, kc, :], t) if tok_fn else hT.t[:, kc, t * 128:(t + 1) * 128]
                        K.op("pe", lambda e, kc=kc, lhsT=lhsT, pb=pb, j=j: e.matmul(pb.t[:, j * 128:j * 128 + ncols], lhsT=lhsT,
                                                                                    rhs=wb.t[:, kc, 0:ncols],
                                                                                    start=(kc == 0), stop=(kc == KC - 1)),
                             reads=[wb, hT], writes=[pb])
                evac_fn(t4, pb)

        sset = [(psb[4], psb[5]), (psb[0], psb[1])]
        oslot = [(psb[6].t[:, 0:65], psb[6]), (psb[2].t[:, 0:65], psb[2])]
        sset1 = [psb[4], psb[5], psb[0], psb[1]]

        def attn_scores(u):
            ktiles = u["kt"]
            n = len(ktiles)
            if u.get("single"):
                k_ = V.ptc % 4
                pa, pb5 = sset1[k_], None
                u["os"] = (pa.t[:, 384:449], pa)
            else:
                k_ = V.ptc % 2
                pa, pb5 = sset[k_]
                u["os"] = oslot[k_]
            V.ptc += 1
            pt = V.PT[k_]
            u["pt"] = pt
            for j, kt in enumerate(ktiles):
                dres = pa if j < 4 else pb5
                dst = pa.t[:, j * 128:(j + 1) * 128] if j < 4 else pb5.t[:, 0:128]
                nadd = len(kt["add"])
                K.op("pe", lambda e, dst=dst, kt=kt, nadd=nadd: e.matmul(dst, lhsT=kt["k"], rhs=u["q"], start=True, stop=(nadd == 0)),
                     reads=[V.qT, V.kT], writes=[dres])
                for ai, (aap, ares) in enumerate(kt["add"]):
                    last = ai == nadd - 1
                    K.op("pe", lambda e, dst=dst, aap=aap, last=last: e.matmul(dst, lhsT=identb.t[:], rhs=aap, start=False, stop=last),
                         reads=[identb, ares], writes=[dres])
            n4 = min(n, 4)
            mul = u.get("mul")
            ex = (V.PTf[k_] if u.get("single") else V.PTf[k_ % 2]) if mul else pt
            K.op("act", lambda e: e.activation(out=ex.t[:, 0:n4 * 128], in_=pa.t[:, 0:n4 * 128], func=AF.Exp, scale=0.125),
                 reads=[pa], writes=[ex])
            if n > 4:
                K.op("act", lambda e: e.activation(out=ex.t[:, 512:640], in_=pb5.t[:, 0:128], func=AF.Exp, scale=0.125),
                     reads=[pb5], pwrites=[ex])
            if mul:
                K.op("dve", lambda e: e.tensor_tensor(out=pt.t[:, 0:n * 128], in0=ex.t[:, 0:n * 128], in1=mul[0], op=ALU.mult),
                     reads=[ex, mul[1]], writes=[pt])

        def attn_pv(u):
            ktiles = u["kt"]
            n = len(ktiles)
            pt = u["pt"]
            oap, ores = u["os"]
            for j, kt in enumerate(ktiles):
                K.op("pe", lambda e, j=j, kt=kt: e.matmul(oap, lhsT=pt.t[:, j * 128:(j + 1) * 128], rhs=kt["v"],
                                                          start=(j == 0), stop=(j == n - 1)),
                     reads=[pt, V.vaug], writes=[ores])
            u["post"](oap, ores)

        def run_units(units, pipelined=True, depth=0):
            if depth:
                V.ptc = 0
                for idx, u in enumerate(units):
                    u["single"] = True
                    attn_scores(u)
                    if idx >= depth:
                        attn_pv(units[idx - depth])
                for u in units[max(0, len(units) - depth):]:
                    attn_pv(u)
                V.ptc = 0
                return
            if not pipelined:
                for u in units:
                    attn_scores(u)
                    attn_pv(u)
                return
            prev = None
            for u in units:
                attn_scores(u)
                if prev is not None:
                    attn_pv(prev)
                prev = u
            if prev is not None:
                attn_pv(prev)

        def transpose_out(src, dstT, dram_dst, dres):
            for t4 in range(NT // 4):
                for j in range(4):
                    t = t4 * 4 + j
                    K.op("pe", lambda e, t=t, j=j: e.transpose(out=pst.t[:, j * 128:(j + 1) * 128], in_=src.t[:, t, :], identity=identb.t[:]),
                         reads=[src, identb], writes=[pst])
                if t4 % 2 == 0:
                    K.op("act", lambda e, t4=t4: e.activation(out=dstT.t[:, t4 * 512:(t4 + 1) * 512], in_=pst.t[:, 0:512], func=AF.Copy),
                         reads=[pst], pwrites=[dstT])
                else:
                    K.op("dve", lambda e, t4=t4: e.tensor_copy(out=dstT.t[:, t4 * 512:(t4 + 1) * 512], in_=pst.t[:, 0:512]),
                         reads=[pst], pwrites=[dstT])
            K.dma("sp", dram_dst, dstT.t[:], reads=[dstT], pwrites=[dres])

        def alloc_AB():
            V.qT = K.sb([128, T], BF16, "qT")
            V.kT = K.sb([128, T], BF16, "kT")
            V.vaug = K.sb([128, NT, 2, 65], BF16, "vaug")
            V.sg = K.sb([128, NT, 128], BF16, "sg")
            V.yg = K.sb([128, NT, 128], BF16, "yg")
            V.ygT = K.sb([128, T], BF16, "ygT")
            V.g8 = K.sb([128, 2, 7, 128], BF16, "g8")
            V.wvg = K.sb([128, KC, 256], BF16, "wvg")
            V.EB = K.sb([128, 2, NCASE, 128], BF16, "EB")
            V.PTf = [K.sb([128, 640], F32, "PTf%d" % i) for i in range(2)]
            V.PT = [K.sb([128, 640 if i < 2 else 384], BF16, "PT%d" % i) for i in range(4)]
            V.ptc = 0
            V.rd = [K.sb([128, 2], F32, "rd%d" % i) for i in range(2)]
            K.op("pool", lambda e: e.memset(V.vaug.t[:], 1.0), writes=[V.vaug])
            V.ropeC = K.sb([128, T], BF16, "ropeC")
            V.ropeS = K.sb([128, T], BF16, "ropeS")
            for c8 in range(8):
                cs = slice(c8 * 512, (c8 + 1) * 512)
                K.dma("pool", V.ropeC.t[:, cs], rope_c[:, cs], pwrites=[V.ropeC])
                K.dma("pool", V.ropeS.t[:, cs], rope_s[:, cs], pwrites=[V.ropeS])
            V.pm = K.sb([128, 128], BF16, "pm")
            K.dma("pool", V.pm.t[:], rope_pm, writes=[V.pm])
            V.qraw = [K.sb([128, 512], BF16, "qraw%d" % i) for i in range(2)]
            V.rt1 = K.sb([128, 512], F32, "rt1")
            V.rt2 = K.sb([128, 512], F32, "rt2")
            V.ndst = [K.sb([128, 2, 65], F32, "ndst%d" % i) for i in range(2)]
            V.nda = [K.sb([128, 3, 130], F32, "nda%d" % i) for i in range(2)]
            V.nds = [K.sb([128, 2, 65], F32, "nds%d" % i) for i in range(2)]

        def phase_A(l):
            qT, kT, vaug, sg, yg, g8 = V.qT, V.kT, V.vaug, V.sg, V.yg, V.g8
            for hp in range(8):
                wq = load_w(win_cols(l, OFF_QA + hp * 128))
                wk = load_w(win_cols(l, OFF_KA + hp * 128))
                K.dma("pool", V.wvg.t[:, :, 0:128], win_cols(l, OFF_VA + hp * 128).rearrange("(k p) n -> p k n", p=128), pwrites=[V.wvg])
                K.dma("pool", V.wvg.t[:, :, 128:256], win_cols(l, OFF_GA + hp * 128).rearrange("(k p) n -> p k n", p=128), pwrites=[V.wvg])
                for hl in range(2):
                    h = hp * 2 + hl
                    r0 = (l * 16 + h) * 128
                    K.dma("pool", g8.t[:, hl, :, :].rearrange("p b c -> p (b c)"), rpbg[r0:r0 + 128, :], pwrites=[g8])
                g8v = g8.t[:].rearrange("p a b c -> p (a b c)")
                K.op("act", lambda e: e.activation(out=g8v, in_=g8v, func=AF.Exp), reads=[g8], writes=[g8])
                for hl in range(2):
                    for (c0_, n_, d0_) in EB_GROUPS:
                        K.op("pool", lambda e, hl=hl, c0_=c0_, n_=n_, d0_=d0_: e.tensor_tensor(
                            out=V.EB.t[:, hl, c0_:c0_ + n_, :], in0=g8.t[:, hl, d0_ + 3:d0_ + 3 + n_, :], in1=namask.t[:, c0_:c0_ + n_, :], op=ALU.mult),
                             reads=[g8, namask], pwrites=[V.EB])
                proj_F(wq, lambda blk, pb: K.op("act", lambda e: e.activation(out=qT.t[:, blk * 512:(blk + 1) * 512], in_=pb.t[:, :], func=AF.Copy),
                                                reads=[pb], pwrites=[qT]))
                proj_F(wk, lambda blk, pb: K.op("dve", lambda e: e.tensor_copy(out=kT.t[:, blk * 512:(blk + 1) * 512], in_=pb.t[:, :]),
                                                reads=[pb], pwrites=[kT]))
                for t2 in range(NT // 2):
                    pb = psb[2 + (t2 % 2)]
                    for j in range(2):
                        t = t2 * 2 + j
                        for kc in range(KC):
                            K.op("pe", lambda e, kc=kc, t=t, j=j, pb=pb: e.matmul(pb.t[:, j * 256:(j + 1) * 256], lhsT=hT.t[:, kc, t * 128:(t + 1) * 128],
                                                                                 rhs=V.wvg.t[:, kc, :], start=(kc == 0), stop=(kc == KC - 1)),
                                 reads=[V.wvg, hT], writes=[pb])
                    pv_ = pb.t[:, :].rearrange("p (t x c) -> p t x c", t=2, x=2)
                    K.op("dve", lambda e, t2=t2, pv_=pv_: e.tensor_copy(out=vaug.t[:, t2 * 2:(t2 + 1) * 2, :, 0:64],
                                                                        in_=pv_[:, :, 0, :].rearrange("p t (h c) -> p t h c", h=2)),
                         reads=[pb], pwrites=[vaug])
                    K.op("act", lambda e, t2=t2, pv_=pv_: e.activation(out=sg.t[:, t2 * 2:(t2 + 1) * 2, :], in_=pv_[:, :, 1, :], func=AF.Silu),
                         reads=[pb], pwrites=[sg])
                units = []
                cnt = 0
                for hl in range(2):
                    prt = slice(hl * 64, hl * 64 + 64)
                    for a in range(NT):
                        cnt += 1
                        kts = []
                        for (kt, ci, d) in per_a[a]:
                            kts.append(dict(k=kT.t[prt, kt * 128:(kt + 1) * 128], add=[], v=vaug.t[:, kt, hl, :]))
                        ci0 = per_a[a][0][1]
                        nci = len(per_a[a])

                        def post(oap, ores, r=V.rd[cnt % 2], a=a, hl=hl):
                            K.op("dve", lambda e: e.reciprocal(out=r.t[:, 0:1], in_=oap[:, 64:65]), reads=[ores], writes=[r])
                            K.op("dve", lambda e: e.scalar_tensor_tensor(out=yg.t[:, a, hl * 64:(hl + 1) * 64], in0=oap[:, 0:64],
                                                                          scalar=r.t[:, 0:1], in1=sg.t[:, a, hl * 64:(hl + 1) * 64],
                                                                          op0=ALU.mult, op1=ALU.mult),
                                 reads=[ores, r, sg], pwrites=[yg])
                        units.append(dict(q=qT.t[prt, a * 128:(a + 1) * 128], kt=kts, post=post,
                                          mul=(V.EB.t[:, hl, ci0:ci0 + nci, :].rearrange("p a b -> p (a b)"), V.EB)))
                run_units(units, pipelined=PIPE_A)
                transpose_out(yg, V.ygT, yagT[hp * 128:(hp + 1) * 128, :], yagT_res)

        def proj_rope(wb, dst, d):
            rt1, rt2 = V.rt1, V.rt2
            L = T // d
            dstv = dst.t[:, :].rearrange("p (r m) -> p m r", r=d) if d > 1 else None
            for blk in range(8):
                pa, pb2 = psb[2 + blk % 2], psb[4 + blk % 2]
                qr = V.qraw[blk % 2]
                cs = slice(blk * 512, (blk + 1) * 512)
                for kc in range(KC):
                    K.op("pe", lambda e, kc=kc, pa=pa: e.matmul(pa.t[:, :], lhsT=wb.t[:, kc, :], rhs=hT.t[:, kc, cs], start=(kc == 0), stop=(kc == KC - 1)),
                         reads=[wb, hT], writes=[pa])
                K.op("dve", lambda e, pa=pa, qr=qr: e.tensor_copy(out=qr.t[:, :], in_=pa.t[:, :]), reads=[pa], writes=[qr])
                K.op("pe", lambda e, pb2=pb2, qr=qr: e.matmul(pb2.t[:, :], lhsT=V.pm.t[:], rhs=qr.t[:, :], start=True, stop=True),
                     reads=[V.pm, qr], writes=[pb2])
                K.op("dve", lambda e, pa=pa: e.tensor_tensor(out=rt1.t[:, :], in0=pa.t[:, :], in1=V.ropeC.t[:, cs], op=ALU.mult),
                     reads=[pa, V.ropeC], writes=[rt1])
                K.op("dve", lambda e, pb2=pb2: e.tensor_tensor(out=rt2.t[:, :], in0=pb2.t[:, :], in1=V.ropeS.t[:, cs], op=ALU.mult),
                     reads=[pb2, V.ropeS], writes=[rt2])
                if d == 1:
                    K.op("pool", lambda e: e.tensor_tensor(out=dst.t[:, cs], in0=rt1.t[:, :], in1=rt2.t[:, :], op=ALU.add),
                         reads=[rt1, rt2], pwrites=[dst])
                else:
                    m0 = blk * 512 // d
                    ov = dstv[:, m0:m0 + 512 // d, :]
                    K.op("pool", lambda e, ov=ov: e.tensor_tensor(out=ov, in0=rt1.t[:, :].rearrange("p (m r) -> p m r", r=d),
                                                                 in1=rt2.t[:, :].rearrange("p (m r) -> p m r", r=d), op=ALU.add),
                         reads=[rt1, rt2], pwrites=[dst])

        def phase_B(l):
            qT, kT, vaug, sg, yg = V.qT, V.kT, V.vaug, V.sg, V.yg
            DIL = (1, 4, 16)
            for sp_ in range(LIM_SP):
                for g, d in list(enumerate(DIL))[:LIM_G]:
                    L = T // d
                    tpr = L // 128
                    hc = (g * 8 + 2 * sp_) * 64
                    wq = load_w(win_cols(l, OFF_QB + hc))
                    wk = load_w(win_cols(l, OFF_KB + hc))
                    wv = load_w(win_cols(l, OFF_VB + hc))
                    proj_rope(wq, qT, d)
                    proj_rope(wk, kT, d)
                    proj_T(wv, lambda t4, pb: K.op("dve", lambda e: e.tensor_copy(out=vaug.t[:, t4 * 4:(t4 + 1) * 4, :, 0:64],
                                                                                   in_=pb.t[:, :].rearrange("p (t h c) -> p t h c", t=4, h=2)),
                                                   reads=[pb], pwrites=[vaug]),
                           tok_fn=lambda ap2, t, d=d: res_cols(ap2, d, t * 128, 128))
                    ndv = nd[g].rearrange("(m r) c -> r m c", r=d)
                    units = []
                    for Tq in range(NT):
                        tq = Tq % tpr
                        r0, m0 = (Tq * 128) // L, (Tq * 128) % L
                        stg = V.ndst[Tq % 2]
                        for hl in range(2):
                            prt = slice(hl * 64, hl * 64 + 64)
                            kts = []
                            dls = [dl for dl in (-1, 0, 1) if 0 <= tq + dl < tpr]
                            for dl in dls:
                                kt = Tq + dl
                                kts.append(dict(k=kT.t[prt, kt * 128:(kt + 1) * 128], add=[], v=vaug.t[:, kt, hl, :]))
                            mulB = (dmask.t[:, dls[0] + 1:dls[0] + 1 + len(dls), :].rearrange("p a b -> p (a b)"), dmask)

                            def post(oap, ores, hl=hl, stg=stg, r0=r0, m0=m0, g=g, ndv=ndv):
                                K.op("dve", lambda e: e.tensor_copy(out=stg.t[:, hl, :], in_=oap), reads=[ores], pwrites=[stg])
                                if hl == 1:
                                    K.dma("sp", ndv[r0, m0:m0 + 128, sp_ * 130:(sp_ + 1) * 130], stg.t[:, :, :].rearrange("p a b -> p (a b)"),
                                          reads=[stg], pwrites=[nd_res[g]])
                            units.append(dict(q=qT.t[prt, Tq * 128:(Tq + 1) * 128], kt=kts, post=post, mul=mulB))
                    run_units(units, pipelined=PIPE_B, depth=DEPTH_B)
                wg = load_w(win_cols(l, OFF_GB + sp_ * 128))
                proj_T(wg, lambda t4, pb: K.op("act", lambda e: e.activation(out=sg.t[:, t4 * 4:(t4 + 1) * 4, :],
                                                                             in_=pb.t[:, :].rearrange("p (t c) -> p t c", t=4), func=AF.Silu),
                                               reads=[pb], pwrites=[sg]))
                for t in range(NT):
                    na_, ns_ = V.nda[t % 2], V.nds[t % 2]
                    for g in range(3):
                        K.dma("sp", na_.t[:, g, :], nd[g][t * 128:(t + 1) * 128, sp_ * 130:(sp_ + 1) * 130], reads=[nd_res[g]], pwrites=[na_])
                    nsv = ns_.t[:, :, :].rearrange("p a b -> p (a b)")
                    K.op("pool", lambda e, na_=na_, nsv=nsv: e.tensor_tensor(out=nsv, in0=na_.t[:, 0, :], in1=na_.t[:, 1, :], op=ALU.add),
                         reads=[na_], writes=[ns_])
                    K.op("pool", lambda e, na_=na_, nsv=nsv: e.tensor_tensor(out=nsv, in0=nsv, in1=na_.t[:, 2, :], op=ALU.add),
                         reads=[na_, ns_], writes=[ns_])
                    r = V.rd[t % 2]
                    K.op("dve", lambda e, r=r, ns_=ns_: e.reciprocal(out=r.t[:, 0:2], in_=ns_.t[:, :, 64]), reads=[ns_], writes=[r])
                    for hl in range(2):
                        K.op("dve", lambda e, r=r, ns_=ns_, t=t, hl=hl: e.scalar_tensor_tensor(
                            out=yg.t[:, t, hl * 64:(hl + 1) * 64], in0=ns_.t[:, hl, 0:64], scalar=r.t[:, hl:hl + 1],
                            in1=sg.t[:, t, hl * 64:(hl + 1) * 64], op0=ALU.mult, op1=ALU.mult),
                             reads=[ns_, r, sg], pwrites=[yg])
                transpose_out(yg, V.ygT, ybgT[sp_ * 128:(sp_ + 1) * 128, :], ybgT_res)

        def tr_to_tok(srcT, dst, dram_view, dres):
            for t4 in range(NT // 4):
                for j in range(4):
                    t = t4 * 4 + j
                    K.op("pe", lambda e, t=t, j=j: e.transpose(out=pst.t[:, j * 128:(j + 1) * 128], in_=srcT.t[:, t * 128:(t + 1) * 128], identity=identb.t[:]),
                         reads=[srcT, identb], writes=[pst])
                src = pst.t[:, 0:512].rearrange("p (t c) -> p t c", t=4)
                if t4 % 2 == 0:
                    K.op("act", lambda e, t4=t4, src=src: e.activation(out=dst.t[:, t4 * 4:(t4 + 1) * 4, :], in_=src, func=AF.Copy),
                         reads=[pst], pwrites=[dst])
                else:
                    K.op("dve", lambda e, t4=t4, src=src: e.tensor_copy(out=dst.t[:, t4 * 4:(t4 + 1) * 4, :], in_=src),
                         reads=[pst], pwrites=[dst])
            for q4 in range(4):
                K.dma("sp", dram_view[:, q4 * 8:(q4 + 1) * 8, :], dst.t[:, q4 * 8:(q4 + 1) * 8, :], reads=[dst], pwrites=[dres])

        def phase_C(l):
            with ExitStack() as subC:
                K.st = subC
                dt_all = K.sb([128, NT, 48], F32, "dt_all")
                a_all = K.sb([128, NT, 48], F32, "a_all")
                dec_all = K.sb([128, NT, 48], F32, "dec_all")
                with ExitStack() as sub1:
                    K.st = sub1
                    cw = K.sb([128, 20, 5], F32, "cw")
                    cb_ = K.sb([128, 20], F32, "cb_")
                    K.dma("sp", cw.t[:], convw[l * 128:(l + 1) * 128, :].rearrange("p (c k) -> p c k", c=20), writes=[cw])
                    K.dma("sp", cb_.t[:], convb[l * 128:(l + 1) * 128, :], writes=[cb_])
                    xpad = K.sb([128, T + 4], BF16, "xpad")
                    K.op("pool", lambda e: e.memset(xpad.t[:, 0:2], 0.0), pwrites=[xpad])
                    K.op("pool", lambda e: e.memset(xpad.t[:, T + 2:T + 4], 0.0), pwrites=[xpad])
                    dg = [K.sb([128, 5, 128], BF16, "dg%d" % i) for i in range(2)]
                    xcT = K.sb([128, T], BF16, "xcT")
                    xtk = K.sb([128, NT, 128], BF16, "xtk")
                    for cbk in range(20):
                        w = load_w(win_cols(l, OFF_XBC + cbk * 128))
                        dgc = dg[cbk % 2]
                        for k in range(5):
                            K.op("pool", lambda e, k=k, dgc=dgc: e.tensor_scalar(out=dgc.t[:, k, :], in0=identf.t[:], scalar1=cw.t[:, cbk, k:k + 1], scalar2=None, op0=ALU.mult),
                                 reads=[identf, cw], pwrites=[dgc])
                        proj_F(w, lambda blk, pb: K.op("act", lambda e: e.activation(out=xpad.t[:, 2 + blk * 512:2 + (blk + 1) * 512], in_=pb.t[:, :], func=AF.Copy),
                                                       reads=[pb], pwrites=[xpad]))
                        for blk in range(8):
                            pc = psb[4 + blk % 2]
                            for k in range(5):
                                K.op("pe", lambda e, k=k, blk=blk, pc=pc: e.matmul(pc.t[:, :], lhsT=dgc.t[:, k, :], rhs=xpad.t[:, blk * 512 + k:blk * 512 + k + 512],
                                                                                  start=(k == 0), stop=(k == 4)),
                                     reads=[dgc, xpad], writes=[pc])
                            K.op("act", lambda e, blk=blk, pc=pc: e.activation(out=xcT.t[:, blk * 512:(blk + 1) * 512], in_=pc.t[:, :], func=AF.Silu,
                                                                               bias=cb_.t[:, cbk:cbk + 1], scale=1.0),
                                 reads=[pc, cb_], pwrites=[xcT])
                        if cbk < 12:
                            tr_to_tok(xcT, xtk, xtok[:, cbk * 128:(cbk + 1) * 128].rearrange("(t p) c -> p t c", p=128), xtok_res)
                        else:
                            K.dma("sp", bcT[(cbk - 12) * 128:(cbk - 11) * 128, :], xcT.t[:], reads=[xcT], pwrites=[bcT_res])
                            if cbk < 16:
                                tr_to_tok(xcT, xtk, btok[:, (cbk - 12) * 128:(cbk - 11) * 128].rearrange("(t p) c -> p t c", p=128), btok_res)
                    alb = K.sb([128, 48], F32, "alb")
                    dtb = K.sb([128, 4, 48], F32, "dtb")
                    A4 = K.sb([128, 4, 48], F32, "A4")
                    dtt = K.sb([128, 4, 48], F32, "dtt")
                    K.dma("sp", alb.t[:], a_log[l:l + 1, :].broadcast_to([128, 48]), writes=[alb])
                    K.op("act", lambda e: e.activation(out=alb.t[:], in_=alb.t[:], func=AF.Exp), reads=[alb], writes=[alb])
                    for j in range(4):
                        K.op("pool", lambda e, j=j: e.tensor_scalar(out=A4.t[:, j, :], in0=alb.t[:], scalar1=-1.0, scalar2=None, op0=ALU.mult),
                             reads=[alb], pwrites=[A4])
                        K.dma("sp", dtb.t[:, j, :], dt_bias[l:l + 1, :].broadcast_to([128, 48]), pwrites=[dtb])
                    wdt = load_w(win_cols(l, OFF_DT, 48), ncols=48)

                    def dt_evac(t4, pb):
                        src = pb.t[:, :].rearrange("p (j c) -> p j c", j=4)[:, :, 0:48]
                        K.op("dve", lambda e: e.tensor_tensor(out=dtt.t[:], in0=src, in1=dtb.t[:], op=ALU.add), reads=[pb, dtb], writes=[dtt])
                        K.op("act", lambda e: e.activation(out=dtt.t[:], in_=dtt.t[:], func=AF.Exp), reads=[dtt], writes=[dtt])
                        K.op("act", lambda e: e.activation(out=dt_all.t[:, t4 * 4:(t4 + 1) * 4, :], in_=dtt.t[:], func=AF.Ln, bias=oneb.t[:, 0:1], scale=1.0),
                             reads=[dtt, oneb], pwrites=[dt_all])
                        K.op("pool", lambda e: e.tensor_tensor(out=a_all.t[:, t4 * 4:(t4 + 1) * 4, :], in0=dt_all.t[:, t4 * 4:(t4 + 1) * 4, :], in1=A4.t[:], op=ALU.mult),
                             reads=[dt_all, A4], pwrites=[a_all])
                    proj_T(wdt, dt_evac, ncols=48)
                    av = a_all.t[:].rearrange("p c h -> p (c h)")
                    dv = dec_all.t[:].rearrange("p c h -> p (c h)")
                    for j in range(3):
                        pb = psb[2 + (j % 2)]
                        K.op("pe", lambda e, j=j, pb=pb: e.matmul(pb.t[:, :], lhsT=onesf.t[:], rhs=av[:, j * 512:(j + 1) * 512], start=True, stop=True),
                             reads=[onesf, a_all], writes=[pb])
                        K.op("act", lambda e, j=j, pb=pb: e.activation(out=dv[:, j * 512:(j + 1) * 512], in_=pb.t[:, :], func=AF.Exp),
                             reads=[pb], pwrites=[dec_all])
                    wz = [K.sb([128, KC, 512], BF16, "wz%d" % i) for i in range(2)]
                    zst = [K.sb([128, 512], BF16, "zst%d" % i) for i in range(2)]
                    for zb in range(3):
                        wzb = wz[zb % 2]
                        K.dma("pool", wzb.t[:], win_cols(l, OFF_Z + zb * 512, 512).rearrange("(k p) n -> p k n", p=128), writes=[wzb])
                        for t in range(NT):
                            pb = psb[2 + (t % 2)]
                            zs = zst[t % 2]
                            for kc in range(KC):
                                K.op("pe", lambda e, kc=kc, t=t, pb=pb: e.matmul(pb.t[:, :], lhsT=hT.t[:, kc, t * 128:(t + 1) * 128], rhs=wzb.t[:, kc, :],
                                                                                start=(kc == 0), stop=(kc == KC - 1)),
                                     reads=[wzb, hT], writes=[pb])
                            K.op("act", lambda e, pb=pb, zs=zs: e.activation(out=zs.t[:, :], in_=pb.t[:, :], func=AF.Silu), reads=[pb], writes=[zs])
                            K.dma("sp", szd[t * 128:(t + 1) * 128, zb * 512:(zb + 1) * 512], zs.t[:, :], reads=[zs], pwrites=[szd_res])
                    K.barrier()
                with ExitStack() as sub2:
                    K.st = sub2
                    xtk2 = [K.sb([128, 1536], BF16, "xtk2_%d" % i) for i in range(2)]
                    btk2 = [K.sb([128, 512], BF16, "btk2_%d" % i) for i in range(2)]
                    bct = [K.sb([128, 8, 128], BF16, "bct%d" % i) for i in range(2)]
                    xdt = K.sb([128, 24, 64], BF16, "xdt")
                    Est = K.sb([128, 24, 64], F32, "Est")
                    Ebf = K.sb([128, 24, 64], BF16, "Ebf")
                    gts = K.sb([128, 128], F32, "gts")
                    E3 = [K.sb([128, 3, 128], F32, "E3_%d" % i) for i in range(2)]
                    MT = [K.sb([128, 3, 128], BF16, "MT%d" % i) for i in range(2)]
                    xw = [K.sb([128, 3, 64], BF16, "xw%d" % i) for i in range(2)]
                    ecum = K.sb([128, 24], F32, "ecum")
                    tmpo = K.sb([128, 6, 64], F32, "tmpo")
                    yacc = [K.sb([128, 1536], F32, "yacc%d" % i) for i in range(2)]
                    yft = K.sb([128, 1536], F32, "yft")
                    tmpx = K.sb([128, 1536], F32, "tmpx")
                    szt2 = K.sb([128, 1536], BF16, "szt2")
                    ycb = K.sb([128, 1536], BF16, "ycb")
                    ycTt = K.sb([128, 12, 128], BF16, "ycTt")
                    dskb = K.sb([128, 24], F32, "dskb")
                    snw = K.sb([128, 1536], F32, "snw")
                    K.dma("sp", dskb.t[:], d_skip[l:l + 1, :].broadcast_to([128, 24]), writes=[dskb])
                    K.dma("sp", snw.t[:], ssm_nw[l:l + 1, :].broadcast_to([128, 1536]), writes=[snw])
                    xdt2 = [xdt, K.sb([128, 24, 64], BF16, "xdtb")]
                    ecum2 = [ecum, K.sb([128, 24], F32, "ecumb")]
                    gts2 = [gts, K.sb([128, 128], F32, "gtsb")]
                    ncum2 = [K.sb([128, 24], F32, "ncum%d" % i) for i in range(2)]
                    pending = []
                    for dirn in (0, 1):
                        K.op("pool", lambda e: e.memset(Est.t[:], 0.0), writes=[Est])
                        K.op("pool", lambda e: e.memset(Ebf.t[:], 0.0), writes=[Ebf])
                        chunks = list(range(NT)) if dirn == 0 else list(range(NT - 1, -1, -1))
                        tri, ntri = (triF, ntriF) if dirn == 0 else (triB, ntriB)
                        mk = mFB.t[:, dirn, :]
                        wc = 127 if dirn == 0 else 0

                        def stage1(u, dirn=dirn, tri=tri, ntri=ntri, mk=mk, wc=wc):
                            c, g, half = u["c"], u["g"], u["half"]
                            i = c % 2
                            xd, ec = xdt2[i], ecum2[i]
                            if g == 0 and half == 0:
                                K.dma("sp", xtk2[i].t[:], xtok[c * 128:(c + 1) * 128, :], reads=[xtok_res], writes=[xtk2[i]])
                                K.dma("sp", btk2[i].t[:], btok[c * 128:(c + 1) * 128, :], reads=[btok_res], writes=[btk2[i]])
                                K.dma("sp", bct[i].t[:], bcT[:, c * 128:(c + 1) * 128].rearrange("(k p) n -> p k n", p=128), reads=[bcT_res], writes=[bct[i]])
                                xv = xtk2[i].t[:].rearrange("p (h c) -> p h c", h=24)
                                dtv = dt_all.t[:, c, dirn * 24:(dirn + 1) * 24].unsqueeze(2).broadcast_to([128, 24, 64])
                                K.op("dve", lambda e: e.tensor_tensor(out=xd.t[:], in0=xv, in1=dtv, op=ALU.mult),
                                     reads=[xtk2[i], dt_all], writes=[xd])
                                K.op("pe", lambda e: e.matmul(psb[0].t[:, 0:24], lhsT=tri.t[:], rhs=a_all.t[:, c, dirn * 24:(dirn + 1) * 24], start=True, stop=True),
                                     reads=[tri, a_all], writes=[psb[0]])
                                K.op("act", lambda e: e.activation(out=ec.t[:], in_=psb[0].t[:, 0:24], func=AF.Exp), reads=[psb[0]], writes=[ec])
                            gt = gts2[g % 2]
                            ncm = ncum2[i]
                            if half == 0:
                                K.op("pe", lambda e: e.matmul(psb[1].t[:, 0:128], lhsT=bct[i].t[:, g, :], rhs=bct[i].t[:, 4 + g, :], start=True, stop=True),
                                     reads=[bct[i]], writes=[psb[1]])
                                K.op("act", lambda e: e.activation(out=gt.t[:], in_=psb[1].t[:, 0:128], func=AF.Copy), reads=[psb[1]], writes=[gt])
                            h0 = g * 6 + half * 3
                            pr = psb[2 + half]
                            for j in range(3):
                                col = dirn * 24 + h0 + j
                                abc = a_all.t[:, c, col:col + 1].broadcast_to([128, 128])
                                dst = pr.t[:, j * 128:(j + 1) * 128]
                                K.op("pe", lambda e, abc=abc, dst=dst: e.matmul(dst, lhsT=abc, rhs=tri.t[:], start=True, stop=False),
                                     reads=[a_all, tri], writes=[pr])
                                K.op("pe", lambda e, abc=abc, dst=dst: e.matmul(dst, lhsT=ntri.t[:], rhs=abc, start=False, stop=False),
                                     reads=[a_all, ntri], writes=[pr])
                                K.op("pe", lambda e, dst=dst: e.matmul(dst, lhsT=identb.t[:], rhs=mk, start=False, stop=True),
                                     reads=[identb, mFB], writes=[pr])
                            e3 = E3[half]
                            K.op("act", lambda e: e.activation(out=e3.t[:].rearrange("p a b -> p (a b)"), in_=pr.t[:, 0:384], func=AF.Exp),
                                 reads=[pr], writes=[e3])
                            mt = MT[half]
                            K.op("dve", lambda e: e.tensor_tensor(out=mt.t[:], in0=e3.t[:], in1=gt.t[:].unsqueeze(1).broadcast_to([128, 3, 128]), op=ALU.mult),
                                 reads=[e3, gt], writes=[mt])
                            xw_ = xw[half]
                            K.op("dve", lambda e: e.tensor_tensor(out=xw_.t[:], in0=xd.t[:, h0:h0 + 3, :],
                                                                  in1=e3.t[:, :, wc:wc + 1].broadcast_to([128, 3, 64]), op=ALU.mult),
                                 reads=[e3, xd], writes=[xw_])

                        def stage2(u, dirn=dirn):
                            c, g, half = u["c"], u["g"], u["half"]
                            i = c % 2
                            xd, ec = xdt2[i], ecum2[i]
                            ya = yacc[i]
                            h0 = g * 6 + half * 3
                            mt, xw_ = MT[half], xw[half]
                            for j in range(3):
                                h = h0 + j
                                hj = half * 3 + j
                                K.op("pe", lambda e, j=j, h=h, hj=hj: e.matmul(psb[4].t[:, hj * 64:(hj + 1) * 64], lhsT=mt.t[:, j, :], rhs=xd.t[:, h, :], start=True, stop=True),
                                     reads=[mt, xd], writes=[psb[4]])
                                K.op("pe", lambda e, h=h, hj=hj: e.matmul(psb[5].t[:, hj * 64:(hj + 1) * 64], lhsT=bct[i].t[:, 4 + g, :], rhs=Ebf.t[:, h, :], start=True, stop=True),
                                     reads=[bct[i], Ebf], writes=[psb[5]])
                                K.op("pe", lambda e, j=j, hj=hj: e.matmul(psb[6].t[:, hj * 64:(hj + 1) * 64], lhsT=btk2[i].t[:, g * 128:(g + 1) * 128], rhs=xw_.t[:, j, :], start=True, stop=True),
                                     reads=[btk2[i], xw_], writes=[psb[6]])
                            if half == 1:
                                g6 = slice(g * 6, (g + 1) * 6)
                                ecv = ec.t[:, g6].unsqueeze(2).broadcast_to([128, 6, 64])
                                K.op("dve", lambda e: e.tensor_tensor(out=tmpo.t[:], in0=psb[5].t[:, 0:384].rearrange("p (a b) -> p a b", a=6), in1=ecv, op=ALU.mult),
                                     reads=[psb[5], ec], writes=[tmpo])
                                K.op("dve", lambda e: e.tensor_tensor(out=ya.t[:, g * 384:(g + 1) * 384], in0=tmpo.t[:].rearrange("p a b -> p (a b)"), in1=psb[4].t[:, 0:384], op=ALU.add),
                                     reads=[tmpo, psb[4]], pwrites=[ya])
                                dcv = dec_all.t[:, c, dirn * 24 + g * 6:dirn * 24 + (g + 1) * 6].unsqueeze(2).broadcast_to([128, 6, 64])
                                K.op("dve", lambda e: e.tensor_tensor(out=Est.t[:, g6, :], in0=Est.t[:, g6, :], in1=dcv, op=ALU.mult),
                                     reads=[Est, dec_all], writes=[Est])
                                K.op("dve", lambda e: e.tensor_tensor(out=Est.t[:, g6, :], in0=Est.t[:, g6, :], in1=psb[6].t[:, 0:384].rearrange("p (a b) -> p a b", a=6), op=ALU.add),
                                     reads=[Est, psb[6]], writes=[Est])
                                K.op("act", lambda e: e.activation(out=Ebf.t[:, g6, :], in_=Est.t[:, g6, :], func=AF.Copy), reads=[Est], writes=[Ebf])
                            if g == 3 and half == 1:
                                if dirn == 0:
                                    K.dma("sp", yfd[c * 128:(c + 1) * 128, :], ya.t[:], reads=[ya], pwrites=[yfd_res])
                                else:
                                    xv = xtk2[i].t[:].rearrange("p (h c) -> p h c", h=24)
                                    K.dma("sp", yft.t[:], yfd[c * 128:(c + 1) * 128, :], reads=[yfd_res], writes=[yft])
                                    K.dma("sp", szt2.t[:], szd[c * 128:(c + 1) * 128, :], reads=[szd_res], writes=[szt2])
                                    K.op("pool", lambda e: e.tensor_tensor(out=ya.t[:], in0=ya.t[:], in1=yft.t[:], op=ALU.add), reads=[ya, yft], writes=[ya])
                                    K.op("dve", lambda e: e.tensor_tensor(out=tmpx.t[:].rearrange("p (h c) -> p h c", h=24), in0=xv,
                                                                          in1=dskb.t[:].unsqueeze(2).broadcast_to([128, 24, 64]), op=ALU.mult),
                                         reads=[xtk2[i], dskb], writes=[tmpx])
                                    K.op("pool", lambda e: e.tensor_tensor(out=ya.t[:], in0=ya.t[:], in1=tmpx.t[:], op=ALU.add), reads=[ya, tmpx], writes=[ya])
                                    K.op("dve", lambda e: e.tensor_tensor(out=ya.t[:], in0=ya.t[:], in1=szt2.t[:], op=ALU.mult), reads=[ya, szt2], writes=[ya])
                                    s_ = sml[i]
                                    rms_scale(ya.t[:], s_, [ya], 1536, tmpx.t[:], tmpx)
                                    K.op("dve", lambda e: e.scalar_tensor_tensor(out=ycb.t[:], in0=ya.t[:], scalar=s_.t[:, 2:3], in1=snw.t[:], op0=ALU.mult, op1=ALU.mult),
                                         reads=[ya, s_, snw], writes=[ycb])
                                    def part2(c=c):
                                        for rnd, (k0, k1) in enumerate(((0, 8), (8, 12))):
                                            for k in range(k0, k1):
                                                K.op("pe", lambda e, k=k, k0=k0: e.transpose(out=pst.t[:, (k - k0) * 128:(k - k0 + 1) * 128], in_=ycb.t[:, k * 128:(k + 1) * 128], identity=identb.t[:]),
                                                     reads=[ycb, identb], writes=[pst])
                                            n_ = k1 - k0
                                            K.op("act", lambda e, k0=k0, k1=k1, n_=n_: e.activation(out=ycTt.t[:, k0:k1, :], in_=pst.t[:, 0:n_ * 128].rearrange("p (k n) -> p k n", k=n_), func=AF.Copy),
                                                 reads=[pst], pwrites=[ycTt])
                                        K.dma("sp", ycT[:, c * 128:(c + 1) * 128].rearrange("(k p) n -> p k n", p=128), ycTt.t[:], reads=[ycTt], pwrites=[ycT_res])
                                    pending.append([part2, 0])

                        units = [dict(c=c, g=g, half=half) for c in chunks for g in range(4) for half in range(2)]
                        prev = None
                        for u in units:
                            stage1(u)
                            if PIPE_C and prev is not None:
                                stage2(prev)
                            if not PIPE_C:
                                stage2(u)
                            prev = u
                            for pd in list(pending):
                                pd[1] += 1
                                if pd[1] >= 4:
                                    pd[0]()
                                    pending.remove(pd)
                        if PIPE_C:
                            stage2(prev)
                        for pd in list(pending):
                            pd[0]()
                            pending.remove(pd)
                    K.barrier()
            K.st = st
        def phase_DE(l, last):
            x_src = x_in if l == 0 else xs
            with ExitStack() as subD:
                K.st = subD
                wD = [K.sb([128, 48, 128], BF16, "wD%d" % i) for i in range(2)]
                yblk = [K.sb([128, 24, 512], BF16, "yblk%d" % i) for i in range(2)]
                sgm = [K.sb([128, 512], F32, "sgm%d" % i) for i in range(3)]
                mm_ = [K.sb([128, 512], F32, "mm%d" % i) for i in range(3)]
                mTb = [K.sb([128, 512], BF16, "mTb%d" % i) for i in range(2)]
                cnt = 0
                for db in range(8):
                    w = wD[db % 2]
                    c0 = db * 128
                    K.dma("pool", w.t[:, 0:8, :], w_oa[l * 1024:(l + 1) * 1024, c0:c0 + 128].rearrange("(k p) n -> p k n", p=128), pwrites=[w])
                    K.dma("pool", w.t[:, 8:12, :], w_ob[l * 512:(l + 1) * 512, c0:c0 + 128].rearrange("(k p) n -> p k n", p=128), pwrites=[w])
                    K.dma("pool", w.t[:, 12:24, :], w_oc[l * 1536:(l + 1) * 1536, c0:c0 + 128].rearrange("(k p) n -> p k n", p=128), pwrites=[w])
                    for ui, off in enumerate((OFF_UA, OFF_UB, OFF_UC)):
                        K.dma("pool", w.t[:, 24 + ui * 8:32 + ui * 8, :], win_cols(l, off + c0).rearrange("(k p) n -> p k n", p=128), pwrites=[w])
                    for tb in range(8):
                        yb = yblk[cnt % 2]
                        mo = mTb[cnt % 2]
                        cnt += 1
                        tc_ = slice(tb * 512, (tb + 1) * 512)
                        K.dma("sp", yb.t[:, 0:8, :], yagT[:, tc_].rearrange("(k p) n -> p k n", p=128), reads=[yagT_res], pwrites=[yb])
                        K.dma("sp", yb.t[:, 8:12, :], ybgT[:, tc_].rearrange("(k p) n -> p k n", p=128), reads=[ybgT_res], pwrites=[yb])
                        K.dma("sp", yb.t[:, 12:24, :], ycT[:, tc_].rearrange("(k p) n -> p k n", p=128), reads=[ycT_res], pwrites=[yb])
                        for ui in range(3):
                            for kc in range(KC):
                                K.op("pe", lambda e, ui=ui, kc=kc: e.matmul(psb[ui].t[:, :], lhsT=w.t[:, 24 + ui * 8 + kc, :], rhs=hT.t[:, kc, tc_], start=(kc == 0), stop=(kc == KC - 1)),
                                     reads=[w, hT], writes=[psb[ui]])
                        for yi, (a, b) in enumerate(((0, 8), (8, 12), (12, 24))):
                            for k in range(a, b):
                                K.op("pe", lambda e, yi=yi, k=k, a=a, b=b: e.matmul(psb[3 + yi].t[:, :], lhsT=w.t[:, k, :], rhs=yb.t[:, k, :], start=(k == a), stop=(k == b - 1)),
                                     reads=[w, yb], writes=[psb[3 + yi]])
                        for ui in range(3):
                            K.op("act", lambda e, ui=ui: e.activation(out=sgm[ui].t[:, :], in_=psb[ui].t[:, :], func=AF.Sigmoid), reads=[psb[ui]], writes=[sgm[ui]])
                            K.op("dve", lambda e, ui=ui: e.tensor_tensor(out=mm_[ui].t[:, :], in0=sgm[ui].t[:, :], in1=psb[3 + ui].t[:, :], op=ALU.mult),
                                 reads=[sgm[ui], psb[3 + ui]], writes=[mm_[ui]])
                        K.op("pool", lambda e: e.tensor_tensor(out=mm_[0].t[:, :], in0=mm_[0].t[:, :], in1=mm_[1].t[:, :], op=ALU.add), reads=[mm_[0], mm_[1]], writes=[mm_[0]])
                        K.op("pool", lambda e, mo=mo: e.tensor_tensor(out=mo.t[:, :], in0=mm_[0].t[:, :], in1=mm_[2].t[:, :], op=ALU.add), reads=[mm_[0], mm_[2]], writes=[mo])
                        K.dma("pool", mrgT[c0:c0 + 128, tc_], mo.t[:, :], reads=[mo], pwrites=[mrgT_res])
                K.barrier()
            K.st = st
            if stop == "D":
                return
            with ExitStack() as subE:
                K.st = subE
                wout = K.sb([128, 8, 1024], BF16, "wout")
                wpgs = K.sb([128, 8, 1024], BF16, "wpgs")
                wpl = K.sb([128, 2, 1024], BF16, "wpl")
                gP = K.sb([128, D], F32, "gP")
                gF = K.sb([128, D], F32, "gF")
                mtl = [K.sb([128, 8, 128], BF16, "mtl%d" % i) for i in range(2)]
                ptl = [K.sb([128, 256], F32, "ptl%d" % i) for i in range(2)]
                x1b = [K.sb([128, D], F32, "x1b%d" % i) for i in range(2)]
                hx = [K.sb([128, 8, 128], BF16, "hx%d" % i) for i in range(2)]
                gate = K.sb([128, D], F32, "gate")
                pe_ = K.sb([128, D], F32, "pe_")
                pT = [K.sb([128, 2, 128], BF16, "pT%d" % i) for i in range(2)]
                for k in range(8):
                    K.dma("pool", wout.t[:, k, :], w_out[l * D + k * 128:l * D + (k + 1) * 128, :], pwrites=[wout])
                    K.dma("pool", wpgs.t[:, k, :], w_pg[l * D + k * 128:l * D + (k + 1) * 128, :], pwrites=[wpgs])
                for k in range(2):
                    K.dma("pool", wpl.t[:, k, :], w_ple[l * 256 + k * 128:l * 256 + (k + 1) * 128, :], pwrites=[wpl])
                K.dma("sp", gP.t[:], ple_norm_w[l:l + 1, :].broadcast_to([128, D]), writes=[gP])
                K.dma("sp", gF.t[:], final_norm_w[0:1, :].broadcast_to([128, D]), writes=[gF])
                def e_stage1(t):
                    i = t % 2
                    mt = mtl[i]
                    x1 = x1b[i]
                    rows = slice(t * 128, (t + 1) * 128)
                    K.dma("sp", mt.t[:], mrgT[:, rows].rearrange("(k p) n -> p k n", p=128), reads=[mrgT_res], writes=[mt])
                    K.dma("sp", xt[i].t[:], x_src[rows, :], reads=[xs_res], writes=[xt[i]])
                    K.dma("sp", ptl[i].t[:], p_in[l * T + t * 128:l * T + (t + 1) * 128, :], writes=[ptl[i]])
                    for half in range(2):
                        hs = slice(half * 512, (half + 1) * 512)
                        for kc in range(KC):
                            K.op("pe", lambda e, half=half, hs=hs, kc=kc: e.matmul(psb[half].t[:, :], lhsT=mt.t[:, kc, :], rhs=wout.t[:, kc, hs], start=(kc == 0), stop=(kc == KC - 1)),
                                 reads=[mt, wout], writes=[psb[half]])
                        K.op("dve", lambda e, half=half, hs=hs: e.tensor_tensor(out=x1.t[:, hs], in0=xt[i].t[:, hs], in1=psb[half].t[:, :], op=ALU.add),
                             reads=[xt[i], psb[half]], pwrites=[x1])
                    rmsnorm_tile(x1, i, gP)
                    to_T(i, lambda half: hx[i].t[:, half * 4:(half + 1) * 4, :], hx[i], pa=2)
                    for k in range(2):
                        K.op("pe", lambda e, k=k: e.transpose(out=psb[6].t[:, k * 128:(k + 1) * 128], in_=ptl[i].t[:, k * 128:(k + 1) * 128], identity=identf.t[:]),
                             reads=[ptl[i], identf], writes=[psb[6]])
                    K.op("act", lambda e: e.activation(out=pT[i].t[:].rearrange("p k n -> p (k n)"), in_=psb[6].t[:, 0:256], func=AF.Copy), reads=[psb[6]], writes=[pT[i]])

                def e_stage2(t):
                    i = t % 2
                    x1 = x1b[i]
                    rows = slice(t * 128, (t + 1) * 128)
                    for half in range(2):
                        hs = slice(half * 512, (half + 1) * 512)
                        for kc in range(KC):
                            K.op("pe", lambda e, half=half, hs=hs, kc=kc: e.matmul(psb[4 + half].t[:, :], lhsT=hx[i].t[:, kc, :], rhs=wpgs.t[:, kc, hs], start=(kc == 0), stop=(kc == KC - 1)),
                                 reads=[hx[i], wpgs], writes=[psb[4 + half]])
                        K.op("act", lambda e, half=half, hs=hs: e.activation(out=gate.t[:, hs], in_=psb[4 + half].t[:, :], func=AF.Sigmoid), reads=[psb[4 + half]], pwrites=[gate])
                    for half in range(2):
                        hs = slice(half * 512, (half + 1) * 512)
                        for k in range(2):
                            K.op("pe", lambda e, half=half, hs=hs, k=k: e.matmul(psb[4 + half].t[:, :], lhsT=pT[i].t[:, k, :], rhs=wpl.t[:, k, hs], start=(k == 0), stop=(k == 1)),
                                 reads=[pT[i], wpl], writes=[psb[4 + half]])
                        K.op("dve", lambda e, half=half, hs=hs: e.tensor_tensor(out=pe_.t[:, hs], in0=gate.t[:, hs], in1=psb[4 + half].t[:, :], op=ALU.mult),
                             reads=[gate, psb[4 + half]], pwrites=[pe_])
                    K.op("pool", lambda e: e.tensor_tensor(out=x1.t[:], in0=x1.t[:], in1=pe_.t[:], op=ALU.add), reads=[x1, pe_], writes=[x1])
                    if not last:
                        K.dma("pool", xs[rows, :], x1.t[:], reads=[x1], pwrites=[xs_res])
                    else:
                        fo = fout[i]
                        s_ = sml2[i]
                        rms_scale(x1.t[:], s_, [x1], D, fo.t[:], fo)
                        K.op("dve", lambda e: e.scalar_tensor_tensor(out=fo.t[:], in0=x1.t[:], scalar=s_.t[:, 2:3], in1=gF.t[:], op0=ALU.mult, op1=ALU.mult),
                             reads=[x1, s_, gF], writes=[fo])
                        K.dma("sp", y_out[rows, :], fo.t[:], reads=[fo])

                fout = [K.sb([128, D], F32, "fout%d" % i) for i in range(2)] if last else None
                sml2 = [K.sb([128, 4], F32, "sml2_%d" % i) for i in range(2)]
                prev = None
                for t in range(NT):
                    e_stage1(t)
                    if prev is not None:
                        e_stage2(prev)
                    prev = t
                e_stage2(prev)
                K.barrier()
            K.st = st

        for l in range(nlayers):
            phase_h(x_in if l == 0 else xs, l)
            if stop == "h":
                break
            if stop in (None, "A", "B", "D", "E"):
                with ExitStack() as sub:
                    K.st = sub
                    alloc_AB()
                    if stop != "B":
                        phase_A(l)
                    if stop != "A":
                        phase_B(l)
                    K.barrier()
                K.st = st
                if stop in ("A", "B"):
                    break
            if stop in (None, "C", "D", "E"):
                phase_C(l)
                if stop == "C":
                    break
            if stop in (None, "D", "E"):
                phase_DE(l, last=(l == nlayers - 1))
                if stop in ("D", "E"):
                    break

        if dbg:
            dk = K.sb([128, 1024], BF16, "dbgk")
            if stop == "h":
                for kc in range(KC):
                    for c4 in range(4):
                        K.op("act", lambda e, kc=kc, c4=c4: e.activation(out=xt[0].t[:, :], in_=hT.t[:, kc, c4 * 1024:(c4 + 1) * 1024], func=AF.Copy),
                             reads=[hT], writes=[xt[0]])
                        K.dma("sp", dbg_out[kc * 128:(kc + 1) * 128, c4 * 1024:(c4 + 1) * 1024], xt[0].t[:, :], reads=[xt[0]])
            elif stop in ("A", "B", "C", "D"):
                srcd, nk_ = {"A": (yagT, 8), "B": (ybgT, 4), "C": (ycT, 12), "D": (mrgT, 8)}[stop]
                for kc in range(nk_):
                    for c4 in range(4):
                        K.dma("sp", dk.t[:, :], srcd[kc * 128:(kc + 1) * 128, c4 * 1024:(c4 + 1) * 1024],
                              reads=[yagT_res, ybgT_res, ycT_res, mrgT_res], writes=[dk])
                        K.op("act", lambda e: e.activation(out=xt[0].t[:, :], in_=dk.t[:, :], func=AF.Copy), reads=[dk], writes=[xt[0]])
                        K.dma("sp", dbg_out[kc * 128:(kc + 1) * 128, c4 * 1024:(c4 + 1) * 1024], xt[0].t[:, :], reads=[xt[0]])
        K.finish()
    return nc


def host_inputs(inputs, b):
    cases, per_a, mask, dr, dc = na_tables()
    f = np.float32
    m = {}
    m["x"] = np.ascontiguousarray(inputs["x"][b], dtype=f)
    m["p"] = np.ascontiguousarray(inputs["p"][:, b], dtype=f).reshape(DEPTH * T, 256)
    m["norm_w"] = np.asarray(inputs["norm_w"], f)
    m["ple_norm_w"] = np.asarray(inputs["ple_norm_w"], f)
    m["final_norm_w"] = np.asarray(inputs["final_norm_w"], f).reshape(1, D)
    m["w_in"] = np.asarray(inputs["w_in"], f).reshape(DEPTH * D, IN_W)
    m["w_oa"] = np.asarray(inputs["w_oa"], f).reshape(DEPTH * 1024, D)
    m["w_ob"] = np.asarray(inputs["w_ob"], f).reshape(DEPTH * 512, D)
    m["w_oc"] = np.asarray(inputs["w_oc"], f).reshape(DEPTH * 1536, D)
    m["w_out"] = np.asarray(inputs["w_out"], f).reshape(DEPTH * D, D)
    m["w_ple"] = np.asarray(inputs["w_ple"], f).reshape(DEPTH * 256, D)
    m["w_ple_gate"] = np.asarray(inputs["w_ple_gate"], f).reshape(DEPTH * D, D)
    rpb = np.asarray(inputs["na_rpb"], f)
    g = rpb[:, :, dr, dc]
    m["rpbg"] = np.ascontiguousarray(g.transpose(0, 1, 3, 2, 4)).reshape(DEPTH * 16 * 128, 7 * 128)
    m["na_mask"] = np.ascontiguousarray(mask.transpose(1, 0, 2)).reshape(128, -1)
    m["dil_mask"] = np.ascontiguousarray(dil_masks().transpose(1, 0, 2)).reshape(128, -1)
    C, S = rope_tables()
    m["rope_c"], m["rope_s"] = C, S
    pm = np.eye(128, dtype=np.float32)
    for hh in range(2):
        b0 = hh * 64
        for i in range(8):
            pm[b0 + i, b0 + i] = 0.0
            pm[b0 + 8 + i, b0 + 8 + i] = 0.0
            pm[b0 + i, b0 + 8 + i] = 1.0
            pm[b0 + 8 + i, b0 + i] = 1.0
    m["rope_pm"] = pm
    cw = np.asarray(inputs["conv_w"], f)
    m["convw"] = np.ascontiguousarray(cw.reshape(DEPTH, 5, 20, 128).transpose(0, 3, 2, 1)).reshape(DEPTH * 128, 100)
    cb = np.asarray(inputs["conv_b"], f)
    m["convb"] = np.ascontiguousarray(cb.reshape(DEPTH, 20, 128).transpose(0, 2, 1)).reshape(DEPTH * 128, 20)
    m["a_log"] = np.asarray(inputs["a_log"], f).reshape(DEPTH, 48)
    m["dt_bias"] = np.asarray(inputs["dt_bias"], f).reshape(DEPTH, 48)
    m["d_skip"] = np.asarray(inputs["d_skip"], f).reshape(DEPTH, 24)
    m["ssm_norm_w"] = np.asarray(inputs["ssm_norm_w"], f).reshape(DEPTH, 1536)
    return m


def kernel(**inputs):
    nc = build()
    in_maps = [host_inputs(inputs, b) for b in range(8)]
    res = run_bass_kernel_spmd(nc, in_maps, core_ids=list(range(8)))
    return np.stack([r["y"] for r in res.results], axis=0).astype(np.float32)
```

```python
import numpy as np
import concourse.bass as bass
import concourse.mybir as mybir
from concourse.bass_utils import run_bass_kernel_spmd
from concourse.alu_op_type import AluOpType as ALU
from contextlib import ExitStack

F32 = mybir.dt.float32
BF16 = mybir.dt.bfloat16
AF = mybir.ActivationFunctionType

T = 4096
NT = 32
D = 1024
KC = 8
DEPTH = 4
IN_W = 16432
EPS = 1e-6
NEG = -30000.0
PIPE_B = True
PIPE_A = True
PIPE_C = True
DEPTH_B = 0
LIM_SP = 4
LIM_G = 3

OFF_QA, OFF_KA, OFF_VA, OFF_GA = 0, 1024, 2048, 3072
OFF_QB, OFF_KB, OFF_VB, OFF_GB = 4096, 5632, 7168, 8704
OFF_XBC, OFF_Z, OFF_DT = 9216, 11776, 13312
OFF_UA, OFF_UB, OFF_UC = 13360, 14384, 15408


class Res:
    __slots__ = ("w", "r", "war")

    def __init__(self):
        self.w = {}
        self.r = {}
        self.war = {}


def _merge(dst, src):
    for k, sv in src.items():
        o = dst.get(k)
        if o is None or o[1] < sv[1]:
            dst[k] = sv


class Buf:
    def __init__(self, t, psum=False):
        self.t = t
        self.r = Res()
        self.psum = psum


class Sched:
    ND = 12

    def __init__(self, nc, st):
        self.nc = nc
        self.st = st
        self.engs = {"pe": nc.tensor, "act": nc.scalar, "dve": nc.vector, "pool": nc.gpsimd, "sp": nc.sync}
        self.csem = {e: st.enter_context(nc.semaphore("c_" + e)) for e in self.engs}
        self.ccnt = {e: 0 for e in self.engs}
        self.seen = {e: {} for e in self.engs}
        self.dsems = {q: [st.enter_context(nc.semaphore("d_%s_%d" % (q, i))) for i in range(self.ND)] for q in ("sp", "pool")}
        self.dcnt = {q: [0] * self.ND for q in ("sp", "pool")}
        self.dnext = {q: 0 for q in ("sp", "pool")}
        self.nbuf = 0

    def sb(self, shape, dt, name=None):
        self.nbuf += 1
        return Buf(self.st.enter_context(self.nc.sbuf_tensor("%s_%d" % (name or "b", self.nbuf), list(shape), dt)))

    def ps(self, shape, dt, name=None):
        self.nbuf += 1
        return Buf(self.st.enter_context(self.nc.psum_tensor("%s_%d" % (name or "p", self.nbuf), list(shape), dt)), psum=True)

    def _deps(self, reads, writes, pwrites):
        deps = {}
        for r in reads:
            _merge(deps, r.w)
        for w in writes:
            _merge(deps, w.w)
            _merge(deps, w.r)
            _merge(deps, w.war)
        for w in pwrites:
            if w.r:
                nw = {}
                _merge(nw, w.r)
                _merge(nw, w.w)
                w.war = nw
                w.w = {}
                w.r = {}
            _merge(deps, w.war)
        return deps

    def _wait(self, e, deps):
        seen = self.seen[e]
        eng = self.engs[e]
        for k, (sem, v) in deps.items():
            if seen.get(k, 0) < v:
                eng.wait_ge(sem, v)
                seen[k] = v

    def _commit(self, tok, reads, writes, pwrites):
        for r in reads:
            _merge(r.r, tok)
        for w in writes:
            w.w = dict(tok)
            w.r = {}
            w.war = {}
        for w in pwrites:
            _merge(w.w, tok)

    def op(self, e, fn, reads=(), writes=(), pwrites=()):
        writes = list(writes) + [b for b in reads if isinstance(b, Buf) and b.psum]
        reads = [b.r if isinstance(b, Buf) else b for b in reads if not (isinstance(b, Buf) and b.psum)]
        writes = [b.r if isinstance(b, Buf) else b for b in writes]
        pwrites = [b.r if isinstance(b, Buf) else b for b in pwrites]
        deps = self._deps(reads, writes, pwrites)
        own = self.csem[e]
        if e == "pe":
            deps.pop(own.num, None)
        self._wait(e, deps)
        ins = fn(self.engs[e])
        self.ccnt[e] += 1
        ins.then_inc(own, 1)
        tok = {own.num: (own, self.ccnt[e])}
        self._commit(tok, reads, writes, pwrites)

    def dma(self, q, out, in_, reads=(), writes=(), pwrites=()):
        reads = [b.r if isinstance(b, Buf) else b for b in reads]
        writes = [b.r if isinstance(b, Buf) else b for b in writes]
        pwrites = [b.r if isinstance(b, Buf) else b for b in pwrites]
        deps = self._deps(reads, writes, pwrites)
        i = self.dnext[q]
        self.dnext[q] = (i + 1) % self.ND
        sem = self.dsems[q][i]
        if self.dcnt[q][i] > 0:
            _merge(deps, {sem.num: (sem, self.dcnt[q][i])})
        self._wait(q, deps)
        self.engs[q].dma_start(out=out, in_=in_).then_inc(sem, 16)
        self.dcnt[q][i] += 16
        tok = {sem.num: (sem, self.dcnt[q][i])}
        self._commit(tok, reads, writes, pwrites)

    def _all_tokens(self):
        deps = {}
        for e in self.engs:
            if self.ccnt[e] > 0:
                deps[self.csem[e].num] = (self.csem[e], self.ccnt[e])
        for q in self.dsems:
            for i, sem in enumerate(self.dsems[q]):
                if self.dcnt[q][i] > 0:
                    deps[sem.num] = (sem, self.dcnt[q][i])
        return deps

    def barrier(self):
        allt = self._all_tokens()
        for e in self.engs:
            deps = dict(allt)
            if e in ("pe", "sp"):
                deps.pop(self.csem[e].num, None)
            self._wait(e, deps)

    def finish(self):
        deps = {}
        for e in self.engs:
            if self.ccnt[e] > 0:
                deps[self.csem[e].num] = (self.csem[e], self.ccnt[e])
        for q in self.dsems:
            for i, sem in enumerate(self.dsems[q]):
                if self.dcnt[q][i] > 0:
                    deps[sem.num] = (sem, self.dcnt[q][i])
        deps.pop(self.csem["sp"].num, None)
        self._wait("sp", deps)


def na_cases():
    cases = []
    per_a = []
    idx = {}

    def win(r):
        r0 = min(max(r - 4, 0), 56)
        return r0

    for a in range(32):
        lst = []
        rows = (2 * a, 2 * a + 1)
        lo = min(win(r) for r in rows)
        hi = max(win(r) + 7 for r in rows)
        for kt in range(lo // 2, hi // 2 + 1):
            key = ("i", kt - a) if 2 <= a <= 29 else (a, kt - a)
            if key not in idx:
                idx[key] = len(cases)
                cases.append((a, kt))
            lst.append((kt, idx[key], kt - a))
        per_a.append(lst)
    return cases, per_a


def na_tables():
    cases, per_a = na_cases()
    nc_ = len(cases)
    mask = np.zeros((nc_, 128, 128), np.float32)
    kr = np.arange(128) // 64
    kc = np.arange(128) % 64
    qr = np.arange(128) // 64
    qc = np.arange(128) % 64
    ws = np.clip(qc - 8, 0, 48)
    colok = (kc[:, None] >= ws[None, :]) & (kc[:, None] < ws[None, :] + 16)
    for ci, (a, kt) in enumerate(cases):
        krow = 2 * kt + kr
        qrow = 2 * a + qr
        r0 = np.clip(qrow - 4, 0, 56)
        rowok = (krow[:, None] >= r0[None, :]) & (krow[:, None] < r0[None, :] + 8)
        ok = rowok & colok
        mask[ci] = np.where(ok, 1.0, 0.0)
    dr = np.zeros((7, 128, 128), np.int64)
    dc = np.zeros((7, 128, 128), np.int64)
    for di, d in enumerate(range(-3, 4)):
        dr[di] = np.clip(2 * d + kr[:, None] - qr[None, :] + 7, 0, 14)
        dc[di] = np.clip(kc[:, None] - qc[None, :] + 15, 0, 30)
    return cases, per_a, mask, dr, dc


def dil_masks():
    m = np.zeros((3, 128, 128), np.float32)
    pk = np.arange(128)[:, None]
    pq = np.arange(128)[None, :]
    for i, dl in enumerate((-1, 0, 1)):
        m[i] = np.where(np.abs(128 * dl + pk - pq) <= 64, 1.0, 0.0)
    return m


def rope_tables():
    inv = 500000.0 ** (-np.arange(0, 16, 2, dtype=np.float32) / 16.0)
    ang = np.arange(T, dtype=np.float32)[:, None] * inv[None, :]
    cos = np.cos(ang).astype(np.float32).T
    sin = np.sin(ang).astype(np.float32).T
    C = np.ones((128, T), np.float32)
    S = np.zeros((128, T), np.float32)
    for hh in range(2):
        b = hh * 64
        C[b:b + 8] = cos
        C[b + 8:b + 16] = cos
        S[b:b + 8] = -sin
        S[b + 8:b + 16] = sin
    return C, S


class Kern:
    pass


def res_cols(ap2, d, i0, n):
    if d == 1:
        return ap2[:, i0:i0 + n]
    L = T // d
    v = ap2.rearrange("p (m r) -> p r m", r=d)
    r0, m0 = i0 // L, i0 % L
    if n <= L - m0:
        return v[:, r0, m0:m0 + n]
    assert m0 == 0 and n % L == 0
    return v[:, r0:r0 + n // L, :]


def build(nlayers=DEPTH, stop=None, dbg=False):
    nc = bass.Bass("TRN2", target_bir_lowering=False)

    def din(name, shape, dt=F32):
        return nc.dram_tensor(name, list(shape), dt, kind="ExternalInput").ap()

    def dscr(name, shape, dt):
        return nc.dram_tensor(name, list(shape), dt, kind="Internal").ap()

    cases, per_a, _, _, _ = na_tables()
    NCASE = len(cases)
    EB_GROUPS = []
    seen_c = set()
    for a in range(32):
        lst = per_a[a]
        if lst[0][1] in seen_c:
            continue
        seen_c.add(lst[0][1])
        assert [c for (_, c, _) in lst] == list(range(lst[0][1], lst[0][1] + len(lst)))
        assert [d for (_, _, d) in lst] == list(range(lst[0][2], lst[0][2] + len(lst)))
        EB_GROUPS.append((lst[0][1], len(lst), lst[0][2]))

    x_in = din("x", [T, D])
    p_in = din("p", [DEPTH * T, 256])
    norm_w = din("norm_w", [DEPTH, D])
    ple_norm_w = din("ple_norm_w", [DEPTH, D])
    final_norm_w = din("final_norm_w", [1, D])
    w_in = din("w_in", [DEPTH * D, IN_W])
    w_oa = din("w_oa", [DEPTH * 1024, D])
    w_ob = din("w_ob", [DEPTH * 512, D])
    w_oc = din("w_oc", [DEPTH * 1536, D])
    w_out = din("w_out", [DEPTH * D, D])
    w_ple = din("w_ple", [DEPTH * 256, D])
    w_pg = din("w_ple_gate", [DEPTH * D, D])
    rpbg = din("rpbg", [DEPTH * 16 * 128, 7 * 128])
    na_mask = din("na_mask", [128, NCASE * 128])
    dil_mask = din("dil_mask", [128, 3 * 128])
    rope_c = din("rope_c", [128, T])
    rope_s = din("rope_s", [128, T])
    rope_pm = din("rope_pm", [128, 128])
    convw = din("convw", [DEPTH * 128, 20 * 5])
    convb = din("convb", [DEPTH * 128, 20])
    a_log = din("a_log", [DEPTH, 48])
    dt_bias = din("dt_bias", [DEPTH, 48])
    d_skip = din("d_skip", [DEPTH, 24])
    ssm_nw = din("ssm_norm_w", [DEPTH, 1536])
    y_out = nc.dram_tensor("y", [T, D], F32, kind="ExternalOutput").ap()

    xs = dscr("xs", [T, D], F32)
    yagT = dscr("yagT", [1024, T], BF16)
    ybgT = dscr("ybgT", [512, T], BF16)
    ycT = dscr("ycT", [1536, T], BF16)
    mrgT = dscr("mrgT", [1024, T], BF16)
    nd = [dscr("nd%d" % g, [T, 8 * 65], F32) for g in range(3)]
    xtok = dscr("xtok", [T, 1536], BF16)
    btok = dscr("btok", [T, 512], BF16)
    bcT = dscr("bcT", [1024, T], BF16)
    szd = dscr("szd", [T, 1536], BF16)
    yfd = dscr("yfd", [T, 1536], F32)
    dbg_out = None
    if dbg:
        dbg_out = nc.dram_tensor("dbg", [1536, T], F32, kind="ExternalOutput").ap()

    with ExitStack() as st:
        K = Sched(nc, st)
        V = Kern()
        identf = K.sb([128, 128], F32, "identf")
        identb = K.sb([128, 128], BF16, "identb")
        epsb = K.sb([128, 1], F32, "epsb")
        oneb = K.sb([128, 1], F32, "oneb")
        onesf = K.sb([128, 128], F32, "onesf")
        triF = K.sb([128, 128], F32, "triF")
        triB = K.sb([128, 128], F32, "triB")
        ntriF = K.sb([128, 128], F32, "ntriF")
        ntriB = K.sb([128, 128], F32, "ntriB")
        mFB = K.sb([128, 2, 128], BF16, "mFB")
        mtmp = K.sb([128, 128], F32, "mtmp")
        K.op("pool", lambda e: e.memset(identf.t[:], 0.0), writes=[identf])
        K.op("pool", lambda e: e.affine_select(out=identf.t[:], in_=identf.t[:], pattern=[[-1, 128]], compare_op=ALU.not_equal,
                                               fill=1.0, base=0, channel_multiplier=1), writes=[identf])
        K.op("pool", lambda e: e.tensor_copy(out=identb.t[:], in_=identf.t[:]), reads=[identf], writes=[identb])
        K.op("pool", lambda e: e.memset(epsb.t[:], EPS), writes=[epsb])
        K.op("pool", lambda e: e.memset(oneb.t[:], 1.0), writes=[oneb])
        mhalf = K.sb([128, 1], F32, "mhalf")
        K.op("pool", lambda e: e.memset(mhalf.t[:], -0.5), writes=[mhalf])
        K.op("pool", lambda e: e.memset(onesf.t[:], 1.0), writes=[onesf])
        for (tb_, sgn) in ((triF, 1), (triB, -1)):
            K.op("pool", lambda e, tb_=tb_: e.memset(tb_.t[:], 1.0), writes=[tb_])
            K.op("pool", lambda e, tb_=tb_, sgn=sgn: e.affine_select(out=tb_.t[:], in_=tb_.t[:], pattern=[[sgn, 128]], compare_op=ALU.is_ge,
                                                                     fill=0.0, base=0, channel_multiplier=-sgn), writes=[tb_])
        K.op("pool", lambda e: e.tensor_scalar(out=ntriF.t[:], in0=triF.t[:], scalar1=-1.0, scalar2=None, op0=ALU.mult), reads=[triF], writes=[ntriF])
        K.op("pool", lambda e: e.tensor_scalar(out=ntriB.t[:], in0=triB.t[:], scalar1=-1.0, scalar2=None, op0=ALU.mult), reads=[triB], writes=[ntriB])
        for i_, tb_ in enumerate((triF, triB)):
            K.op("pool", lambda e, tb_=tb_: e.tensor_scalar(out=mtmp.t[:], in0=tb_.t[:], scalar1=-1.0, scalar2=-NEG, op0=ALU.add, op1=ALU.mult),
                 reads=[tb_], writes=[mtmp])
            K.op("pool", lambda e, i_=i_: e.tensor_copy(out=mFB.t[:, i_, :], in_=mtmp.t[:]), reads=[mtmp], writes=[mFB])
        namask = K.sb([128, NCASE, 128], BF16, "namask")
        dmask = K.sb([128, 3, 128], BF16, "dmask")
        K.dma("pool", namask.t[:], na_mask.rearrange("p (c n) -> p c n", c=NCASE), writes=[namask])
        K.dma("pool", dmask.t[:], dil_mask.rearrange("p (c n) -> p c n", c=3), writes=[dmask])

        hT = K.sb([128, KC, T], BF16, "hT")
        gB = K.sb([128, D], F32, "gB")
        psb = [K.ps([128, 512], F32, "psb%d" % i) for i in range(7)]
        pst = K.ps([128, 1024], BF16, "pst")
        xt = [K.sb([128, D], F32, "xt%d" % i) for i in range(2)]
        hf = [K.sb([128, D], F32, "hf%d" % i) for i in range(2)]
        sml = [K.sb([128, 4], F32, "sml%d" % i) for i in range(2)]
        wsl = [K.sb([128, KC, 128], BF16, "wsl%d" % i) for i in range(5)]
        wctr = [0]
        nd_res = [Res() for _ in range(3)]
        yagT_res, ybgT_res, ycT_res, mrgT_res, xs_res = Res(), Res(), Res(), Res(), Res()
        xtok_res, btok_res, bcT_res, szd_res, yfd_res = Res(), Res(), Res(), Res(), Res()

        def rms_scale(src_ap, s, src_reads, n=D, junk_ap=None, junk_res=None):
            K.op("act", lambda e: e.activation(out=junk_ap, in_=src_ap, func=AF.Square, accum_out=s.t[:, 0:1]),
                 reads=src_reads, writes=[junk_res, s])
            K.op("dve", lambda e: e.tensor_scalar(out=s.t[:, 1:2], in0=s.t[:, 0:1], scalar1=1.0 / n, scalar2=EPS, op0=ALU.mult, op1=ALU.add),
                 reads=[s], writes=[s])
            K.op("pool", lambda e: e.tensor_tensor(out=s.t[:, 2:3], in0=s.t[:, 1:2], in1=mhalf.t[:, 0:1], op=ALU.pow),
                 reads=[s, mhalf], writes=[s])

        def rmsnorm_tile(src, i, gBuf):
            s = sml[i]
            rms_scale(src.t[:], s, [src], D, hf[i].t[:], hf[i])
            K.op("dve", lambda e: e.scalar_tensor_tensor(out=hf[i].t[:], in0=src.t[:], scalar=s.t[:, 2:3], in1=gBuf.t[:],
                                                         op0=ALU.mult, op1=ALU.mult),
                 reads=[src, s, gBuf], writes=[hf[i]])

        def to_T(i, dst_fn, dres, pa=0):
            for half in range(2):
                pb = psb[pa + half]
                for k4 in range(4):
                    kc = half * 4 + k4
                    K.op("pe", lambda e, kc=kc, k4=k4, pb=pb: e.transpose(out=pb.t[:, k4 * 128:(k4 + 1) * 128],
                                                                          in_=hf[i].t[:, kc * 128:(kc + 1) * 128], identity=identf.t[:]),
                         reads=[hf[i], identf], writes=[pb])
                src = pb.t[:, :].rearrange("p (k n) -> p k n", k=4)
                dst = dst_fn(half)
                if half == 0:
                    K.op("act", lambda e, src=src, dst=dst: e.activation(out=dst, in_=src, func=AF.Copy), reads=[pb], pwrites=[dres])
                else:
                    K.op("dve", lambda e, src=src, dst=dst: e.tensor_copy(out=dst, in_=src), reads=[pb], pwrites=[dres])

        def phase_h(x_src, l):
            K.dma("sp", gB.t[:], norm_w[l:l + 1, :].broadcast_to([128, D]), writes=[gB])
            for t in range(NT):
                i = t % 2
                K.dma("sp", xt[i].t[:], x_src[t * 128:(t + 1) * 128, :], reads=[xs_res], writes=[xt[i]])
                rmsnorm_tile(xt[i], i, gB)
                to_T(i, lambda half, t=t: hT.t[:, half * 4:(half + 1) * 4, t * 128:(t + 1) * 128], hT)

        def load_w(src_rows_ap, ncols=128, nk=KC):
            b = wsl[wctr[0] % len(wsl)]
            wctr[0] += 1
            K.dma("pool", b.t[:, 0:nk, 0:ncols], src_rows_ap.rearrange("(k p) n -> p k n", p=128), writes=[b])
            return b

        def win_cols(l, c0, n=128):
            return w_in[l * D:(l + 1) * D, c0:c0 + n]

        def proj_F(wb, evac_fn, nblk=8):
            for blk in range(nblk):
                pb = psb[2 + (blk % 2)]
                for kc in range(KC):
                    rhs = hT.t[:, kc, blk * 512:(blk + 1) * 512]
                    K.op("pe", lambda e, kc=kc, rhs=rhs, pb=pb: e.matmul(pb.t[:, :], lhsT=wb.t[:, kc, :], rhs=rhs,
                                                                         start=(kc == 0), stop=(kc == KC - 1)),
                         reads=[wb, hT], writes=[pb])
                evac_fn(blk, pb)

        def proj_T(wb, evac_fn, tok_fn=None, ncols=128):
            for t4 in range(NT // 4):
                pb = psb[2 + (t4 % 2)]
                for j in range(4):
                    t = t4 * 4 + j
                    for kc in range(KC):
                        lhsT = tok_fn(hT.t[:, kc, :], t) if tok_fn else hT.t[:, kc, t * 128:(t + 1) * 128]
                        K.op("pe", lambda e, kc=kc, lhsT=lhsT, pb=pb, j=j: e.matmul(pb.t[:, j * 128:j * 128 + ncols], lhsT=lhsT,
                                                                                    rhs=wb.t[:, kc, 0:ncols],
                                                                                    start=(kc == 0), stop=(kc == KC - 1)),
                             reads=[wb, hT], writes=[pb])
                evac_fn(t4, pb)

        sset = [(psb[4], psb[5]), (psb[0], psb[1])]
        oslot = [(psb[6].t[:, 0:65], psb[6]), (psb[2].t[:, 0:65], psb[2])]
        sset1 = [psb[4], psb[5], psb[0], psb[1]]

        def attn_scores(u):
            ktiles = u["kt"]
            n = len(ktiles)
            if u.get("single"):
                k_ = V.ptc % 4
                pa, pb5 = sset1[k_], None
                u["os"] = (pa.t[:, 384:449], pa)
            else:
                k_ = V.ptc % 2
                pa, pb5 = sset[k_]
                u["os"] = oslot[k_]
            V.ptc += 1
            pt = V.PT[k_]
            u["pt"] = pt
            for j, kt in enumerate(ktiles):
                dres = pa if j < 4 else pb5
                dst = pa.t[:, j * 128:(j + 1) * 128] if j < 4 else pb5.t[:, 0:128]
                nadd = len(kt["add"])
                K.op("pe", lambda e, dst=dst, kt=kt, nadd=nadd: e.matmul(dst, lhsT=kt["k"], rhs=u["q"], start=True, stop=(nadd == 0)),
                     reads=[V.qT, V.kT], writes=[dres])
                for ai, (aap, ares) in enumerate(kt["add"]):
                    last = ai == nadd - 1
                    K.op("pe", lambda e, dst=dst, aap=aap, last=last: e.matmul(dst, lhsT=identb.t[:], rhs=aap, start=False, stop=last),
                         reads=[identb, ares], writes=[dres])
            n4 = min(n, 4)
            mul = u.get("mul")
            ex = (V.PTf[k_] if u.get("single") else V.PTf[k_ % 2]) if mul else pt
            K.op("act", lambda e: e.activation(out=ex.t[:, 0:n4 * 128], in_=pa.t[:, 0:n4 * 128], func=AF.Exp, scale=0.125),
                 reads=[pa], writes=[ex])
            if n > 4:
                K.op("act", lambda e: e.activation(out=ex.t[:, 512:640], in_=pb5.t[:, 0:128], func=AF.Exp, scale=0.125),
                     reads=[pb5], pwrites=[ex])
            if mul:
                K.op("dve", lambda e: e.tensor_tensor(out=pt.t[:, 0:n * 128], in0=ex.t[:, 0:n * 128], in1=mul[0], op=ALU.mult),
                     reads=[ex, mul[1]], writes=[pt])

        def attn_pv(u):
            ktiles = u["kt"]
            n = len(ktiles)
            pt = u["pt"]
            oap, ores = u["os"]
            for j, kt in enumerate(ktiles):
                K.op("pe", lambda e, j=j, kt=kt: e.matmul(oap, lhsT=pt.t[:, j * 128:(j + 1) * 128], rhs=kt["v"],
                                                          start=(j == 0), stop=(j == n - 1)),
                     reads=[pt, V.vaug], writes=[ores])
            u["post"](oap, ores)

        def run_units(units, pipelined=True, depth=0):
            if depth:
                V.ptc = 0
                for idx, u in enumerate(units):
                    u["single"] = True
                    attn_scores(u)
                    if idx >= depth:
                        attn_pv(units[idx - depth])
                for u in units[max(0, len(units) - depth):]:
                    attn_pv(u)
                V.ptc = 0
                return
            if not pipelined:
                for u in units:
                    attn_scores(u)
                    attn_pv(u)
                return
            prev = None
            for u in units:
                attn_scores(u)
                if prev is not None:
                    attn_pv(prev)
                prev = u
            if prev is not None:
                attn_pv(prev)

        def transpose_out(src, dstT, dram_dst, dres):
            for t4 in range(NT // 4):
                for j in range(4):
                    t = t4 * 4 + j
                    K.op("pe", lambda e, t=t, j=j: e.transpose(out=pst.t[:, j * 128:(j + 1) * 128], in_=src.t[:, t, :], identity=identb.t[:]),
                         reads=[src, identb], writes=[pst])
                if t4 % 2 == 0:
                    K.op("act", lambda e, t4=t4: e.activation(out=dstT.t[:, t4 * 512:(t4 + 1) * 512], in_=pst.t[:, 0:512], func=AF.Copy),
                         reads=[pst], pwrites=[dstT])
                else:
                    K.op("dve", lambda e, t4=t4: e.tensor_copy(out=dstT.t[:, t4 * 512:(t4 + 1) * 512], in_=pst.t[:, 0:512]),
                         reads=[pst], pwrites=[dstT])
            K.dma("sp", dram_dst, dstT.t[:], reads=[dstT], pwrites=[dres])

        def alloc_AB():
            V.qT = K.sb([128, T], BF16, "qT")
            V.kT = K.sb([128, T], BF16, "kT")
            V.vaug = K.sb([128, NT, 2, 65], BF16, "vaug")
            V.sg = K.sb([128, NT, 128], BF16, "sg")
            V.yg = K.sb([128, NT, 128], BF16, "yg")
            V.ygT = K.sb([128, T], BF16, "ygT")
            V.g8 = K.sb([128, 2, 7, 128], BF16, "g8")
            V.wvg = K.sb([128, KC, 256], BF16, "wvg")
            V.EB = K.sb([128, 2, NCASE, 128], BF16, "EB")
            V.PTf = [K.sb([128, 640], F32, "PTf%d" % i) for i in range(2)]
            V.PT = [K.sb([128, 640 if i < 2 else 384], BF16, "PT%d" % i) for i in range(4)]
            V.ptc = 0
            V.rd = [K.sb([128, 2], F32, "rd%d" % i) for i in range(2)]
            K.op("pool", lambda e: e.memset(V.vaug.t[:], 1.0), writes=[V.vaug])
            V.ropeC = K.sb([128, T], BF16, "ropeC")
            V.ropeS = K.sb([128, T], BF16, "ropeS")
            for c8 in range(8):
                cs = slice(c8 * 512, (c8 + 1) * 512)
                K.dma("pool", V.ropeC.t[:, cs], rope_c[:, cs], pwrites=[V.ropeC])
                K.dma("pool", V.ropeS.t[:, cs], rope_s[:, cs], pwrites=[V.ropeS])
            V.pm = K.sb([128, 128], BF16, "pm")
            K.dma("pool", V.pm.t[:], rope_pm, writes=[V.pm])
            V.qraw = [K.sb([128, 512], BF16, "qraw%d" % i) for i in range(2)]
            V.rt1 = K.sb([128, 512], F32, "rt1")
            V.rt2 = K.sb([128, 512], F32, "rt2")
            V.ndst = [K.sb([128, 2, 65], F32, "ndst%d" % i) for i in range(2)]
            V.nda = [K.sb([128, 3, 130], F32, "nda%d" % i) for i in range(2)]
            V.nds = [K.sb([128, 2, 65], F32, "nds%d" % i) for i in range(2)]

        def phase_A(l):
            qT, kT, vaug, sg, yg, g8 = V.qT, V.kT, V.vaug, V.sg, V.yg, V.g8
            for hp in range(8):
                wq = load_w(win_cols(l, OFF_QA + hp * 128))
                wk = load_w(win_cols(l, OFF_KA + hp * 128))
                K.dma("pool", V.wvg.t[:, :, 0:128], win_cols(l, OFF_VA + hp * 128).rearrange("(k p) n -> p k n", p=128), pwrites=[V.wvg])
                K.dma("pool", V.wvg.t[:, :, 128:256], win_cols(l, OFF_GA + hp * 128).rearrange("(k p) n -> p k n", p=128), pwrites=[V.wvg])
                for hl in range(2):
                    h = hp * 2 + hl
                    r0 = (l * 16 + h) * 128
                    K.dma("pool", g8.t[:, hl, :, :].rearrange("p b c -> p (b c)"), rpbg[r0:r0 + 128, :], pwrites=[g8])
                g8v = g8.t[:].rearrange("p a b c -> p (a b c)")
                K.op("act", lambda e: e.activation(out=g8v, in_=g8v, func=AF.Exp), reads=[g8], writes=[g8])
                for hl in range(2):
                    for (c0_, n_, d0_) in EB_GROUPS:
                        K.op("pool", lambda e, hl=hl, c0_=c0_, n_=n_, d0_=d0_: e.tensor_tensor(
                            out=V.EB.t[:, hl, c0_:c0_ + n_, :], in0=g8.t[:, hl, d0_ + 3:d0_ + 3 + n_, :], in1=namask.t[:, c0_:c0_ + n_, :], op=ALU.mult),
                             reads=[g8, namask], pwrites=[V.EB])
                proj_F(wq, lambda blk, pb: K.op("act", lambda e: e.activation(out=qT.t[:, blk * 512:(blk + 1) * 512], in_=pb.t[:, :], func=AF.Copy),
                                                reads=[pb], pwrites=[qT]))
                proj_F(wk, lambda blk, pb: K.op("dve", lambda e: e.tensor_copy(out=kT.t[:, blk * 512:(blk + 1) * 512], in_=pb.t[:, :]),
                                                reads=[pb], pwrites=[kT]))
                for t2 in range(NT // 2):
                    pb = psb[2 + (t2 % 2)]
                    for j in range(2):
                        t = t2 * 2 + j
                        for kc in range(KC):
                            K.op("pe", lambda e, kc=kc, t=t, j=j, pb=pb: e.matmul(pb.t[:, j * 256:(j + 1) * 256], lhsT=hT.t[:, kc, t * 128:(t + 1) * 128],
                                                                                 rhs=V.wvg.t[:, kc, :], start=(kc == 0), stop=(kc == KC - 1)),
                                 reads=[V.wvg, hT], writes=[pb])
                    pv_ = pb.t[:, :].rearrange("p (t x c) -> p t x c", t=2, x=2)
                    K.op("dve", lambda e, t2=t2, pv_=pv_: e.tensor_copy(out=vaug.t[:, t2 * 2:(t2 + 1) * 2, :, 0:64],
                                                                        in_=pv_[:, :, 0, :].rearrange("p t (h c) -> p t h c", h=2)),
                         reads=[pb], pwrites=[vaug])
                    K.op("act", lambda e, t2=t2, pv_=pv_: e.activation(out=sg.t[:, t2 * 2:(t2 + 1) * 2, :], in_=pv_[:, :, 1, :], func=AF.Silu),
                         reads=[pb], pwrites=[sg])
                units = []
                cnt = 0
                for hl in range(2):
                    prt = slice(hl * 64, hl * 64 + 64)
                    for a in range(NT):
                        cnt += 1
                        kts = []
                        for (kt, ci, d) in per_a[a]:
                            kts.append(dict(k=kT.t[prt, kt * 128:(kt + 1) * 128], add=[], v=vaug.t[:, kt, hl, :]))
                        ci0 = per_a[a][0][1]
                        nci = len(per_a[a])

                        def post(oap, ores, r=V.rd[cnt % 2], a=a, hl=hl):
                            K.op("dve", lambda e: e.reciprocal(out=r.t[:, 0:1], in_=oap[:, 64:65]), reads=[ores], writes=[r])
                            K.op("dve", lambda e: e.scalar_tensor_tensor(out=yg.t[:, a, hl * 64:(hl + 1) * 64], in0=oap[:, 0:64],
                                                                          scalar=r.t[:, 0:1], in1=sg.t[:, a, hl * 64:(hl + 1) * 64],
                                                                          op0=ALU.mult, op1=ALU.mult),
                                 reads=[ores, r, sg], pwrites=[yg])
                        units.append(dict(q=qT.t[prt, a * 128:(a + 1) * 128], kt=kts, post=post,
                                          mul=(V.EB.t[:, hl, ci0:ci0 + nci, :].rearrange("p a b -> p (a b)"), V.EB)))
                run_units(units, pipelined=PIPE_A)
                transpose_out(yg, V.ygT, yagT[hp * 128:(hp + 1) * 128, :], yagT_res)

        def proj_rope(wb, dst, d):
            rt1, rt2 = V.rt1, V.rt2
            L = T // d
            dstv = dst.t[:, :].rearrange("p (r m) -> p m r", r=d) if d > 1 else None
            for blk in range(8):
                pa, pb2 = psb[2 + blk % 2], psb[4 + blk % 2]
                qr = V.qraw[blk % 2]
                cs = slice(blk * 512, (blk + 1) * 512)
                for kc in range(KC):
                    K.op("pe", lambda e, kc=kc, pa=pa: e.matmul(pa.t[:, :], lhsT=wb.t[:, kc, :], rhs=hT.t[:, kc, cs], start=(kc == 0), stop=(kc == KC - 1)),
                         reads=[wb, hT], writes=[pa])
                K.op("dve", lambda e, pa=pa, qr=qr: e.tensor_copy(out=qr.t[:, :], in_=pa.t[:, :]), reads=[pa], writes=[qr])
                K.op("pe", lambda e, pb2=pb2, qr=qr: e.matmul(pb2.t[:, :], lhsT=V.pm.t[:], rhs=qr.t[:, :], start=True, stop=True),
                     reads=[V.pm, qr], writes=[pb2])
                K.op("dve", lambda e, pa=pa: e.tensor_tensor(out=rt1.t[:, :], in0=pa.t[:, :], in1=V.ropeC.t[:, cs], op=ALU.mult),
                     reads=[pa, V.ropeC], writes=[rt1])
                K.op("dve", lambda e, pb2=pb2: e.tensor_tensor(out=rt2.t[:, :], in0=pb2.t[:, :], in1=V.ropeS.t[:, cs], op=ALU.mult),
                     reads=[pb2, V.ropeS], writes=[rt2])
                if d == 1:
                    K.op("pool", lambda e: e.tensor_tensor(out=dst.t[:, cs], in0=rt1.t[:, :], in1=rt2.t[:, :], op=ALU.add),
                         reads=[rt1, rt2], pwrites=[dst])
                else:
                    m0 = blk * 512 // d
                    ov = dstv[:, m0:m0 + 512 // d, :]
                    K.op("pool", lambda e, ov=ov: e.tensor_tensor(out=ov, in0=rt1.t[:, :].rearrange("p (m r) -> p m r", r=d),
                                                                 in1=rt2.t[:, :].rearrange("p (m r) -> p m r", r=d), op=ALU.add),
                         reads=[rt1, rt2], pwrites=[dst])

        def phase_B(l):
            qT, kT, vaug, sg, yg = V.qT, V.kT, V.vaug, V.sg, V.yg
            DIL = (1, 4, 16)
            for sp_ in range(LIM_SP):
                for g, d in list(enumerate(DIL))[:LIM_G]:
                    L = T // d
                    tpr = L // 128
                    hc = (g * 8 + 2 * sp_) * 64
                    wq = load_w(win_cols(l, OFF_QB + hc))
                    wk = load_w(win_cols(l, OFF_KB + hc))
                    wv = load_w(win_cols(l, OFF_VB + hc))
                    proj_rope(wq, qT, d)
                    proj_rope(wk, kT, d)
                    proj_T(wv, lambda t4, pb: K.op("dve", lambda e: e.tensor_copy(out=vaug.t[:, t4 * 4:(t4 + 1) * 4, :, 0:64],
                                                                                   in_=pb.t[:, :].rearrange("p (t h c) -> p t h c", t=4, h=2)),
                                                   reads=[pb], pwrites=[vaug]),
                           tok_fn=lambda ap2, t, d=d: res_cols(ap2, d, t * 128, 128))
                    ndv = nd[g].rearrange("(m r) c -> r m c", r=d)
                    units = []
                    for Tq in range(NT):
                        tq = Tq % tpr
                        r0, m0 = (Tq * 128) // L, (Tq * 128) % L
                        stg = V.ndst[Tq % 2]
                        for hl in range(2):
                            prt = slice(hl * 64, hl * 64 + 64)
                            kts = []
                            dls = [dl for dl in (-1, 0, 1) if 0 <= tq + dl < tpr]
                            for dl in dls:
                                kt = Tq + dl
                                kts.append(dict(k=kT.t[prt, kt * 128:(kt + 1) * 128], add=[], v=vaug.t[:, kt, hl, :]))
                            mulB = (dmask.t[:, dls[0] + 1:dls[0] + 1 + len(dls), :].rearrange("p a b -> p (a b)"), dmask)

                            def post(oap, ores, hl=hl, stg=stg, r0=r0, m0=m0, g=g, ndv=ndv):
                                K.op("dve", lambda e: e.tensor_copy(out=stg.t[:, hl, :], in_=oap), reads=[ores], pwrites=[stg])
                                if hl == 1:
                                    K.dma("sp", ndv[r0, m0:m0 + 128, sp_ * 130:(sp_ + 1) * 130], stg.t[:, :, :].rearrange("p a b -> p (a b)"),
                                          reads=[stg], pwrites=[nd_res[g]])
                            units.append(dict(q=qT.t[prt, Tq * 128:(Tq + 1) * 128], kt=kts, post=post, mul=mulB))
                    run_units(units, pipelined=PIPE_B, depth=DEPTH_B)
                wg = load_w(win_cols(l, OFF_GB + sp_ * 128))
                proj_T(wg, lambda t4, pb: K.op("act", lambda e: e.activation(out=sg.t[:, t4 * 4:(t4 + 1) * 4, :],
                                                                             in_=pb.t[:, :].rearrange("p (t c) -> p t c", t=4), func=AF.Silu),
                                               reads=[pb], pwrites=[sg]))
                for t in range(NT):
                    na_, ns_ = V.nda[t % 2], V.nds[t % 2]
                    for g in range(3):
                        K.dma("sp", na_.t[:, g, :], nd[g][t * 128:(t + 1) * 128, sp_ * 130:(sp_ + 1) * 130], reads=[nd_res[g]], pwrites=[na_])
                    nsv = ns_.t[:, :, :].rearrange("p a b -> p (a b)")
                    K.op("pool", lambda e, na_=na_, nsv=nsv: e.tensor_tensor(out=nsv, in0=na_.t[:, 0, :], in1=na_.t[:, 1, :], op=ALU.add),
                         reads=[na_], writes=[ns_])
                    K.op("pool", lambda e, na_=na_, nsv=nsv: e.tensor_tensor(out=nsv, in0=nsv, in1=na_.t[:, 2, :], op=ALU.add),
                         reads=[na_, ns_], writes=[ns_])
                    r = V.rd[t % 2]
                    K.op("dve", lambda e, r=r, ns_=ns_: e.reciprocal(out=r.t[:, 0:2], in_=ns_.t[:, :, 64]), reads=[ns_], writes=[r])
                    for hl in range(2):
                        K.op("dve", lambda e, r=r, ns_=ns_, t=t, hl=hl: e.scalar_tensor_tensor(
                            out=yg.t[:, t, hl * 64:(hl + 1) * 64], in0=ns_.t[:, hl, 0:64], scalar=r.t[:, hl:hl + 1],
                            in1=sg.t[:, t, hl * 64:(hl + 1) * 64], op0=ALU.mult, op1=ALU.mult),
                             reads=[ns_, r, sg], pwrites=[yg])
                transpose_out(yg, V.ygT, ybgT[sp_ * 128:(sp_ + 1) * 128, :], ybgT_res)

        def tr_to_tok(srcT, dst, dram_view, dres):
            for t4 in range(NT // 4):
                for j in range(4):
                    t = t4 * 4 + j
                    K.op("pe", lambda e, t=t, j=j: e.transpose(out=pst.t[:, j * 128:(j + 1) * 128], in_=srcT.t[:, t * 128:(t + 1) * 128], identity=identb.t[:]),
                         reads=[srcT, identb], writes=[pst])
                src = pst.t[:, 0:512].rearrange("p (t c) -> p t c", t=4)
                if t4 % 2 == 0:
                    K.op("act", lambda e, t4=t4, src=src: e.activation(out=dst.t[:, t4 * 4:(t4 + 1) * 4, :], in_=src, func=AF.Copy),
                         reads=[pst], pwrites=[dst])
                else:
                    K.op("dve", lambda e, t4=t4, src=src: e.tensor_copy(out=dst.t[:, t4 * 4:(t4 + 1) * 4, :], in_=src),
                         reads=[pst], pwrites=[dst])
            for q4 in range(4):
                K.dma("sp", dram_view[:, q4 * 8:(q4 + 1) * 8, :], dst.t[:, q4 * 8:(q4 + 1) * 8, :], reads=[dst], pwrites=[dres])

        def phase_C(l):
            with ExitStack() as subC:
                K.st = subC
                dt_all = K.sb([128, NT, 48], F32, "dt_all")
                a_all = K.sb([128, NT, 48], F32, "a_all")
                dec_all = K.sb([128, NT, 48], F32, "dec_all")
                with ExitStack() as sub1:
                    K.st = sub1
                    cw = K.sb([128, 20, 5], F32, "cw")
                    cb_ = K.sb([128, 20], F32, "cb_")
                    K.dma("sp", cw.t[:], convw[l * 128:(l + 1) * 128, :].rearrange("p (c k) -> p c k", c=20), writes=[cw])
                    K.dma("sp", cb_.t[:], convb[l * 128:(l + 1) * 128, :], writes=[cb_])
                    xpad = K.sb([128, T + 4], BF16, "xpad")
                    K.op("pool", lambda e: e.memset(xpad.t[:, 0:2], 0.0), pwrites=[xpad])
                    K.op("pool", lambda e: e.memset(xpad.t[:, T + 2:T + 4], 0.0), pwrites=[xpad])
                    dg = [K.sb([128, 5, 128], BF16, "dg%d" % i) for i in range(2)]
                    xcT = K.sb([128, T], BF16, "xcT")
                    xtk = K.sb([128, NT, 128], BF16, "xtk")
                    for cbk in range(20):
                        w = load_w(win_cols(l, OFF_XBC + cbk * 128))
                        dgc = dg[cbk % 2]
                        for k in range(5):
                            K.op("pool", lambda e, k=k, dgc=dgc: e.tensor_scalar(out=dgc.t[:, k, :], in0=identf.t[:], scalar1=cw.t[:, cbk, k:k + 1], scalar2=None, op0=ALU.mult),
                                 reads=[identf, cw], pwrites=[dgc])
                        proj_F(w, lambda blk, pb: K.op("act", lambda e: e.activation(out=xpad.t[:, 2 + blk * 512:2 + (blk + 1) * 512], in_=pb.t[:, :], func=AF.Copy),
                                                       reads=[pb], pwrites=[xpad]))
                        for blk in range(8):
                            pc = psb[4 + blk % 2]
                            for k in range(5):
                                K.op("pe", lambda e, k=k, blk=blk, pc=pc: e.matmul(pc.t[:, :], lhsT=dgc.t[:, k, :], rhs=xpad.t[:, blk * 512 + k:blk * 512 + k + 512],
                                                                                  start=(k == 0), stop=(k == 4)),
                                     reads=[dgc, xpad], writes=[pc])
                            K.op("act", lambda e, blk=blk, pc=pc: e.activation(out=xcT.t[:, blk * 512:(blk + 1) * 512], in_=pc.t[:, :], func=AF.Silu,
                                                                               bias=cb_.t[:, cbk:cbk + 1], scale=1.0),
                                 reads=[pc, cb_], pwrites=[xcT])
                        if cbk < 12:
                            tr_to_tok(xcT, xtk, xtok[:, cbk * 128:(cbk + 1) * 128].rearrange("(t p) c -> p t c", p=128), xtok_res)
                        else:
                            K.dma("sp", bcT[(cbk - 12) * 128:(cbk - 11) * 128, :], xcT.t[:], reads=[xcT], pwrites=[bcT_res])
                            if cbk < 16:
                                tr_to_tok(xcT, xtk, btok[:, (cbk - 12) * 128:(cbk - 11) * 128].rearrange("(t p) c -> p t c", p=128), btok_res)
                    alb = K.sb([128, 48], F32, "alb")
                    dtb = K.sb([128, 4, 48], F32, "dtb")
                    A4 = K.sb([128, 4, 48], F32, "A4")
                    dtt = K.sb([128, 4, 48], F32, "dtt")
                    K.dma("sp", alb.t[:], a_log[l:l + 1, :].broadcast_to([128, 48]), writes=[alb])
                    K.op("act", lambda e: e.activation(out=alb.t[:], in_=alb.t[:], func=AF.Exp), reads=[alb], writes=[alb])
                    for j in range(4):
                        K.op("pool", lambda e, j=j: e.tensor_scalar(out=A4.t[:, j, :], in0=alb.t[:], scalar1=-1.0, scalar2=None, op0=ALU.mult),
                             reads=[alb], pwrites=[A4])
                        K.dma("sp", dtb.t[:, j, :], dt_bias[l:l + 1, :].broadcast_to([128, 48]), pwrites=[dtb])
                    wdt = load_w(win_cols(l, OFF_DT, 48), ncols=48)

                    def dt_evac(t4, pb):
                        src = pb.t[:, :].rearrange("p (j c) -> p j c", j=4)[:, :, 0:48]
                        K.op("dve", lambda e: e.tensor_tensor(out=dtt.t[:], in0=src, in1=dtb.t[:], op=ALU.add), reads=[pb, dtb], writes=[dtt])
                        K.op("act", lambda e: e.activation(out=dtt.t[:], in_=dtt.t[:], func=AF.Exp), reads=[dtt], writes=[dtt])
                        K.op("act", lambda e: e.activation(out=dt_all.t[:, t4 * 4:(t4 + 1) * 4, :], in_=dtt.t[:], func=AF.Ln, bias=oneb.t[:, 0:1], scale=1.0),
                             reads=[dtt, oneb], pwrites=[dt_all])
                        K.op("pool", lambda e: e.tensor_tensor(out=a_all.t[:, t4 * 4:(t4 + 1) * 4, :], in0=dt_all.t[:, t4 * 4:(t4 + 1) * 4, :], in1=A4.t[:], op=ALU.mult),
                             reads=[dt_all, A4], pwrites=[a_all])
                    proj_T(wdt, dt_evac, ncols=48)
                    av = a_all.t[:].rearrange("p c h -> p (c h)")
                    dv = dec_all.t[:].rearrange("p c h -> p (c h)")
                    for j in range(3):
                        pb = psb[2 + (j % 2)]
                        K.op("pe", lambda e, j=j, pb=pb: e.matmul(pb.t[:, :], lhsT=onesf.t[:], rhs=av[:, j * 512:(j + 1) * 512], start=True, stop=True),
                             reads=[onesf, a_all], writes=[pb])
                        K.op("act", lambda e, j=j, pb=pb: e.activation(out=dv[:, j * 512:(j + 1) * 512], in_=pb.t[:, :], func=AF.Exp),
                             reads=[pb], pwrites=[dec_all])
                    wz = [K.sb([128, KC, 512], BF16, "wz%d" % i) for i in range(2)]
                    zst = [K.sb([128, 512], BF16, "zst%d" % i) for i in range(2)]
                    for zb in range(3):
                        wzb = wz[zb % 2]
                        K.dma("pool", wzb.t[:], win_cols(l, OFF_Z + zb * 512, 512).rearrange("(k p) n -> p k n", p=128), writes=[wzb])
                        for t in range(NT):
                            pb = psb[2 + (t % 2)]
                            zs = zst[t % 2]
                            for kc in range(KC):
                                K.op("pe", lambda e, kc=kc, t=t, pb=pb: e.matmul(pb.t[:, :], lhsT=hT.t[:, kc, t * 128:(t + 1) * 128], rhs=wzb.t[:, kc, :],
                                                                                start=(kc == 0), stop=(kc == KC - 1)),
                                     reads=[wzb, hT], writes=[pb])
                            K.op("act", lambda e, pb=pb, zs=zs: e.activation(out=zs.t[:, :], in_=pb.t[:, :], func=AF.Silu), reads=[pb], writes=[zs])
                            K.dma("sp", szd[t * 128:(t + 1) * 128, zb * 512:(zb + 1) * 512], zs.t[:, :], reads=[zs], pwrites=[szd_res])
                    K.barrier()
                with ExitStack() as sub2:
                    K.st = sub2
                    xtk2 = [K.sb([128, 1536], BF16, "xtk2_%d" % i) for i in range(2)]
                    btk2 = [K.sb([128, 512], BF16, "btk2_%d" % i) for i in range(2)]
                    bct = [K.sb([128, 8, 128], BF16, "bct%d" % i) for i in range(2)]
                    xdt = K.sb([128, 24, 64], BF16, "xdt")
                    Est = K.sb([128, 24, 64], F32, "Est")
                    Ebf = K.sb([128, 24, 64], BF16, "Ebf")
                    gts = K.sb([128, 128], F32, "gts")
                    E3 = [K.sb([128, 3, 128], F32, "E3_%d" % i) for i in range(2)]
                    MT = [K.sb([128, 3, 128], BF16, "MT%d" % i) for i in range(2)]
                    xw = [K.sb([128, 3, 64], BF16, "xw%d" % i) for i in range(2)]
                    ecum = K.sb([128, 24], F32, "ecum")
                    tmpo = K.sb([128, 6, 64], F32, "tmpo")
                    yacc = [K.sb([128, 1536], F32, "yacc%d" % i) for i in range(2)]
                    yft = K.sb([128, 1536], F32, "yft")
                    tmpx = K.sb([128, 1536], F32, "tmpx")
                    szt2 = K.sb([128, 1536], BF16, "szt2")
                    ycb = K.sb([128, 1536], BF16, "ycb")
                    ycTt = K.sb([128, 12, 128], BF16, "ycTt")
                    dskb = K.sb([128, 24], F32, "dskb")
                    snw = K.sb([128, 1536], F32, "snw")
                    K.dma("sp", dskb.t[:], d_skip[l:l + 1, :].broadcast_to([128, 24]), writes=[dskb])
                    K.dma("sp", snw.t[:], ssm_nw[l:l + 1, :].broadcast_to([128, 1536]), writes=[snw])
                    xdt2 = [xdt, K.sb([128, 24, 64], BF16, "xdtb")]
                    ecum2 = [ecum, K.sb([128, 24], F32, "ecumb")]
                    gts2 = [gts, K.sb([128, 128], F32, "gtsb")]
                    ncum2 = [K.sb([128, 24], F32, "ncum%d" % i) for i in range(2)]
                    pending = []
                    for dirn in (0, 1):
                        K.op("pool", lambda e: e.memset(Est.t[:], 0.0), writes=[Est])
                        K.op("pool", lambda e: e.memset(Ebf.t[:], 0.0), writes=[Ebf])
                        chunks = list(range(NT)) if dirn == 0 else list(range(NT - 1, -1, -1))
                        tri, ntri = (triF, ntriF) if dirn == 0 else (triB, ntriB)
                        mk = mFB.t[:, dirn, :]
                        wc = 127 if dirn == 0 else 0

                        def stage1(u, dirn=dirn, tri=tri, ntri=ntri, mk=mk, wc=wc):
                            c, g, half = u["c"], u["g"], u["half"]
                            i = c % 2
                            xd, ec = xdt2[i], ecum2[i]
                            if g == 0 and half == 0:
                                K.dma("sp", xtk2[i].t[:], xtok[c * 128:(c + 1) * 128, :], reads=[xtok_res], writes=[xtk2[i]])
                                K.dma("sp", btk2[i].t[:], btok[c * 128:(c + 1) * 128, :], reads=[btok_res], writes=[btk2[i]])
                                K.dma("sp", bct[i].t[:], bcT[:, c * 128:(c + 1) * 128].rearrange("(k p) n -> p k n", p=128), reads=[bcT_res], writes=[bct[i]])
                                xv = xtk2[i].t[:].rearrange("p (h c) -> p h c", h=24)
                                dtv = dt_all.t[:, c, dirn * 24:(dirn + 1) * 24].unsqueeze(2).broadcast_to([128, 24, 64])
                                K.op("dve", lambda e: e.tensor_tensor(out=xd.t[:], in0=xv, in1=dtv, op=ALU.mult),
                                     reads=[xtk2[i], dt_all], writes=[xd])
                                K.op("pe", lambda e: e.matmul(psb[0].t[:, 0:24], lhsT=tri.t[:], rhs=a_all.t[:, c, dirn * 24:(dirn + 1) * 24], start=True, stop=True),
                                     reads=[tri, a_all], writes=[psb[0]])
                                K.op("act", lambda e: e.activation(out=ec.t[:], in_=psb[0].t[:, 0:24], func=AF.Exp), reads=[psb[0]], writes=[ec])
                            gt = gts2[g % 2]
                            ncm = ncum2[i]
                            if half == 0:
                                K.op("pe", lambda e: e.matmul(psb[1].t[:, 0:128], lhsT=bct[i].t[:, g, :], rhs=bct[i].t[:, 4 + g, :], start=True, stop=True),
                                     reads=[bct[i]], writes=[psb[1]])
                                K.op("act", lambda e: e.activation(out=gt.t[:], in_=psb[1].t[:, 0:128], func=AF.Copy), reads=[psb[1]], writes=[gt])
                            h0 = g * 6 + half * 3
                            pr = psb[2 + half]
                            for j in range(3):
                                col = dirn * 24 + h0 + j
                                abc = a_all.t[:, c, col:col + 1].broadcast_to([128, 128])
                                dst = pr.t[:, j * 128:(j + 1) * 128]
                                K.op("pe", lambda e, abc=abc, dst=dst: e.matmul(dst, lhsT=abc, rhs=tri.t[:], start=True, stop=False),
                                     reads=[a_all, tri], writes=[pr])
                                K.op("pe", lambda e, abc=abc, dst=dst: e.matmul(dst, lhsT=ntri.t[:], rhs=abc, start=False, stop=False),
                                     reads=[a_all, ntri], writes=[pr])
                                K.op("pe", lambda e, dst=dst: e.matmul(dst, lhsT=identb.t[:], rhs=mk, start=False, stop=True),
                                     reads=[identb, mFB], writes=[pr])
                            e3 = E3[half]
                            K.op("act", lambda e: e.activation(out=e3.t[:].rearrange("p a b -> p (a b)"), in_=pr.t[:, 0:384], func=AF.Exp),
                                 reads=[pr], writes=[e3])
                            mt = MT[half]
                            K.op("dve", lambda e: e.tensor_tensor(out=mt.t[:], in0=e3.t[:], in1=gt.t[:].unsqueeze(1).broadcast_to([128, 3, 128]), op=ALU.mult),
                                 reads=[e3, gt], writes=[mt])
                            xw_ = xw[half]
                            K.op("dve", lambda e: e.tensor_tensor(out=xw_.t[:], in0=xd.t[:, h0:h0 + 3, :],
                                                                  in1=e3.t[:, :, wc:wc + 1].broadcast_to([128, 3, 64]), op=ALU.mult),
                                 reads=[e3, xd], writes=[xw_])

                        def stage2(u, dirn=dirn):
                            c, g, half = u["c"], u["g"], u["half"]
                            i = c % 2
                            xd, ec = xdt2[i], ecum2[i]
                            ya = yacc[i]
                            h0 = g * 6 + half * 3
                            mt, xw_ = MT[half], xw[half]
                            for j in range(3):
                                h = h0 + j
                                hj = half * 3 + j
                                K.op("pe", lambda e, j=j, h=h, hj=hj: e.matmul(psb[4].t[:, hj * 64:(hj + 1) * 64], lhsT=mt.t[:, j, :], rhs=xd.t[:, h, :], start=True, stop=True),
                                     reads=[mt, xd], writes=[psb[4]])
                                K.op("pe", lambda e, h=h, hj=hj: e.matmul(psb[5].t[:, hj * 64:(hj + 1) * 64], lhsT=bct[i].t[:, 4 + g, :], rhs=Ebf.t[:, h, :], start=True, stop=True),
                                     reads=[bct[i], Ebf], writes=[psb[5]])
                                K.op("pe", lambda e, j=j, hj=hj: e.matmul(psb[6].t[:, hj * 64:(hj + 1) * 64], lhsT=btk2[i].t[:, g * 128:(g + 1) * 128], rhs=xw_.t[:, j, :], start=True, stop=True),
                                     reads=[btk2[i], xw_], writes=[psb[6]])
                            if half == 1:
                                g6 = slice(g * 6, (g + 1) * 6)
                                ecv = ec.t[:, g6].unsqueeze(2).broadcast_to([128, 6, 64])
                                K.op("dve", lambda e: e.tensor_tensor(out=tmpo.t[:], in0=psb[5].t[:, 0:384].rearrange("p (a b) -> p a b", a=6), in1=ecv, op=ALU.mult),
                                     reads=[psb[5], ec], writes=[tmpo])
                                K.op("dve", lambda e: e.tensor_tensor(out=ya.t[:, g * 384:(g + 1) * 384], in0=tmpo.t[:].rearrange("p a b -> p (a b)"), in1=psb[4].t[:, 0:384], op=ALU.add),
                                     reads=[tmpo, psb[4]], pwrites=[ya])
                                dcv = dec_all.t[:, c, dirn * 24 + g * 6:dirn * 24 + (g + 1) * 6].unsqueeze(2).broadcast_to([128, 6, 64])
                                K.op("dve", lambda e: e.tensor_tensor(out=Est.t[:, g6, :], in0=Est.t[:, g6, :], in1=dcv, op=ALU.mult),
                                     reads=[Est, dec_all], writes=[Est])
                                K.op("dve", lambda e: e.tensor_tensor(out=Est.t[:, g6, :], in0=Est.t[:, g6, :], in1=psb[6].t[:, 0:384].rearrange("p (a b) -> p a b", a=6), op=ALU.add),
                                     reads=[Est, psb[6]], writes=[Est])
                                K.op("act", lambda e: e.activation(out=Ebf.t[:, g6, :], in_=Est.t[:, g6, :], func=AF.Copy), reads=[Est], writes=[Ebf])
                            if g == 3 and half == 1:
                                if dirn == 0:
                                    K.dma("sp", yfd[c * 128:(c + 1) * 128, :], ya.t[:], reads=[ya], pwrites=[yfd_res])
                                else:
                                    xv = xtk2[i].t[:].rearrange("p (h c) -> p h c", h=24)
                                    K.dma("sp", yft.t[:], yfd[c * 128:(c + 1) * 128, :], reads=[yfd_res], writes=[yft])
                                    K.dma("sp", szt2.t[:], szd[c * 128:(c + 1) * 128, :], reads=[szd_res], writes=[szt2])
                                    K.op("pool", lambda e: e.tensor_tensor(out=ya.t[:], in0=ya.t[:], in1=yft.t[:], op=ALU.add), reads=[ya, yft], writes=[ya])
                                    K.op("dve", lambda e: e.tensor_tensor(out=tmpx.t[:].rearrange("p (h c) -> p h c", h=24), in0=xv,
                                                                          in1=dskb.t[:].unsqueeze(2).broadcast_to([128, 24, 64]), op=ALU.mult),
                                         reads=[xtk2[i], dskb], writes=[tmpx])
                                    K.op("pool", lambda e: e.tensor_tensor(out=ya.t[:], in0=ya.t[:], in1=tmpx.t[:], op=ALU.add), reads=[ya, tmpx], writes=[ya])
                                    K.op("dve", lambda e: e.tensor_tensor(out=ya.t[:], in0=ya.t[:], in1=szt2.t[:], op=ALU.mult), reads=[ya, szt2], writes=[ya])
                                    s_ = sml[i]
                                    rms_scale(ya.t[:], s_, [ya], 1536, tmpx.t[:], tmpx)
                                    K.op("dve", lambda e: e.scalar_tensor_tensor(out=ycb.t[:], in0=ya.t[:], scalar=s_.t[:, 2:3], in1=snw.t[:], op0=ALU.mult, op1=ALU.mult),
                                         reads=[ya, s_, snw], writes=[ycb])
                                    def part2(c=c):
                                        for rnd, (k0, k1) in enumerate(((0, 8), (8, 12))):
                                            for k in range(k0, k1):
                                                K.op("pe", lambda e, k=k, k0=k0: e.transpose(out=pst.t[:, (k - k0) * 128:(k - k0 + 1) * 128], in_=ycb.t[:, k * 128:(k + 1) * 128], identity=identb.t[:]),
                                                     reads=[ycb, identb], writes=[pst])
                                            n_ = k1 - k0
                                            K.op("act", lambda e, k0=k0, k1=k1, n_=n_: e.activation(out=ycTt.t[:, k0:k1, :], in_=pst.t[:, 0:n_ * 128].rearrange("p (k n) -> p k n", k=n_), func=AF.Copy),
                                                 reads=[pst], pwrites=[ycTt])
                                        K.dma("sp", ycT[:, c * 128:(c + 1) * 128].rearrange("(k p) n -> p k n", p=128), ycTt.t[:], reads=[ycTt], pwrites=[ycT_res])
                                    pending.append([part2, 0])

                        units = [dict(c=c, g=g, half=half) for c in chunks for g in range(4) for half in range(2)]
                        prev = None
                        for u in units:
                            stage1(u)
                            if PIPE_C and prev is not None:
                                stage2(prev)
                            if not PIPE_C:
                                stage2(u)
                            prev = u
                            for pd in list(pending):
                                pd[1] += 1
                                if pd[1] >= 4:
                                    pd[0]()
                                    pending.remove(pd)
                        if PIPE_C:
                            stage2(prev)
                        for pd in list(pending):
                            pd[0]()
                            pending.remove(pd)
                    K.barrier()
            K.st = st
        def phase_DE(l, last):
            x_src = x_in if l == 0 else xs
            with ExitStack() as subD:
                K.st = subD
                wD = [K.sb([128, 48, 128], BF16, "wD%d" % i) for i in range(2)]
                yblk = [K.sb([128, 24, 512], BF16, "yblk%d" % i) for i in range(2)]
                sgm = [K.sb([128, 512], F32, "sgm%d" % i) for i in range(3)]
                mm_ = [K.sb([128, 512], F32, "mm%d" % i) for i in range(3)]
                mTb = [K.sb([128, 512], BF16, "mTb%d" % i) for i in range(2)]
                cnt = 0
                for db in range(8):
                    w = wD[db % 2]
                    c0 = db * 128
                    K.dma("pool", w.t[:, 0:8, :], w_oa[l * 1024:(l + 1) * 1024, c0:c0 + 128].rearrange("(k p) n -> p k n", p=128), pwrites=[w])
                    K.dma("pool", w.t[:, 8:12, :], w_ob[l * 512:(l + 1) * 512, c0:c0 + 128].rearrange("(k p) n -> p k n", p=128), pwrites=[w])
                    K.dma("pool", w.t[:, 12:24, :], w_oc[l * 1536:(l + 1) * 1536, c0:c0 + 128].rearrange("(k p) n -> p k n", p=128), pwrites=[w])
                    for ui, off in enumerate((OFF_UA, OFF_UB, OFF_UC)):
                        K.dma("pool", w.t[:, 24 + ui * 8:32 + ui * 8, :], win_cols(l, off + c0).rearrange("(k p) n -> p k n", p=128), pwrites=[w])
                    for tb in range(8):
                        yb = yblk[cnt % 2]
                        mo = mTb[cnt % 2]
                        cnt += 1
                        tc_ = slice(tb * 512, (tb + 1) * 512)
                        K.dma("sp", yb.t[:, 0:8, :], yagT[:, tc_].rearrange("(k p) n -> p k n", p=128), reads=[yagT_res], pwrites=[yb])
                        K.dma("sp", yb.t[:, 8:12, :], ybgT[:, tc_].rearrange("(k p) n -> p k n", p=128), reads=[ybgT_res], pwrites=[yb])
                        K.dma("sp", yb.t[:, 12:24, :], ycT[:, tc_].rearrange("(k p) n -> p k n", p=128), reads=[ycT_res], pwrites=[yb])
                        for ui in range(3):
                            for kc in range(KC):
                                K.op("pe", lambda e, ui=ui, kc=kc: e.matmul(psb[ui].t[:, :], lhsT=w.t[:, 24 + ui * 8 + kc, :], rhs=hT.t[:, kc, tc_], start=(kc == 0), stop=(kc == KC - 1)),
                                     reads=[w, hT], writes=[psb[ui]])
                        for yi, (a, b) in enumerate(((0, 8), (8, 12), (12, 24))):
                            for k in range(a, b):
                                K.op("pe", lambda e, yi=yi, k=k, a=a, b=b: e.matmul(psb[3 + yi].t[:, :], lhsT=w.t[:, k, :], rhs=yb.t[:, k, :], start=(k == a), stop=(k == b - 1)),
                                     reads=[w, yb], writes=[psb[3 + yi]])
                        for ui in range(3):
                            K.op("act", lambda e, ui=ui: e.activation(out=sgm[ui].t[:, :], in_=psb[ui].t[:, :], func=AF.Sigmoid), reads=[psb[ui]], writes=[sgm[ui]])
                            K.op("dve", lambda e, ui=ui: e.tensor_tensor(out=mm_[ui].t[:, :], in0=sgm[ui].t[:, :], in1=psb[3 + ui].t[:, :], op=ALU.mult),
                                 reads=[sgm[ui], psb[3 + ui]], writes=[mm_[ui]])
                        K.op("pool", lambda e: e.tensor_tensor(out=mm_[0].t[:, :], in0=mm_[0].t[:, :], in1=mm_[1].t[:, :], op=ALU.add), reads=[mm_[0], mm_[1]], writes=[mm_[0]])
                        K.op("pool", lambda e, mo=mo: e.tensor_tensor(out=mo.t[:, :], in0=mm_[0].t[:, :], in1=mm_[2].t[:, :], op=ALU.add), reads=[mm_[0], mm_[2]], writes=[mo])
                        K.dma("pool", mrgT[c0:c0 + 128, tc_], mo.t[:, :], reads=[mo], pwrites=[mrgT_res])
                K.barrier()
            K.st = st
            if stop == "D":
                return
            with ExitStack() as subE:
                K.st = subE
                wout = K.sb([128, 8, 1024], BF16, "wout")
                wpgs = K.sb([128, 8, 1024], BF16, "wpgs")
                wpl = K.sb([128, 2, 1024], BF16, "wpl")
                gP = K.sb([128, D], F32, "gP")
                gF = K.sb([128, D], F32, "gF")
                mtl = [K.sb([128, 8, 128], BF16, "mtl%d" % i) for i in range(2)]
                ptl = [K.sb([128, 256], F32, "ptl%d" % i) for i in range(2)]
                x1b = [K.sb([128, D], F32, "x1b%d" % i) for i in range(2)]
                hx = [K.sb([128, 8, 128], BF16, "hx%d" % i) for i in range(2)]
                gate = K.sb([128, D], F32, "gate")
                pe_ = K.sb([128, D], F32, "pe_")
                pT = [K.sb([128, 2, 128], BF16, "pT%d" % i) for i in range(2)]
                for k in range(8):
                    K.dma("pool", wout.t[:, k, :], w_out[l * D + k * 128:l * D + (k + 1) * 128, :], pwrites=[wout])
                    K.dma("pool", wpgs.t[:, k, :], w_pg[l * D + k * 128:l * D + (k + 1) * 128, :], pwrites=[wpgs])
                for k in range(2):
                    K.dma("pool", wpl.t[:, k, :], w_ple[l * 256 + k * 128:l * 256 + (k + 1) * 128, :], pwrites=[wpl])
                K.dma("sp", gP.t[:], ple_norm_w[l:l + 1, :].broadcast_to([128, D]), writes=[gP])
                K.dma("sp", gF.t[:], final_norm_w[0:1, :].broadcast_to([128, D]), writes=[gF])
                def e_stage1(t):
                    i = t % 2
                    mt = mtl[i]
                    x1 = x1b[i]
                    rows = slice(t * 128, (t + 1) * 128)
                    K.dma("sp", mt.t[:], mrgT[:, rows].rearrange("(k p) n -> p k n", p=128), reads=[mrgT_res], writes=[mt])
                    K.dma("sp", xt[i].t[:], x_src[rows, :], reads=[xs_res], writes=[xt[i]])
                    K.dma("sp", ptl[i].t[:], p_in[l * T + t * 128:l * T + (t + 1) * 128, :], writes=[ptl[i]])
                    for half in range(2):
                        hs = slice(half * 512, (half + 1) * 512)
                        for kc in range(KC):
                            K.op("pe", lambda e, half=half, hs=hs, kc=kc: e.matmul(psb[half].t[:, :], lhsT=mt.t[:, kc, :], rhs=wout.t[:, kc, hs], start=(kc == 0), stop=(kc == KC - 1)),
                                 reads=[mt, wout], writes=[psb[half]])
                        K.op("dve", lambda e, half=half, hs=hs: e.tensor_tensor(out=x1.t[:, hs], in0=xt[i].t[:, hs], in1=psb[half].t[:, :], op=ALU.add),
                             reads=[xt[i], psb[half]], pwrites=[x1])
                    rmsnorm_tile(x1, i, gP)
                    to_T(i, lambda half: hx[i].t[:, half * 4:(half + 1) * 4, :], hx[i], pa=2)
                    for k in range(2):
                        K.op("pe", lambda e, k=k: e.transpose(out=psb[6].t[:, k * 128:(k + 1) * 128], in_=ptl[i].t[:, k * 128:(k + 1) * 128], identity=identf.t[:]),
                             reads=[ptl[i], identf], writes=[psb[6]])
                    K.op("act", lambda e: e.activation(out=pT[i].t[:].rearrange("p k n -> p (k n)"), in_=psb[6].t[:, 0:256], func=AF.Copy), reads=[psb[6]], writes=[pT[i]])

                def e_stage2(t):
                    i = t % 2
                    x1 = x1b[i]
                    rows = slice(t * 128, (t + 1) * 128)
                    for half in range(2):
                        hs = slice(half * 512, (half + 1) * 512)
                        for kc in range(KC):
                            K.op("pe", lambda e, half=half, hs=hs, kc=kc: e.matmul(psb[4 + half].t[:, :], lhsT=hx[i].t[:, kc, :], rhs=wpgs.t[:, kc, hs], start=(kc == 0), stop=(kc == KC - 1)),
                                 reads=[hx[i], wpgs], writes=[psb[4 + half]])
                        K.op("act", lambda e, half=half, hs=hs: e.activation(out=gate.t[:, hs], in_=psb[4 + half].t[:, :], func=AF.Sigmoid), reads=[psb[4 + half]], pwrites=[gate])
                    for half in range(2):
                        hs = slice(half * 512, (half + 1) * 512)
                        for k in range(2):
                            K.op("pe", lambda e, half=half, hs=hs, k=k: e.matmul(psb[4 + half].t[:, :], lhsT=pT[i].t[:, k, :], rhs=wpl.t[:, k, hs], start=(k == 0), stop=(k == 1)),
                                 reads=[pT[i], wpl], writes=[psb[4 + half]])
                        K.op("dve", lambda e, half=half, hs=hs: e.tensor_tensor(out=pe_.t[:, hs], in0=gate.t[:, hs], in1=psb[4 + half].t[:, :], op=ALU.mult),
                             reads=[gate, psb[4 + half]], pwrites=[pe_])
                    K.op("pool", lambda e: e.tensor_tensor(out=x1.t[:], in0=x1.t[:], in1=pe_.t[:], op=ALU.add), reads=[x1, pe_], writes=[x1])
                    if not last:
                        K.dma("pool", xs[rows, :], x1.t[:], reads=[x1], pwrites=[xs_res])
                    else:
                        fo = fout[i]
                        s_ = sml2[i]
                        rms_scale(x1.t[:], s_, [x1], D, fo.t[:], fo)
                        K.op("dve", lambda e: e.scalar_tensor_tensor(out=fo.t[:], in0=x1.t[:], scalar=s_.t[:, 2:3], in1=gF.t[:], op0=ALU.mult, op1=ALU.mult),
                             reads=[x1, s_, gF], writes=[fo])
                        K.dma("sp", y_out[rows, :], fo.t[:], reads=[fo])

                fout = [K.sb([128, D], F32, "fout%d" % i) for i in range(2)] if last else None
                sml2 = [K.sb([128, 4], F32, "sml2_%d" % i) for i in range(2)]
                prev = None
                for t in range(NT):
                    e_stage1(t)
                    if prev is not None:
                        e_stage2(prev)
                    prev = t
                e_stage2(prev)
                K.barrier()
            K.st = st

        for l in range(nlayers):
            phase_h(x_in if l == 0 else xs, l)
            if stop == "h":
                break
            if stop in (None, "A", "B", "D", "E"):
                with ExitStack() as sub:
                    K.st = sub
                    alloc_AB()
                    if stop != "B":
                        phase_A(l)
                    if stop != "A":
                        phase_B(l)
                    K.barrier()
                K.st = st
                if stop in ("A", "B"):
                    break
            if stop in (None, "C", "D", "E"):
                phase_C(l)
                if stop == "C":
                    break
            if stop in (None, "D", "E"):
                phase_DE(l, last=(l == nlayers - 1))
                if stop in ("D", "E"):
                    break

        if dbg:
            dk = K.sb([128, 1024], BF16, "dbgk")
            if stop == "h":
                for kc in range(KC):
                    for c4 in range(4):
                        K.op("act", lambda e, kc=kc, c4=c4: e.activation(out=xt[0].t[:, :], in_=hT.t[:, kc, c4 * 1024:(c4 + 1) * 1024], func=AF.Copy),
                             reads=[hT], writes=[xt[0]])
                        K.dma("sp", dbg_out[kc * 128:(kc + 1) * 128, c4 * 1024:(c4 + 1) * 1024], xt[0].t[:, :], reads=[xt[0]])
            elif stop in ("A", "B", "C", "D"):
                srcd, nk_ = {"A": (yagT, 8), "B": (ybgT, 4), "C": (ycT, 12), "D": (mrgT, 8)}[stop]
                for kc in range(nk_):
                    for c4 in range(4):
                        K.dma("sp", dk.t[:, :], srcd[kc * 128:(kc + 1) * 128, c4 * 1024:(c4 + 1) * 1024],
                              reads=[yagT_res, ybgT_res, ycT_res, mrgT_res], writes=[dk])
                        K.op("act", lambda e: e.activation(out=xt[0].t[:, :], in_=dk.t[:, :], func=AF.Copy), reads=[dk], writes=[xt[0]])
                        K.dma("sp", dbg_out[kc * 128:(kc + 1) * 128, c4 * 1024:(c4 + 1) * 1024], xt[0].t[:, :], reads=[xt[0]])
        K.finish()
    return nc


def host_inputs(inputs, b):
    cases, per_a, mask, dr, dc = na_tables()
    f = np.float32
    m = {}
    m["x"] = np.ascontiguousarray(inputs["x"][b], dtype=f)
    m["p"] = np.ascontiguousarray(inputs["p"][:, b], dtype=f).reshape(DEPTH * T, 256)
    m["norm_w"] = np.asarray(inputs["norm_w"], f)
    m["ple_norm_w"] = np.asarray(inputs["ple_norm_w"], f)
    m["final_norm_w"] = np.asarray(inputs["final_norm_w"], f).reshape(1, D)
    m["w_in"] = np.asarray(inputs["w_in"], f).reshape(DEPTH * D, IN_W)
    m["w_oa"] = np.asarray(inputs["w_oa"], f).reshape(DEPTH * 1024, D)
    m["w_ob"] = np.asarray(inputs["w_ob"], f).reshape(DEPTH * 512, D)
    m["w_oc"] = np.asarray(inputs["w_oc"], f).reshape(DEPTH * 1536, D)
    m["w_out"] = np.asarray(inputs["w_out"], f).reshape(DEPTH * D, D)
    m["w_ple"] = np.asarray(inputs["w_ple"], f).reshape(DEPTH * 256, D)
    m["w_ple_gate"] = np.asarray(inputs["w_ple_gate"], f).reshape(DEPTH * D, D)
    rpb = np.asarray(inputs["na_rpb"], f)
    g = rpb[:, :, dr, dc]
    m["rpbg"] = np.ascontiguousarray(g.transpose(0, 1, 3, 2, 4)).reshape(DEPTH * 16 * 128, 7 * 128)
    m["na_mask"] = np.ascontiguousarray(mask.transpose(1, 0, 2)).reshape(128, -1)
    m["dil_mask"] = np.ascontiguousarray(dil_masks().transpose(1, 0, 2)).reshape(128, -1)
    C, S = rope_tables()
    m["rope_c"], m["rope_s"] = C, S
    pm = np.eye(128, dtype=np.float32)
    for hh in range(2):
        b0 = hh * 64
        for i in range(8):
            pm[b0 + i, b0 + i] = 0.0
            pm[b0 + 8 + i, b0 + 8 + i] = 0.0
            pm[b0 + i, b0 + 8 + i] = 1.0
            pm[b0 + 8 + i, b0 + i] = 1.0
    m["rope_pm"] = pm
    cw = np.asarray(inputs["conv_w"], f)
    m["convw"] = np.ascontiguousarray(cw.reshape(DEPTH, 5, 20, 128).transpose(0, 3, 2, 1)).reshape(DEPTH * 128, 100)
    cb = np.asarray(inputs["conv_b"], f)
    m["convb"] = np.ascontiguousarray(cb.reshape(DEPTH, 20, 128).transpose(0, 2, 1)).reshape(DEPTH * 128, 20)
    m["a_log"] = np.asarray(inputs["a_log"], f).reshape(DEPTH, 48)
    m["dt_bias"] = np.asarray(inputs["dt_bias"], f).reshape(DEPTH, 48)
    m["d_skip"] = np.asarray(inputs["d_skip"], f).reshape(DEPTH, 24)
    m["ssm_norm_w"] = np.asarray(inputs["ssm_norm_w"], f).reshape(DEPTH, 1536)
    return m


def kernel(**inputs):
    nc = build()
    in_maps = [host_inputs(inputs, b) for b in range(8)]
    res = run_bass_kernel_spmd(nc, in_maps, core_ids=list(range(8)))
    return np.stack([r["y"] for r in res.results], axis=0).astype(np.float32)
```
